# Optimizing a Trainium2 kernel written in Bass

```python
import jax
import jax.numpy as jnp
from jax import lax
import numpy as np

D_MODEL = 1024
BATCH = 8
SEQ = 2048
DEPTH = 2
DEC_BATCH = 128
DEC_SEQ = 1
PAST_LEN = 16384
PAGE_SIZE = 128

PLE_DIM = 256
D_FF = 4 * D_MODEL
N_BRANCH = 3
EPS = 1e-6
CHUNK = 64

S5_WIDTH = D_MODEL // 2
S5_GROUP = 16
S5_GROUPS = S5_WIDTH // S5_GROUP
S5_STATE = 64

GDN_HEADS = 4
GDN_DK = 128
GDN_DV = 128
GDN_CONV_W = 4
GDN_QK_W = GDN_HEADS * GDN_DK
GDN_V_W = GDN_HEADS * GDN_DV
GDN_CONV_CH = 2 * GDN_QK_W + GDN_V_W

ML_HEADS = 4
ML_DK = 64
ML_DV = 128
ML_QK_W = ML_HEADS * ML_DK
ML_V_W = ML_HEADS * ML_DV
GATE_CAP = 15.0

IN_SPLITS = (S5_WIDTH, GDN_CONV_CH, GDN_V_W, GDN_HEADS, GDN_HEADS,
             ML_QK_W, ML_QK_W, ML_V_W, ML_V_W, ML_HEADS, ML_HEADS, N_BRANCH * D_MODEL)
D_IN = sum(IN_SPLITS)
IN_OFFSETS = tuple(int(o) for o in np.cumsum(IN_SPLITS)[:-1])

kernel_name = 'hybrid_s5_gdn_mlstm_step'


def rmsnorm(x, g):
    xf = x.astype(jnp.float32)
    y = xf * lax.rsqrt(jnp.mean(xf * xf, axis=-1, keepdims=True) + EPS)
    return (y * g.astype(jnp.float32)).astype(x.dtype)


def head_rmsnorm(x, g):
    return x * lax.rsqrt(jnp.mean(x * x, axis=-1, keepdims=True) + EPS) * g


def l2norm(x):
    return x * lax.rsqrt(jnp.sum(x * x, axis=-1, keepdims=True) + EPS)


def softcap(x, cap):
    return cap * jnp.tanh(x / cap)


def chunk_len(L):
    return CHUNK if L % CHUNK == 0 else L


def s5_mixer(u, h0_re, h0_im, A_re, A_im, log_dt, B_re, B_im, C_re, C_im, d_skip, w_glu, b_glu):
    f = jnp.float32
    bsz, L, _ = u.shape
    uf = u.astype(f).reshape(bsz, L, S5_GROUPS, S5_GROUP)
    a_re, a_im = A_re.astype(f), A_im.astype(f)
    dt = jnp.exp(log_dt.astype(f))[:, None]
    mag = jnp.exp(a_re * dt)
    lam_re, lam_im = mag * jnp.cos(a_im * dt), mag * jnp.sin(a_im * dt)
    inv = 1.0 / (a_re * a_re + a_im * a_im)
    zr, zi = lam_re - 1.0, lam_im
    fac_re = (zr * a_re + zi * a_im) * inv
    fac_im = (zi * a_re - zr * a_im) * inv
    b_re, b_im = B_re.astype(f), B_im.astype(f)
    bb_re = fac_re[..., None] * b_re - fac_im[..., None] * b_im
    bb_im = fac_re[..., None] * b_im + fac_im[..., None] * b_re
    bu_re = jnp.einsum('blgc,gpc->blgp', uf, bb_re)
    bu_im = jnp.einsum('blgc,gpc->blgp', uf, bb_im)
    shp = (1, L, S5_GROUPS, S5_STATE)
    el_re = jnp.broadcast_to(lam_re, shp)
    el_im = jnp.broadcast_to(lam_im, shp)

    def combine(e1, e2):
        a1r, a1i, b1r, b1i = e1
        a2r, a2i, b2r, b2i = e2
        return (a2r * a1r - a2i * a1i, a2r * a1i + a2i * a1r,
                a2r * b1r - a2i * b1i + b2r, a2r * b1i + a2i * b1r + b2i)

    pr, pim, hr, hi = lax.associative_scan(combine, (el_re, el_im, bu_re, bu_im), axis=1)
    g0r = h0_re.astype(f)[:, None]
    g0i = h0_im.astype(f)[:, None]
    h_re = hr + pr * g0r - pim * g0i
    h_im = hi + pr * g0i + pim * g0r
    y = (jnp.einsum('blgp,gcp->blgc', h_re, C_re.astype(f))
         - jnp.einsum('blgp,gcp->blgc', h_im, C_im.astype(f))
         + d_skip.astype(f).reshape(S5_GROUPS, S5_GROUP) * uf)
    y = jax.nn.gelu(y.reshape(bsz, L, S5_WIDTH))
    y = y * jax.nn.sigmoid(y @ w_glu.astype(f) + b_glu.astype(f))
    dtype = u.dtype
    return y.astype(dtype), h_re[:, -1].astype(dtype), h_im[:, -1].astype(dtype)


def causal_conv(x, buf, w):
    L = x.shape[1]
    xx = jnp.concatenate([buf, x], axis=1)
    out = xx[:, 0:L] * w[0]
    for j in range(1, GDN_CONV_W):
        out = out + xx[:, j:j + L] * w[j]
    return out, xx[:, L:]


def gdn_mixer(qkv, z, b_pre, a_pre, conv_buf, S0, conv_w, A_log, dt_bias, norm_g):
    f = jnp.float32
    bsz, L, _ = qkv.shape
    conv, new_buf = causal_conv(qkv.astype(f), conv_buf.astype(f), conv_w.astype(f))
    conv = jax.nn.silu(conv)
    q, k, v = jnp.split(conv, (GDN_QK_W, 2 * GDN_QK_W), axis=-1)
    q = l2norm(q.reshape(bsz, L, GDN_HEADS, GDN_DK)) * (GDN_DK ** -0.5)
    k = l2norm(k.reshape(bsz, L, GDN_HEADS, GDN_DK))
    v = v.reshape(bsz, L, GDN_HEADS, GDN_DV)
    beta = jax.nn.sigmoid(b_pre.astype(f))
    g = -jnp.exp(A_log.astype(f)) * jax.nn.softplus(a_pre.astype(f) + dt_bias.astype(f))
    c = chunk_len(L)
    n_chunks = L // c

    def to_chunks(t):
        t = t.reshape((bsz, n_chunks, c) + t.shape[2:])
        return jnp.moveaxis(jnp.moveaxis(t, 3, 2), 1, 0)

    qc, kc, vc, gc, bc = (to_chunks(t) for t in (q, k, v, g, beta))
    gcum = jnp.cumsum(gc, axis=-1)
    idx = jnp.arange(c)
    incl = idx[:, None] >= idx[None, :]
    strict = idx[:, None] > idx[None, :]
    decay = jnp.exp(jnp.where(incl, gcum[..., :, None] - gcum[..., None, :], -jnp.inf))
    kb = kc * bc[..., None]
    a_low = jnp.where(strict, jnp.einsum('nbhtd,nbhsd->nbhts', kb, kc) * decay, 0.0)
    rhs = jnp.concatenate([vc * bc[..., None], kb * jnp.exp(gcum)[..., None]], axis=-1)
    sol = lax.linalg.triangular_solve(a_low + jnp.eye(c, dtype=f), rhs, left_side=True,
                                      lower=True, unit_diagonal=True)
    u, w = sol[..., :GDN_DV], sol[..., GDN_DV:]
    attn = jnp.where(incl, jnp.einsum('nbhtd,nbhsd->nbhts', qc, kc) * decay, 0.0)
    q_dec = qc * jnp.exp(gcum)[..., None]
    k_dec = kc * jnp.exp(gcum[..., -1:] - gcum)[..., None]
    g_last = jnp.exp(gcum[..., -1])

    def step(S, xs):
        u_i, w_i, attn_i, q_i, k_i, gl_i = xs
        v_new = u_i - jnp.einsum('bhtd,bhde->bhte', w_i, S)
        o_i = jnp.einsum('bhtd,bhde->bhte', q_i, S) + jnp.einsum('bhts,bhse->bhte', attn_i, v_new)
        S = S * gl_i[..., None, None] + jnp.einsum('bhtd,bhte->bhde', k_i, v_new)
        return S, o_i

    S_fin, o = lax.scan(step, S0.astype(f), (u, w, attn, q_dec, k_dec, g_last))
    o = jnp.moveaxis(jnp.moveaxis(o, 0, 1), 2, 3).reshape(bsz, L, GDN_HEADS, GDN_DV)
    o = head_rmsnorm(o, norm_g.astype(f)) * jax.nn.silu(z.astype(f).reshape(bsz, L, GDN_HEADS, GDN_DV))
    dtype = qkv.dtype
    return o.reshape(bsz, L, GDN_V_W).astype(dtype), new_buf.astype(dtype), S_fin.astype(dtype)


def mlstm_mixer(q, k, v, o_pre, i_pre, f_pre, C0, n0, m0, b_i, b_f, norm_g):
    f = jnp.float32
    bsz, L, _ = q.shape
    q = q.astype(f).reshape(bsz, L, ML_HEADS, ML_DK)
    k = k.astype(f).reshape(bsz, L, ML_HEADS, ML_DK) * (ML_DK ** -0.5)
    v = v.astype(f).reshape(bsz, L, ML_HEADS, ML_DV)
    li = softcap(i_pre.astype(f) + b_i.astype(f), GATE_CAP)
    lf = jax.nn.log_sigmoid(softcap(f_pre.astype(f) + b_f.astype(f), GATE_CAP))
    c = chunk_len(L)
    n_chunks = L // c

    def to_chunks(t):
        return jnp.moveaxis(t.reshape((bsz, n_chunks, c) + t.shape[2:]), 1, 0)

    qc, kc, vc, lic = (to_chunks(t) for t in (q, k, v, li))
    bcum = jnp.cumsum(to_chunks(lf), axis=2)
    idx = jnp.arange(c)
    incl = (idx[:, None] >= idx[None, :])[None, :, :, None]

    def step(carry, xs):
        C, n, m = carry
        q_i, k_i, v_i, li_i, bc_i = xs
        logw = jnp.where(incl, bc_i[:, :, None, :] - bc_i[:, None, :, :] + li_i[:, None, :, :], -jnp.inf)
        inter = bc_i + m[:, None, :]
        m_t = jnp.maximum(inter, jnp.max(logw, axis=2))
        wts = jnp.exp(logw - m_t[:, :, None, :])
        sc = jnp.exp(inter - m_t)
        s_qk = jnp.einsum('bthd,bshd->btsh', q_i, k_i) * wts
        num = jnp.einsum('btsh,bshe->bthe', s_qk, v_i) + sc[..., None] * jnp.einsum('bthd,bhde->bthe', q_i, C)
        den = jnp.sum(s_qk, axis=2) + sc * jnp.einsum('bthd,bhd->bth', q_i, n)
        h = num / jnp.maximum(jnp.abs(den), jnp.exp(-m_t))[..., None]
        w_last = wts[:, -1]
        sc_last = sc[:, -1]
        C = sc_last[..., None, None] * C + jnp.einsum('bsh,bshd,bshe->bhde', w_last, k_i, v_i)
        n = sc_last[..., None] * n + jnp.einsum('bsh,bshd->bhd', w_last, k_i)
        return (C, n, m_t[:, -1]), h

    (C_fin, n_fin, m_fin), h = lax.scan(step, (C0.astype(f), n0.astype(f), m0.astype(f)),
                                        (qc, kc, vc, lic, bcum))
    h = jnp.moveaxis(h, 0, 1).reshape(bsz, L, ML_HEADS, ML_DV)
    h = head_rmsnorm(h, norm_g.astype(f).reshape(ML_HEADS, ML_DV))
    h = h * jax.nn.sigmoid(o_pre.astype(f).reshape(bsz, L, ML_HEADS, ML_DV))
    dtype = o_pre.dtype
    return (h.reshape(bsz, L, ML_V_W).astype(dtype), C_fin.astype(dtype),
            n_fin.astype(dtype), m_fin.astype(dtype))


def block(x, p, s5_h_re, s5_h_im, conv_buf, gdn_S, ml_C, ml_n, ml_m,
          norm1_g, w_in, s5_A_re, s5_A_im, s5_log_dt, s5_B_re, s5_B_im, s5_C_re, s5_C_im,
          s5_D, s5_w_glu, s5_b_glu, gdn_conv_w, gdn_A_log, gdn_dt_bias, gdn_norm_g,
          ml_b_i, ml_b_f, ml_norm_g, w_br_s5, w_br_gdn, w_br_ml, w_out, norm2_g,
          w_up, w_down, w_ple, w_ple_gate):
    bsz, L, _ = x.shape
    xn = rmsnorm(x, norm1_g)
    (u, qkv, z, b_pre, a_pre, mq, mk, mv, mo, mi, mf, gate_pre) = jnp.split(xn @ w_in, IN_OFFSETS, axis=-1)
    y_s5, s5_h_re, s5_h_im = s5_mixer(u, s5_h_re, s5_h_im, s5_A_re, s5_A_im, s5_log_dt, s5_B_re, s5_B_im,
                                      s5_C_re, s5_C_im, s5_D, s5_w_glu, s5_b_glu)
    y_gdn, conv_buf, gdn_S = gdn_mixer(qkv, z, b_pre, a_pre, conv_buf, gdn_S, gdn_conv_w,
                                       gdn_A_log, gdn_dt_bias, gdn_norm_g)
    y_ml, ml_C, ml_n, ml_m = mlstm_mixer(mq, mk, mv, mo, mi, mf, ml_C, ml_n, ml_m, ml_b_i, ml_b_f, ml_norm_g)
    gates = jax.nn.sigmoid(gate_pre.astype(jnp.float32)).reshape(bsz, L, N_BRANCH, D_MODEL)
    merged = (gates[:, :, 0] * (y_s5 @ w_br_s5) + gates[:, :, 1] * (y_gdn @ w_br_gdn)
              + gates[:, :, 2] * (y_ml @ w_br_ml))
    x = x + (merged.astype(x.dtype) @ w_out).astype(x.dtype)
    hdn = jax.nn.relu(rmsnorm(x, norm2_g) @ w_up)
    x = x + ((hdn * hdn) @ w_down).astype(x.dtype)
    x = x + ((p @ w_ple) * jax.nn.sigmoid(x @ w_ple_gate)).astype(x.dtype)
    return x, (s5_h_re, s5_h_im, conv_buf, gdn_S, ml_C, ml_n, ml_m)


def setup_inputs(seed: int = 0) -> dict:
    key = jax.random.key(seed)
    keys = iter(jax.random.split(key, 64))
    f = jnp.float32

    def nrm(shape, scale):
        return scale * jax.random.normal(next(keys), shape, f)

    def gain(shape):
        return 1.0 + nrm(shape, 0.01)

    def log_uniform(shape, lo, hi):
        return jax.random.uniform(next(keys), shape, f, minval=float(np.log(lo)), maxval=float(np.log(hi)))

    Dp = DEPTH
    gdn_dt = jnp.exp(log_uniform((Dp, GDN_HEADS), 1e-3, 1e-1))
    return {
        'x_prompt': nrm((BATCH, SEQ, D_MODEL), 1.0),
        'x_sample': nrm((DEC_BATCH, DEC_SEQ, D_MODEL), 1.0),
        'state_s5_re': nrm((Dp, DEC_BATCH, S5_GROUPS, S5_STATE), 0.1),
        'state_s5_im': nrm((Dp, DEC_BATCH, S5_GROUPS, S5_STATE), 0.1),
        'state_gdn_conv': nrm((Dp, DEC_BATCH, GDN_CONV_W - 1, GDN_CONV_CH), 1.0),
        'state_gdn': nrm((Dp, DEC_BATCH, GDN_HEADS, GDN_DK, GDN_DV), 0.5),
        'state_mlstm_C': nrm((Dp, DEC_BATCH, ML_HEADS, ML_DK, ML_DV), 0.5),
        'state_mlstm_n': nrm((Dp, DEC_BATCH, ML_HEADS, ML_DK), 0.5),
        'state_mlstm_m': nrm((Dp, DEC_BATCH, ML_HEADS), 1.0),
        'p_prompt': nrm((Dp, BATCH, SEQ, PLE_DIM), 1.0),
        'p_sample': nrm((Dp, DEC_BATCH, DEC_SEQ, PLE_DIM), 1.0),
        'norm1_g': gain((Dp, D_MODEL)),
        'w_in': nrm((Dp, D_MODEL, D_IN), D_MODEL ** -0.5),
        's5_A_re': -0.5 + nrm((Dp, S5_GROUPS, S5_STATE), 0.01),
        's5_A_im': jnp.pi * jnp.arange(S5_STATE, dtype=f) + nrm((Dp, S5_GROUPS, S5_STATE), 0.01),
        's5_log_dt': log_uniform((Dp, S5_GROUPS), 1e-3, 1e-1),
        's5_B_re': nrm((Dp, S5_GROUPS, S5_STATE, S5_GROUP), (2 * S5_GROUP) ** -0.5),
        's5_B_im': nrm((Dp, S5_GROUPS, S5_STATE, S5_GROUP), (2 * S5_GROUP) ** -0.5),
        's5_C_re': nrm((Dp, S5_GROUPS, S5_GROUP, S5_STATE), 0.5),
        's5_C_im': nrm((Dp, S5_GROUPS, S5_GROUP, S5_STATE), 0.5),
        's5_D': nrm((Dp, S5_WIDTH), 1.0),
        's5_w_glu': nrm((Dp, S5_WIDTH, S5_WIDTH), S5_WIDTH ** -0.5),
        's5_b_glu': nrm((Dp, S5_WIDTH), 0.01),
        'gdn_conv_w': nrm((Dp, GDN_CONV_W, GDN_CONV_CH), GDN_CONV_W ** -0.5),
        'gdn_A_log': jnp.log(jax.random.uniform(next(keys), (Dp, GDN_HEADS), f, minval=1.0, maxval=16.0)),
        'gdn_dt_bias': jnp.log(jnp.expm1(gdn_dt)),
        'gdn_norm_g': gain((Dp, GDN_DV)),
        'ml_b_i': nrm((Dp, ML_HEADS), 0.1),
        'ml_b_f': 3.0 + nrm((Dp, ML_HEADS), 0.5),
        'ml_norm_g': gain((Dp, ML_V_W)),
        'w_br_s5': nrm((Dp, S5_WIDTH, D_MODEL), S5_WIDTH ** -0.5),
        'w_br_gdn': nrm((Dp, GDN_V_W, D_MODEL), GDN_V_W ** -0.5),
        'w_br_ml': nrm((Dp, ML_V_W, D_MODEL), ML_V_W ** -0.5),
        'w_out': nrm((Dp, D_MODEL, D_MODEL), D_MODEL ** -0.5),
        'norm2_g': gain((Dp, D_MODEL)),
        'w_up': nrm((Dp, D_MODEL, D_FF), D_MODEL ** -0.5),
        'w_down': nrm((Dp, D_FF, D_MODEL), D_FF ** -0.5),
        'w_ple': nrm((Dp, PLE_DIM, D_MODEL), PLE_DIM ** -0.5),
        'w_ple_gate': nrm((Dp, D_MODEL, D_MODEL), D_MODEL ** -0.5),
        'final_norm_g': gain((D_MODEL,)),
    }


def reference(x_prompt, x_sample, state_s5_re, state_s5_im, state_gdn_conv, state_gdn,
              state_mlstm_C, state_mlstm_n, state_mlstm_m, p_prompt, p_sample,
              norm1_g, w_in, s5_A_re, s5_A_im, s5_log_dt, s5_B_re, s5_B_im, s5_C_re, s5_C_im,
              s5_D, s5_w_glu, s5_b_glu, gdn_conv_w, gdn_A_log, gdn_dt_bias, gdn_norm_g,
              ml_b_i, ml_b_f, ml_norm_g, w_br_s5, w_br_gdn, w_br_ml, w_out, norm2_g,
              w_up, w_down, w_ple, w_ple_gate, final_norm_g):
    layer_weights = (norm1_g, w_in, s5_A_re, s5_A_im, s5_log_dt, s5_B_re, s5_B_im, s5_C_re, s5_C_im,
                     s5_D, s5_w_glu, s5_b_glu, gdn_conv_w, gdn_A_log, gdn_dt_bias, gdn_norm_g,
                     ml_b_i, ml_b_f, ml_norm_g, w_br_s5, w_br_gdn, w_br_ml, w_out, norm2_g,
                     w_up, w_down, w_ple, w_ple_gate)
    bp = x_prompt.shape[0]
    dt = x_prompt.dtype
    xp, xs = x_prompt, x_sample
    new_p, new_s = [], []
    for i in range(DEPTH):
        lw = [w[i] for w in layer_weights]
        zero_state = (jnp.zeros((bp, S5_GROUPS, S5_STATE), dt),
                      jnp.zeros((bp, S5_GROUPS, S5_STATE), dt),
                      jnp.zeros((bp, GDN_CONV_W - 1, GDN_CONV_CH), dt),
                      jnp.zeros((bp, GDN_HEADS, GDN_DK, GDN_DV), dt),
                      jnp.zeros((bp, ML_HEADS, ML_DK, ML_DV), dt),
                      jnp.zeros((bp, ML_HEADS, ML_DK), dt),
                      jnp.zeros((bp, ML_HEADS), dt))
        xp, sp_i = block(xp, p_prompt[i], *zero_state, *lw)
        xs, ss_i = block(xs, p_sample[i], state_s5_re[i], state_s5_im[i], state_gdn_conv[i], state_gdn[i],
                         state_mlstm_C[i], state_mlstm_n[i], state_mlstm_m[i], *lw)
        new_p.append(sp_i)
        new_s.append(ss_i)
    y_prompt = rmsnorm(xp, final_norm_g)
    y_sample = rmsnorm(xs, final_norm_g)
    sp = [jnp.stack(t) for t in zip(*new_p)]
    ss = [jnp.stack(t) for t in zip(*new_s)]
    return (y_prompt, y_sample, sp[0], ss[0], sp[1], ss[1], sp[2], ss[2], sp[3], ss[3],
            sp[4], ss[4], sp[5], ss[5], sp[6], ss[6])
```

```python
import contextlib
import numpy as np
import concourse.bass as bass
import concourse.mybir as mybir
from concourse.bass_utils import run_bass_kernel_spmd

F32 = mybir.dt.float32
BF16 = mybir.dt.bfloat16
I32 = mybir.dt.int32
ALU = mybir.AluOpType
AF = mybir.ActivationFunctionType
AX = mybir.AxisListType

SAME_ENG_SYNC = "all"
DMA_SLOTS = 8


class V:
    __slots__ = ("ap", "keys")

    def __init__(self, ap, keys):
        self.ap = ap
        self.keys = tuple(keys)

    def __getitem__(self, idx):
        return V(self.ap[idx], self.keys)

    def k(self, *keys):
        return V(self.ap, keys)

    def re(self, pat, **kw):
        return V(self.ap.rearrange(pat, **kw), self.keys)

    def bc(self, shape):
        return V(self.ap.to_broadcast(shape), self.keys)

    def bcast(self, axis, n):
        shp = list(self.ap.shape)
        shp.insert(axis, n)
        return V(self.ap.unsqueeze(axis).to_broadcast(shp), self.keys)

    def cast(self, dt):
        return V(self.ap.bitcast(dt), self.keys)


class Prog:
    def __init__(self, nc):
        self.nc = nc
        self.es = contextlib.ExitStack()
        self.ops = []
        self.last_w = {}
        self.readers = {}
        self.out_dmas = []
        self.nbuf = 0

    def sb(self, name, shape, dtype=F32):
        t = self.es.enter_context(self.nc.sbuf_tensor(name, list(shape), dtype))
        self.nbuf += 1
        return V(t[:], (("sb", name),))

    def ps(self, name, shape, dtype=F32):
        t = self.es.enter_context(self.nc.psum_tensor(name, list(shape), dtype))
        return V(t[:], (("ps", name),))

    def dram(self, name, shape, dtype=F32, kind="ExternalInput"):
        t = self.nc.dram_tensor(name, list(shape), dtype, kind=kind)
        return V(t.ap(), (("dr", name),))

    def op(self, eng, fn, reads, writes, dma=False, out=False):
        i = len(self.ops)
        deps = set()
        for v in reads:
            for k in v.keys:
                if k[0] == "dr" and k not in self.last_w:
                    continue
                j = self.last_w.get(k)
                if j is not None:
                    deps.add(j)
        raw = set(deps)
        for v in writes:
            for k in v.keys:
                j = self.last_w.get(k)
                if j is not None:
                    deps.add(j)
                deps.update(self.readers.get(k, ()))
        for v in reads:
            for k in v.keys:
                self.readers.setdefault(k, []).append(i)
        for v in writes:
            for k in v.keys:
                self.last_w[k] = i
                self.readers[k] = []
        deps.discard(i)
        self.ops.append((eng, fn, deps, dma, raw))
        if out:
            self.out_dmas.append(i)
        return i

    def dma(self, eng, out, in_, is_out=False, **kw):
        return self.op(eng, lambda e: e.dma_start(out=out.ap, in_=in_.ap, **kw), [in_], [out], dma=True, out=is_out)

    def mm(self, out, lhsT, rhs, start=True, stop=True, **kw):
        return self.op("pe", lambda e: e.matmul(out.ap, lhsT.ap, rhs.ap, start=start, stop=stop, **kw),
                       [lhsT, rhs], [out])

    def tr(self, out, in_, ident, **kw):
        return self.op("pe", lambda e: e.transpose(out.ap, in_.ap, ident.ap, **kw), [in_, ident], [out])

    def act(self, out, in_, func, bias=None, scale=1.0, accum=None, eng="act"):
        reads = [in_]
        b = bias
        s = scale
        if isinstance(bias, V):
            reads.append(bias)
            b = bias.ap
        if isinstance(scale, V):
            reads.append(scale)
            s = scale.ap
        writes = [out]
        kw = {}
        if accum is not None:
            writes.append(accum)
            kw["accum_out"] = accum.ap
        if b is not None:
            kw["bias"] = b
        return self.op(eng, lambda e: e.activation(out.ap, in_.ap, func, scale=s, **kw), reads, writes)

    def tt(self, out, a, b, op, eng="dve"):
        return self.op(eng, lambda e: e.tensor_tensor(out.ap, a.ap, b.ap, op), [a, b], [out])

    def ts(self, out, a, s1, op0, s2=None, op1=None, accum=None, eng="dve"):
        reads = [a]
        x1, x2 = s1, s2
        if isinstance(s1, V):
            reads.append(s1)
            x1 = s1.ap
        if isinstance(s2, V):
            reads.append(s2)
            x2 = s2.ap
        writes = [out]
        kw = {}
        if op1 is not None:
            kw["op1"] = op1
        if accum is not None:
            writes.append(accum)
            kw["accum_out"] = accum.ap
        return self.op(eng, lambda e: e.tensor_scalar(out.ap, a.ap, x1, x2, op0, **kw), reads, writes)

    def stt(self, out, a, s, b, op0, op1, accum=None, eng="dve"):
        reads = [a, b]
        x = s
        if isinstance(s, V):
            reads.append(s)
            x = s.ap
        writes = [out]
        kw = {}
        if accum is not None:
            writes.append(accum)
            kw["accum_out"] = accum.ap
        return self.op(eng, lambda e: e.scalar_tensor_tensor(out.ap, a.ap, x, b.ap, op0, op1, **kw), reads, writes)

    def copy(self, out, in_, eng="dve"):
        if eng == "act":
            return self.op(eng, lambda e: e.copy(out.ap, in_.ap), [in_], [out])
        return self.op(eng, lambda e: e.tensor_copy(out.ap, in_.ap), [in_], [out])

    def memset(self, out, val, eng="pool"):
        return self.op(eng, lambda e: e.memset(out.ap, val), [], [out])

    def scan(self, out, d0, d1, init, op0=ALU.mult, op1=ALU.add):
        reads = [d0, d1]
        x = init
        if isinstance(init, V):
            reads.append(init)
            x = init.ap
        return self.op("dve", lambda e: e.tensor_tensor_scan(out.ap, d0.ap, d1.ap, x, op0, op1), reads, [out])

    def reduce(self, out, in_, op, axis=AX.X, eng="dve"):
        return self.op(eng, lambda e: e.tensor_reduce(out.ap, in_.ap, axis, op), [in_], [out])

    def recip(self, out, in_):
        return self.op("dve", lambda e: e.reciprocal(out.ap, in_.ap), [in_], [out])

    def emit(self):
        nc = self.nc
        ops = self.ops
        n = len(ops)
        has_dep = [False] * n
        def skip(i, j):
            eng, _, _, dma, raw = ops[i]
            je, _, _, jd, _ = ops[j]
            if jd or dma or je != eng:
                return False
            if eng == "pe":
                return True
            if not SAME_ENG_SYNC:
                return True
            if SAME_ENG_SYNC == "all":
                return False
            return j not in raw

        for i, (eng, fn, deps, dma, raw) in enumerate(ops):
            for j in deps:
                if skip(i, j):
                    continue
                has_dep[j] = True
        engs = ["pe", "act", "dve", "pool", "sp"]
        sems = {e: self.es.enter_context(nc.semaphore("s_" + e)) for e in engs}
        dsems = {e: [self.es.enter_context(nc.semaphore("d_%s_%d" % (e, s))) for s in range(DMA_SLOTS)]
                 for e in ("sp", "act", "pool")}
        cnt = {e: 0 for e in engs}
        dcnt = {e: 0 for e in engs}
        sig = [None] * n
        prog = {e: [] for e in engs}
        waited = {e: {} for e in engs}

        def add_wait(e, sem, val):
            w = waited[e]
            key = id(sem)
            if w.get(key, 0) >= val:
                return
            w[key] = val
            prog[e].append(("w", sem, val))

        for i, (eng, fn, deps, dma, raw) in enumerate(ops):
            best = {}
            for j in deps:
                if skip(i, j):
                    continue
                s, v = sig[j]
                key = id(s)
                if key not in best or best[key][1] < v:
                    best[key] = (s, v)
            if dma:
                q = dcnt[eng]
                dcnt[eng] += 1
                slot = q % DMA_SLOTS
                rnd = q // DMA_SLOTS
                s = dsems[eng][slot]
                if rnd > 0:
                    key = id(s)
                    v = 16 * rnd
                    if key not in best or best[key][1] < v:
                        best[key] = (s, v)
                sig[i] = (s, 16 * (rnd + 1))
                for s_, v_ in best.values():
                    add_wait(eng, s_, v_)
                prog[eng].append(("d", fn, s))
            else:
                for s_, v_ in best.values():
                    add_wait(eng, s_, v_)
                if has_dep[i]:
                    cnt[eng] += 1
                    sig[i] = (sems[eng], cnt[eng])
                    prog[eng].append(("i", fn, sems[eng]))
                else:
                    prog[eng].append(("i", fn, None))
        last = {}
        for i in self.out_dmas:
            s, v = sig[i]
            if id(s) not in last or last[id(s)][1] < v:
                last[id(s)] = (s, v)
        for s, v in last.values():
            add_wait("sp", s, v)
        self.stats = {e: len(prog[e]) for e in engs}

        def run(e, items):
            for it in items:
                if it[0] == "w":
                    e.wait_ge(it[1], it[2])
                elif it[0] == "d":
                    it[1](e).then_inc(it[2], 16)
                else:
                    ins = it[1](e)
                    if it[2] is not None:
                        ins.then_inc(it[2], 1)

        with nc.Block() as block:
            @block.tensor
            def _(e):
                run(e, prog["pe"])

            @block.scalar
            def _(e):
                run(e, prog["act"])

            @block.vector
            def _(e):
                run(e, prog["dve"])

            @block.gpsimd
            def _(e):
                run(e, prog["pool"])

            @block.sync
            def _(e):
                run(e, prog["sp"])
        self.es.close()


D = 1024
DEPTH = 2
NS = 16
PLE = 256
DFF = 4096
EPS = 1e-6
D_IN = 7184
OFF_U, OFF_QKV, OFF_Z, OFF_B, OFF_A = 0, 512, 2048, 2560, 2564
OFF_MQ, OFF_MK, OFF_MV, OFF_MO, OFF_MI, OFF_MF, OFF_G = 2568, 2824, 3080, 3592, 4104, 4108, 4112

WEIGHT_NAMES = ['norm1_g', 'w_in', 's5_A_re', 's5_A_im', 's5_log_dt', 's5_B_re', 's5_B_im', 's5_C_re', 's5_C_im',
                's5_D', 's5_w_glu', 's5_b_glu', 'gdn_conv_w', 'gdn_A_log', 'gdn_dt_bias', 'gdn_norm_g',
                'ml_b_i', 'ml_b_f', 'ml_norm_g', 'w_br_s5', 'w_br_gdn', 'w_br_ml', 'w_out', 'norm2_g',
                'w_up', 'w_down', 'w_ple', 'w_ple_gate', 'final_norm_g']
WEIGHT_SHAPES = {
    'norm1_g': (2, 1024), 'w_in': (2, 1024, 7184), 's5_A_re': (2, 32, 64), 's5_A_im': (2, 32, 64),
    's5_log_dt': (2, 32), 's5_B_re': (2, 32, 64, 16), 's5_B_im': (2, 32, 64, 16), 's5_C_re': (2, 32, 16, 64),
    's5_C_im': (2, 32, 16, 64), 's5_D': (2, 512), 's5_w_glu': (2, 512, 512), 's5_b_glu': (2, 512),
    'gdn_conv_w': (2, 4, 1536), 'gdn_A_log': (2, 4), 'gdn_dt_bias': (2, 4), 'gdn_norm_g': (2, 128),
    'ml_b_i': (2, 4), 'ml_b_f': (2, 4), 'ml_norm_g': (2, 512), 'w_br_s5': (2, 512, 1024),
    'w_br_gdn': (2, 512, 1024), 'w_br_ml': (2, 512, 1024), 'w_out': (2, 1024, 1024), 'norm2_g': (2, 1024),
    'w_up': (2, 1024, 4096), 'w_down': (2, 4096, 1024), 'w_ple': (2, 256, 1024), 'w_ple_gate': (2, 1024, 1024),
    'final_norm_g': (1024,),
}

C_ID, C_ONE, C_EPS, C_TV = 0, 128, 256, 704
C_TRIU, C_SEL63, C_MSLN, C_NEGU = 322, 386, 514, 578
NCST = 704 + 129
CH = 128
PI = float(np.pi)
TWO_PI = float(2 * np.pi)
CW1 = 6.28125
CW2 = float(2 * np.pi - 6.28125)


def make_consts():
    c = np.zeros((128, NCST), np.float32)
    c[:, C_ID:C_ID + 128] = np.eye(128, dtype=np.float32)
    c[:, C_ONE:C_ONE + 128] = 1.0
    c[:, C_EPS] = EPS
    c[:, C_TV:C_TV + CH + 1] = np.arange(CH + 1, dtype=np.float32)[None, :]
    i = np.arange(64)
    c[0:64, C_TRIU:C_TRIU + 64] = (i[:, None] <= i[None, :]).astype(np.float32)
    c[63, C_SEL63:C_SEL63 + 128] = 1.0
    c[0:64, C_MSLN:C_MSLN + 64] = -(i[:, None] > i[None, :]).astype(np.float32)
    c[0:64, C_NEGU:C_NEGU + 64] = np.where(i[:, None] >= i[None, :], 0.0, -30000.0)
    return c


class Ctx:
    pass


WCH = 2048
NBLK = 24


def build_program(NP, stage="all", debug=False):
    BR_S5 = stage in ("s5", "all")
    BR_GDN = stage in ("gdn", "all")
    BR_ML = stage in ("ml", "all")
    nc = bass.Bass("TRN2", target_bir_lowering=False)
    P = Prog(nc)
    NTOK = NP + NS
    tiles = [(i * 512, 512) for i in range(NP // 512)] + [(NP, NS)]
    n_ptiles = NP // 512

    xin = P.dram("xin", [NTOK, D])
    pin = P.dram("pin", [DEPTH, NTOK, PLE])
    cstd = P.dram("cst", [128, NCST])
    W = {n: P.dram(n, WEIGHT_SHAPES[n]) for n in WEIGHT_NAMES}
    st_s5 = [P.dram("st_s5re", [DEPTH, NS, 2048]), P.dram("st_s5im", [DEPTH, NS, 2048])]
    yout = P.dram("y", [NTOK, D], kind="ExternalOutput")
    o_s5p = [P.dram("o_s5re_p", [DEPTH, 2048], kind="ExternalOutput"), P.dram("o_s5im_p", [DEPTH, 2048], kind="ExternalOutput")]
    o_s5s = [P.dram("o_s5re_s", [DEPTH, NS, 2048], kind="ExternalOutput"), P.dram("o_s5im_s", [DEPTH, NS, 2048], kind="ExternalOutput")]

    cst = P.sb("cst_sb", [128, NCST])
    ident = cst[:, C_ID:C_ID + 128]
    ones = cst[:, C_ONE:C_ONE + 128]
    epsc = cst[:, C_EPS:C_EPS + 1]
    tvec = cst[:, C_TV:C_TV + CH + 1]
    xT = P.sb("xT", [128, 8, NTOK])
    xn = P.sb("xn", [128, 8, 512], BF16)
    mg = P.sb("mg", [128, 8, 512], BF16)
    rstd = P.sb("rstd", [128, 512])
    sqt = P.sb("sqt", [128, 512])
    tmpA = P.sb("tmpA", [128, 512])
    tmpB = P.sb("tmpB", [128, 512])
    g1 = P.sb("g1", [128, DEPTH, 8])
    g2 = P.sb("g2", [128, DEPTH, 8])
    pT = P.sb("pT", [128, 2, 512], BF16)
    ldp = P.sb("ldp", [128, 256])
    NWR = 5
    wring = [P.sb("wr%d" % i, [128, WCH], BF16) for i in range(NWR)]
    big = P.sb("big", [128, NBLK * 512])
    psum = P.ps("psum", [128, 8 * 512])
    st = Ctx()
    st.bank = 0
    st.wr = 0
    st.reserved = set()
    st.tmp = 0
    st.dbg = {}

    def dbg(name, v, shape):
        if not debug or name in st.dbg:
            return
        d = P.dram("dbg_" + name, list(shape), kind="ExternalOutput")
        st.dbg[name] = d
        P.dma("pool", d, v, is_out=True)

    def scr(b0, nb, dtype=F32):
        v = V(big.ap[:, b0 * 512:(b0 + nb) * 512], [("big", b) for b in range(b0, b0 + nb)])
        if dtype != F32:
            v = v.cast(dtype)
        return v

    def bank(n=1):
        while True:
            b = st.bank % 8
            if b % n == 0 and b + n <= 8 and not any((b + i) in st.reserved for i in range(n)):
                break
            st.bank += 1
        st.bank += n
        return V(psum.ap[:, b * 512:(b + n) * 512], [("ps", b + i) for i in range(n)])

    BIGW = ['w_in', 's5_w_glu', 'w_br_s5', 'w_br_gdn', 'w_br_ml', 'w_out', 'w_up', 'w_down', 'w_ple', 'w_ple_gate']
    WIN_GROUPS = [(0, 512), (512, 2568), (2568, 4112), (4112, 5136), (5136, 6160), (6160, 7184)]
    Wb = {}
    for l in range(DEPTH):
        for n in BIGW:
            shp = WEIGHT_SHAPES[n][1:]
            Wb[(n, l)] = P.dram("%s_bf%d" % (n, l), list(shp), BF16, kind="Internal")

    def wkeys(name, l, c0, c1):
        if name != 'w_in':
            return (("drbf", name, l, 0),)
        return tuple(("drbf", name, l, g) for g, (a, b) in enumerate(WIN_GROUPS) if a < c1 and c0 < b)

    st.pending = []
    CAST_ORDER = [('w_in', 0), ('s5_w_glu', 0), ('w_br_s5', 0), ('w_in', 3), ('w_in', 1), ('w_br_gdn', 0), ('w_in', 4),
                  ('w_in', 2), ('w_br_ml', 0), ('w_in', 5), ('w_out', 0), ('w_up', 0), ('w_down', 0), ('w_ple_gate', 0), ('w_ple', 0)]

    def cast_weights(l):
        for (n, g) in CAST_ORDER:
            (a, b) = WIN_GROUPS[g] if n == 'w_in' else (0, WEIGHT_SHAPES[n][2])
            K_ = WEIGHT_SHAPES[n][1]
            for r0 in range(0, K_, 1024):
                r1 = min(K_, r0 + 1024)
                dst = V(Wb[(n, l)].ap[r0:r1, a:b], wkeys(n, l, a, b))
                st.pending.append((dst, W[n][l][r0:r1, a:b]))

    def pump(k=1):
        for _ in range(k):
            if st.pending:
                dst, src = st.pending.pop(0)
                P.dma("pool", dst, src)

    def wload(name, l, KC, c0, cols, k0=0):
        buf = wring[st.wr % NWR]
        st.wr += 1
        dst = buf[:, 0:KC * cols].re("p (k n) -> p k n", k=KC)
        src = V(Wb[(name, l)].ap.rearrange("(k p) n -> p k n", p=128)[:, k0:k0 + KC, c0:c0 + cols], wkeys(name, l, c0, c0 + cols))
        while st.pending and any(kk in st.pending[0][0].keys for kk in src.keys) or \
                any(any(kk in p[0].keys for kk in src.keys) for p in st.pending):
            pump(1)
        P.dma("sp", dst, src)
        if st.wr % 3 == 0:
            pump(1)
        return dst

    P.dma("sp", cst, cstd)
    cast_weights(0)
    cast_weights(1)
    pump(4)
    P.dma("sp", g1, W['norm1_g'].re("l (k p) -> p l k", p=128), allow_slow_non_contiguous=True)
    P.dma("sp", g2, W['norm2_g'].re("l (k p) -> p l k", p=128), allow_slow_non_contiguous=True)

    for b0 in range(0, NTOK, 128):
        nb = min(128, NTOK - b0)
        ldx = scr(2 * ((b0 // 128) % 3), 2)
        P.dma("sp", ldx[0:nb, :], xin[b0:b0 + nb, :])
        pb = bank(2)
        for kc in range(8):
            P.tr(pb[:, kc * 128:kc * 128 + nb], ldx[0:nb, kc * 128:(kc + 1) * 128], ident[0:nb, 0:nb])
        P.copy(xT[:, :, b0:b0 + nb], pb.re("p (k n) -> p k n", k=8)[:, :, 0:nb], eng="act")

    def rmsnorm(xt, T, gcol):
        pb = bank()
        sqs = (sqt, tmpA, tmpB)
        for kc in range(8):
            sq_ = sqs[kc % 3]
            P.act(sq_[:, :T], xt[:, kc, :], AF.Square)
            P.mm(pb[:, :T], ones, sq_[:, :T], start=(kc == 0), stop=(kc == 7))
        P.act(rstd[:, :T], pb[:, :T], AF.Ln, bias=epsc, scale=1.0 / D)
        P.act(rstd[:, :T], rstd[:, :T], AF.Exp, scale=-0.5)
        for kc in range(8):
            P.stt(xn[:, kc, :T], xt[:, kc, :], gcol[:, kc:kc + 1], rstd[:, :T], ALU.mult, ALU.mult)

    def dense_fm(name, l, KC, n_out_tiles, act, T, cb, c_base=0, k0=0):
        per = max(1, WCH // (KC * 128))
        for f0 in range(0, n_out_tiles, per):
            nt = min(per, n_out_tiles - f0)
            wb = wload(name, l, KC, c_base + f0 * 128, nt * 128, k0=k0)
            for j in range(nt):
                pb = bank()
                for kc in range(KC):
                    P.mm(pb[:, :T], wb[:, kc, j * 128:(j + 1) * 128], act[:, kc, :T], start=(kc == 0), stop=(kc == KC - 1))
                cb(f0 + j, pb)

    def branch_out(l, T, ybT, wname, br):
        for f0 in range(0, 8, 2):
            wg = wload('w_in', l, 8, OFF_G + br * 1024 + f0 * 128, 256)
            wbr = wload(wname, l, 4, f0 * 128, 256)
            for j in range(2):
                ft = f0 + j
                pg = bank()
                for kc in range(8):
                    P.mm(pg[:, :T], wg[:, kc, j * 128:(j + 1) * 128], xn[:, kc, :T], start=(kc == 0), stop=(kc == 7))
                pp = bank()
                fo = j * 128
                for kc in range(4):
                    P.mm(pp[:, :T], wbr[:, kc, fo:fo + 128], ybT[:, kc, :T], start=(kc == 0), stop=(kc == 3))
                tA_, tB_ = ((tmpA, tmpB), (sqt, rstd))[ft % 2]
                P.act(tA_[:, :T], pg[:, :T], AF.Sigmoid)
                P.tt(tB_[:, :T], tA_[:, :T], pp[:, :T], ALU.mult)
                P.tt(mg[:, ft, :T], mg[:, ft, :T], tB_[:, :T], ALU.add, eng="pool")

    s5c = P.sb("s5c", [128, 12, 16])
    A_RE, A_IM, DT, MAG, TH, LR, LI, FR, FI, T1, T2, T3 = [s5c[:, i, :].k(("s5c", i)) for i in range(12)]
    cosT = P.sb("cosT", [128, 16, CH + 1])
    sinT = P.sb("sinT", [128, 16, CH + 1])
    nsinC = P.sb("nsinC", [128, 16])
    bbT = P.sb("bbT", [128, 16, 2, 128], BF16)
    CTp = P.sb("CTp", [128, 16, 2, 128], BF16)
    s5D = P.sb("s5D", [128, 4])
    s5bg = P.sb("s5bg", [128, 4])
    carry = P.sb("carry", [128, 16, 2])
    ctmp = P.sb("ctmp", [128, 16, 2])
    hlast = P.sb("hlast", [128, 2, 16])

    def sin_reduced(dst, ang, q, qi, m):
        P.ts(q, ang, 1.0 / TWO_PI, ALU.mult)
        P.copy(qi, q)
        P.copy(q, qi)
        P.stt(m, q, -CW1, ang, ALU.mult, ALU.add)
        P.stt(m, q, -CW2, m, ALU.mult, ALU.add)
        P.ts(q, m, PI, ALU.is_gt)
        P.stt(m, q, -TWO_PI, m, ALU.mult, ALU.add)
        P.ts(q, m, -PI, ALU.is_lt)
        P.stt(m, q, TWO_PI, m, ALU.mult, ALU.add)
        P.act(dst, m, AF.Sin)

    def s5_setup(l):
        n = 16 * (CH + 1)
        P.dma("sp", A_RE, W['s5_A_re'][l].re("(st gi) p -> (gi p) st", gi=2), allow_slow_non_contiguous=True)
        P.dma("sp", A_IM, W['s5_A_im'][l].re("(st gi) p -> (gi p) st", gi=2), allow_slow_non_contiguous=True)
        for gi in range(2):
            P.dma("sp", DT[gi * 64:(gi + 1) * 64, :],
                  W['s5_log_dt'][l].re("(st gi) -> gi st", gi=2)[gi:gi + 1, :].bc([64, 16]), allow_slow_non_contiguous=True)
        P.dma("sp", s5D, W['s5_D'][l].re("(ct p) -> p ct", p=128), allow_slow_non_contiguous=True)
        P.dma("sp", s5bg, W['s5_b_glu'][l].re("(ct p) -> p ct", p=128), allow_slow_non_contiguous=True)
        P.act(DT, DT, AF.Exp)
        P.tt(MAG, A_RE, DT, ALU.mult)
        P.act(MAG, MAG, AF.Exp)
        P.tt(TH, A_IM, DT, ALU.mult)
        ang = scr(0, 5)[:, 0:n]
        q = scr(5, 5)[:, 0:n]
        qi = scr(10, 5).cast(I32)[:, 0:n]
        m = scr(15, 5)[:, 0:n]
        P.tt(ang.re("p (s t) -> p s t", s=16), TH.bcast(2, CH + 1), tvec.bcast(1, 16), ALU.mult)
        sin_reduced(sinT.re("p s t -> p (s t)"), ang, q, qi, m)
        P.ts(ang, ang, PI / 2, ALU.add)
        sin_reduced(cosT.re("p s t -> p (s t)"), ang, q, qi, m)
        P.ts(nsinC, sinT[:, :, CH], -1.0, ALU.mult)
        P.tt(LR, MAG, cosT[:, :, 1], ALU.mult)
        P.tt(LI, MAG, sinT[:, :, 1], ALU.mult)
        P.tt(T1, A_RE, A_RE, ALU.mult)
        P.tt(T2, A_IM, A_IM, ALU.mult)
        P.tt(T1, T1, T2, ALU.add)
        P.recip(T1, T1)
        P.ts(T2, LR, -1.0, ALU.add)
        P.tt(FR, T2, A_RE, ALU.mult)
        P.tt(T3, LI, A_IM, ALU.mult)
        P.tt(FR, FR, T3, ALU.add)
        P.tt(FR, FR, T1, ALU.mult)
        P.tt(FI, LI, A_RE, ALU.mult)
        P.tt(T3, T2, A_IM, ALU.mult)
        P.tt(FI, FI, T3, ALU.subtract)
        P.tt(FI, FI, T1, ALU.mult)
        Bre = scr(0, 1)[:, 0:256].re("p (s c) -> p s c", s=16)
        Bim = scr(1, 1)[:, 0:256].re("p (s c) -> p s c", s=16)
        bbr = scr(2, 1)[:, 0:256].re("p (s c) -> p s c", s=16)
        bbi = scr(3, 1)[:, 0:256].re("p (s c) -> p s c", s=16)
        tt1 = scr(4, 1)[:, 0:256].re("p (s c) -> p s c", s=16)
        P.dma("sp", Bre, W['s5_B_re'][l].re("(st gi) p c -> (gi p) st c", gi=2), allow_slow_non_contiguous=True)
        P.dma("sp", Bim, W['s5_B_im'][l].re("(st gi) p c -> (gi p) st c", gi=2), allow_slow_non_contiguous=True)
        frb, fib = FR.bcast(2, 16), FI.bcast(2, 16)
        P.tt(bbr, Bre, frb, ALU.mult)
        P.tt(tt1, Bim, fib, ALU.mult)
        P.tt(bbr, bbr, tt1, ALU.subtract)
        P.tt(bbi, Bim, frb, ALU.mult)
        P.tt(tt1, Bre, fib, ALU.mult)
        P.tt(bbi, bbi, tt1, ALU.add)
        bbBD = scr(8, 8).re("p (ct j r n) -> p ct j r n", ct=4, j=4, r=2)
        P.memset(bbBD, 0.0)
        for r, bbx in enumerate((bbr, bbi)):
            b4 = bbx.re("p (ct j) c -> p ct j c", ct=4)
            for j in range(4):
                for gi in range(2):
                    P.copy(bbBD[gi * 64:(gi + 1) * 64, :, j, r, 32 * j + 16 * gi:32 * j + 16 * gi + 16],
                           b4[gi * 64:(gi + 1) * 64, :, j, :], eng="pool")
        for ct in range(4):
            for r in range(2):
                pb = bank()
                for j in range(4):
                    P.tr(pb[:, j * 128:(j + 1) * 128], bbBD[:, ct, j, r, :], ident)
                P.copy(bbT[:, 4 * ct:4 * ct + 4, r, :], pb.re("p (j n) -> p j n", j=4), eng="act")
        P.memset(CTp, 0.0)
        CnBD = scr(16, 4).re("p (s n) -> p s n", s=16)
        for r, cname in enumerate(('s5_C_re', 's5_C_im')):
            P.memset(CnBD[0:32], 0.0)
            for gi in range(2):
                P.dma("sp", CnBD[gi * 16:(gi + 1) * 16, :, gi * 64:(gi + 1) * 64],
                      W[cname][l].re("(st gi) c p -> gi c st p", gi=2)[gi], allow_slow_non_contiguous=True)
            pb = bank()
            for s_ in range(16):
                P.tr(pb[:, s_ * 32:(s_ + 1) * 32], CnBD[0:32, s_, :], ident[0:32, 0:32])
            pv = pb.re("p (ct j c) -> p ct j c", ct=4, j=4)
            c4 = CTp.re("p (ct j) r n -> p ct j r n", ct=4)
            for j in range(4):
                P.act(c4[:, :, j, r, 32 * j:32 * j + 32], pv[:, :, j, :], AF.Copy, scale=(1.0 if r == 0 else -1.0))

    def s5_branch(l, ti, t0, T):
        is_s = (ti == n_ptiles)
        uT = scr(0, 4).re("p (c t) -> p c t", c=4)
        uTb = scr(4, 2, BF16).re("p (c t) -> p c t", c=4)
        y2b = scr(10, 2, BF16).re("p (c t) -> p c t", c=4)
        ysT = scr(12, 2, BF16).re("p (c t) -> p c t", c=4)
        plist = list(range(20, NBLK)) if is_s else [6, 7, 8, 9] + list(range(14, NBLK))

        def tmp():
            b = plist[st.tmp % len(plist)]
            st.tmp += 1
            return scr(b, 1)

        def cb_u(ft, pb):
            P.copy(uT[:, ft, :T], pb[:, :T], eng="act")
            P.copy(uTb[:, ft, :T], pb[:, :T], eng="act")
        dense_fm('w_in', l, 8, 4, xn, T, cb_u, c_base=OFF_U)

        if is_s:
            h0 = []
            for r in range(2):
                lds = scr(6, 4)
                P.dma("sp", lds[0:NS, :], st_s5[r][l])
                pb = bank()
                for s_ in range(16):
                    P.tr(pb[:, s_ * NS:(s_ + 1) * NS], lds[0:NS, s_ * 128:(s_ + 1) * 128], ident[0:NS, 0:NS])
                hh = scr(14 + r, 1)[:, 0:256]
                P.copy(hh, pb[:, 0:256], eng="act")
                h0.append(hh.re("p (s n) -> p s n", s=16))
            pbu = [bank(), bank()]
            for s_ in range(16):
                for r in range(2):
                    P.mm(pbu[r][:, s_ * NS:(s_ + 1) * NS], bbT[:, s_, r, :], uTb[:, s_ // 4, :NS])
            lrb, lib = LR.bcast(2, NS), LI.bcast(2, NS)
            hr = scr(16, 1)[:, 0:256].re("p (s n) -> p s n", s=16)
            hi = scr(17, 1)[:, 0:256].re("p (s n) -> p s n", s=16)
            t1 = scr(18, 1)[:, 0:256].re("p (s n) -> p s n", s=16)
            P.tt(hr, h0[0], lrb, ALU.mult)
            P.tt(t1, h0[1], lib, ALU.mult)
            P.tt(hr, hr, t1, ALU.subtract)
            P.tt(hr, hr, pbu[0][:, 0:256].re("p (s n) -> p s n", s=16), ALU.add)
            P.tt(hi, h0[1], lrb, ALU.mult)
            P.tt(t1, h0[0], lib, ALU.mult)
            P.tt(hi, hi, t1, ALU.add)
            P.tt(hi, hi, pbu[1][:, 0:256].re("p (s n) -> p s n", s=16), ALU.add)
            hb = scr(19, 1, BF16)[:, 0:512].re("p (r s n) -> p r s n", r=2, s=16)
            P.copy(hb[:, 0], hr, eng="act")
            P.copy(hb[:, 1], hi, eng="act")
            for r, hx in enumerate((hr, hi)):
                pb4 = bank(4)
                for s_ in range(16):
                    P.tr(pb4[0:NS, s_ * 128:(s_ + 1) * 128], hx[:, s_, :], ident)
                lds = scr(6, 4)
                P.copy(lds[0:NS, :], pb4[0:NS, :], eng="act")
                P.dma("pool", o_s5s[r][l], lds[0:NS, :], is_out=True)

        nch = max(1, T // CH)

        def v3(x):
            return x.re("p (k t) -> p k t", k=nch)

        def ck(tile_, s_, c):
            return V(tile_.ap[:, s_, c:c + 1], ((tile_.keys[0][1], s_, c),))

        for ct in range(4):
            py = bank()
            pyb = (st.bank - 1) % 8
            st.reserved.add(pyb)
            for jp in ((0, 1), (2, 3)):
                hb_pair = []
                if is_s:
                    for j in jp:
                        s_ = 4 * ct + j
                        hb_pair.append((hb[:, 0, s_, :], hb[:, 1, s_, :]))
                else:
                    AC = []
                    for j in jp:
                        s_ = 4 * ct + j
                        pre, pim = bank(), bank()
                        P.mm(pre[:, :T], bbT[:, s_, 0, :], uTb[:, ct, :T])
                        P.mm(pim[:, :T], bbT[:, s_, 1, :], uTb[:, ct, :T])
                        Rr, Ii = tmp(), tmp()
                        P.copy(Rr, pre, eng="act")
                        P.copy(Ii, pim, eng="act")
                        cb_ = cosT[:, s_, 0:CH].bcast(1, nch)
                        sb_ = sinT[:, s_, 0:CH].bcast(1, nch)
                        A, B, C, Dd = tmp(), tmp(), tmp(), tmp()
                        P.tt(v3(B), v3(Ii), sb_, ALU.mult, eng="pool")
                        P.tt(v3(Dd), v3(Rr), sb_, ALU.mult, eng="pool")
                        P.tt(v3(A), v3(Rr), cb_, ALU.mult)
                        P.tt(v3(C), v3(Ii), cb_, ALU.mult)
                        P.tt(A, A, B, ALU.add)
                        P.tt(C, C, Dd, ALU.subtract)
                        AC.append((A, C))
                    G = [(tmp(), tmp()) for _ in jp]
                    for k in range(nch):
                        sl = slice(k * CH, (k + 1) * CH)
                        first = (ti == 0 and k == 0)
                        e = (k + 1) * CH - 1
                        for idx, j in enumerate(jp):
                            s_ = 4 * ct + j
                            rb = MAG[:, s_:s_ + 1].bc([128, CH])
                            P.scan(G[idx][0][:, sl], rb, AC[idx][0][:, sl], 0.0 if first else ck(carry, s_, 0))
                            P.scan(G[idx][1][:, sl], rb, AC[idx][1][:, sl], 0.0 if first else ck(carry, s_, 1))
                        for idx, j in enumerate(jp):
                            s_ = 4 * ct + j
                            cC = cosT[:, s_, CH:CH + 1]
                            P.ts(ck(ctmp, s_, 0), G[idx][0][:, e:e + 1], cC, ALU.mult)
                            P.ts(ck(ctmp, s_, 1), G[idx][1][:, e:e + 1], cC, ALU.mult)
                        for idx, j in enumerate(jp):
                            s_ = 4 * ct + j
                            sC, nsC = sinT[:, s_, CH:CH + 1], nsinC[:, s_:s_ + 1]
                            P.stt(ck(carry, s_, 0), G[idx][1][:, e:e + 1], nsC, ck(ctmp, s_, 0), ALU.mult, ALU.add)
                            P.stt(ck(carry, s_, 1), G[idx][0][:, e:e + 1], sC, ck(ctmp, s_, 1), ALU.mult, ALU.add)
                    for idx, j in enumerate(jp):
                        s_ = 4 * ct + j
                        Gr, Gi = G[idx]
                        cb_ = cosT[:, s_, 0:CH].bcast(1, nch)
                        sb_ = sinT[:, s_, 0:CH].bcast(1, nch)
                        E, F, G2, H2 = tmp(), tmp(), tmp(), tmp()
                        P.tt(v3(F), v3(Gi), sb_, ALU.mult, eng="pool")
                        P.tt(v3(H2), v3(Gr), sb_, ALU.mult, eng="pool")
                        P.tt(v3(E), v3(Gr), cb_, ALU.mult)
                        P.tt(v3(G2), v3(Gi), cb_, ALU.mult)
                        Hrb, Hib = tmp().cast(BF16)[:, 0:512], tmp().cast(BF16)[:, 0:512]
                        if ti == n_ptiles - 1:
                            P.tt(E, E, F, ALU.subtract)
                            P.tt(G2, G2, H2, ALU.add)
                            P.copy(Hrb, E, eng="act")
                            P.copy(Hib, G2, eng="act")
                        else:
                            P.tt(Hrb, E, F, ALU.subtract)
                            P.tt(Hib, G2, H2, ALU.add)
                        if ti == n_ptiles - 1:
                            P.copy(hlast[:, 0, s_:s_ + 1], E[:, T - 1:T], eng="act")
                            P.copy(hlast[:, 1, s_:s_ + 1], G2[:, T - 1:T], eng="act")
                        hb_pair.append((Hrb, Hib))
                for idx, j in enumerate(jp):
                    s_ = 4 * ct + j
                    Hrb, Hib = hb_pair[idx]
                    P.mm(py[:, :T], CTp[:, s_, 0, :], Hrb[:, :T], start=(j == 0), stop=False)
                    P.mm(py[:, :T], CTp[:, s_, 1, :], Hib[:, :T], start=False, stop=(j == 3))
            yv = tmp()
            P.stt(yv[:, :T], uT[:, ct, :T], s5D[:, ct:ct + 1], py[:, :T], ALU.mult, ALU.add)
            st.reserved.discard(pyb)
            sq = tmp()
            P.tt(sq[:, :T], yv[:, :T], yv[:, :T], ALU.mult, eng="pool")
            P.ts(sq[:, :T], sq[:, :T], 0.044715, ALU.mult, s2=1.0, op1=ALU.add)
            P.tt(sq[:, :T], sq[:, :T], yv[:, :T], ALU.mult, eng="pool")
            P.act(sq[:, :T], sq[:, :T], AF.Sigmoid, scale=1.5957691216057308)
            P.tt(y2b[:, ct, :T], yv[:, :T], sq[:, :T], ALU.mult)
        if ti == n_ptiles - 1:
            for r in range(2):
                P.dma("pool", o_s5p[r][l].re("(s p) -> p s", p=128), hlast[:, r, :], is_out=True, allow_slow_non_contiguous=True)

        def cb_glu(ft, pb):
            sg = tmp()
            P.act(sg[:, :T], pb[:, :T], AF.Sigmoid, bias=s5bg[:, ft:ft + 1])
            P.tt(ysT[:, ft, :T], y2b[:, ft, :T], sg[:, :T], ALU.mult)
        dense_fm('s5_w_glu', l, 4, 4, y2b, T, cb_glu)
        branch_out(l, T, ysT, 'w_br_s5', 0)

    triU = cst[0:64, C_TRIU:C_TRIU + 64]
    msln = cst[0:64, C_MSLN:C_MSLN + 64]
    negU = cst[0:64, C_NEGU:C_NEGU + 64]
    id64 = cst[0:64, C_ID:C_ID + 64]
    ones64 = cst[0:64, C_ONE:C_ONE + 128]
    st_conv = P.dram("st_conv", [DEPTH, NS * 3, 1536])
    st_gdn = P.dram("st_gdn", [DEPTH, NS, 4, 128, 128])
    o_conv_p = P.dram("o_conv_p", [DEPTH, 3, 1536], kind="ExternalOutput")
    o_conv_s = P.dram("o_conv_s", [DEPTH, NS * 3, 1536], kind="ExternalOutput")
    o_gdn_p = P.dram("o_gdn_p", [DEPTH, 4, 128, 128], kind="ExternalOutput")
    o_gdn_s = P.dram("o_gdn_s", [DEPTH, NS, 4, 128, 128], kind="ExternalOutput")
    cw = P.sb("cw", [128, 12, 4])
    gA = P.sb("gA", [64, 4])
    gdtb = P.sb("gdtb", [64, 4])
    gng = P.sb("gng", [64, 128])
    gngc = P.sb("gngc", [128, 1])
    gS = P.sb("gS", [128, 4, 128])
    gtail = P.sb("gtail", [128, 12, 3])
    gt = P.sb("gt", [64, 12, 32])
    glS = P.sb("glS", [128, 32])

    def gdn_setup(l):
        for j in range(4):
            P.dma("sp", cw[:, :, j], W['gdn_conv_w'][l][j].re("(ct p) -> p ct", p=128), allow_slow_non_contiguous=True)
        P.dma("sp", gA, W['gdn_A_log'][l].re("(o h) -> o h", o=1).bc([64, 4]))
        P.dma("sp", gdtb, W['gdn_dt_bias'][l].re("(o h) -> o h", o=1).bc([64, 4]))
        P.dma("sp", gng, W['gdn_norm_g'][l].re("(o n) -> o n", o=1).bc([64, 128]))
        P.dma("sp", gngc, W['gdn_norm_g'][l].re("(n o) -> n o", o=1))
        P.act(gA, gA, AF.Exp)
        P.ts(gA, gA, -1.0, ALU.mult)
        P.memset(gS, 0.0)
        P.memset(gtail, 0.0)
        P.memset(gt, 0.0)

    def softplus_ip(x, t2):
        P.ts(t2, x, -1.0, ALU.mult)
        P.tt(t2, t2, x, ALU.max)
        P.act(t2, t2, AF.Exp, scale=-1.0)
        P.act(t2, t2, AF.Ln, bias=ones[0:x.ap.shape[0], 0:1])
        P.ts(x, x, 0.0, ALU.max)
        P.tt(x, x, t2, ALU.add)

    def gdn_branch(l, ti, t0, T):
        NCk = T // 64
        BETA, G, GC, EG, KD, BEG, X1, X2 = range(8)

        def q8(i):
            return gt[:, i, :].re("p (c h) -> p c h", c=8)

        def c8(v):
            return v.re("p (c n) -> p c n", c=8)
        ygT = scr(22, 2, BF16).re("p (h t) -> p h t", h=4)
        wba = wload('w_in', l, 8, OFF_B, 8)
        pbT = bank()
        for kc in range(8):
            P.mm(pbT[0:8, :T], wba[:, kc, :], xn[:, kc, :T], start=(kc == 0), stop=(kc == 7))
        baT = scr(21, 1)[0:8, :]
        P.copy(baT[:, :T], pbT[0:8, :T], eng="act")
        pba = bank()
        for c in range(NCk):
            P.tr(pba[0:64, c * 8:(c + 1) * 8], baT[:, c * 64:(c + 1) * 64], ident[0:8, 0:8])
        pv = c8(pba[0:64, 0:64])
        P.act(q8(BETA), pv[:, :, 0:4], AF.Sigmoid)
        P.tt(q8(X1), pv[:, :, 4:8], gdtb.bcast(1, 8), ALU.add)
        softplus_ip(gt[:, X1, :], gt[:, X2, :])
        P.tt(q8(G), q8(X1), gA.bcast(1, 8), ALU.mult)
        pg = bank()
        P.mm(pg[0:64, 0:32], triU, gt[:, G, :])
        P.mm(pg[:, 32:64], ones64, gt[:, G, :])
        P.copy(gt[:, GC, :], pg[0:64, 0:32], eng="act")
        P.act(gt[:, EG, :], pg[0:64, 0:32], AF.Exp)
        P.tt(gt[:, KD, :], pg[0:64, 32:64], gt[:, GC, :], ALU.subtract)
        P.act(gt[:, KD, :], gt[:, KD, :], AF.Exp)
        P.act(glS, pg[:, 32:64], AF.Exp)
        P.tt(gt[:, BEG, :], gt[:, BETA, :], gt[:, EG, :], ALU.mult)

        for h in range(4):
            pre = scr(0, 2)[:, 0:515]
            qc, kc_, vc = scr(2, 1), scr(3, 1), scr(4, 1)
            qkb = scr(5, 1, BF16)
            qTb, kTb = qkb[:, 0:512], qkb[:, 512:1024]
            kbg, kdec, vb = c8(scr(6, 2)[0:64]), c8(scr(8, 2)[0:64]), c8(scr(10, 2)[0:64])
            Nm, At = c8(scr(13, 1)[0:64]), c8(scr(14, 1)[0:64])

            def proj_conv(which, dst):
                ct = which * 4 + h
                pre = scr((0, 15, 17)[which], 2)[:, 0:515]

                def s1():
                    wq = wload('w_in', l, 8, OFF_QKV + ct * 128, 128)
                    pb = bank()
                    for kc in range(8):
                        P.mm(pb[:, :T], wq[:, kc, :], xn[:, kc, :T], start=(kc == 0), stop=(kc == 7))
                    P.copy(pre[:, 0:3], gtail[:, ct, :], eng="pool")
                    P.copy(pre[:, 3:3 + T], pb[:, :T], eng="act")
                    P.copy(gtail[:, ct, :], pre[:, T:T + 3], eng="pool")

                def s2():
                    P.ts(dst[:, :T], pre[:, 0:T], cw[:, ct, 0:1], ALU.mult)
                    P.stt(dst[:, :T], pre[:, 1:1 + T], cw[:, ct, 1:2], dst[:, :T], ALU.mult, ALU.add)

                def s3():
                    P.stt(dst[:, :T], pre[:, 2:2 + T], cw[:, ct, 2:3], dst[:, :T], ALU.mult, ALU.add)
                    P.stt(dst[:, :T], pre[:, 3:3 + T], cw[:, ct, 3:4], dst[:, :T], ALU.mult, ALU.add)
                return [s1, s2, s3]

            def l2norm(dst, scl):
                sq = scr(21, 1) if scl == 1.0 else scr(16, 1)

                def s1():
                    P.act(sq[:, :T], dst[:, :T], AF.Square)
                    pb = bank()
                    P.mm(pb[:, :T], ones, sq[:, :T])
                    P.act(sq[:, :T], pb[:, :T], AF.Ln, bias=epsc)
                    P.act(sq[:, :T], sq[:, :T], AF.Exp, scale=-0.5)

                def s2():
                    P.stt(dst[:, :T], dst[:, :T], scl, sq[:, :T], ALU.mult, ALU.mult)
                return [s1, s2]

            for s in proj_conv(1, kc_) + proj_conv(0, qc) + proj_conv(2, vc):
                s()
            wz = wload('w_in', l, 8, OFF_Z + h * 128, 128)
            pz = bank()
            for kc in range(8):
                P.mm(pz[:, :T], wz[:, kc, :], xn[:, kc, :T], start=(kc == 0), stop=(kc == 7))
            zsT = scr(20, 1)
            for dst in (kc_, qc, vc):
                P.act(dst[:, :T], dst[:, :T], AF.Silu)
            P.act(zsT[:, :T], pz[:, :T], AF.Silu)
            for s in l2norm(kc_, 1.0) + l2norm(qc, 128.0 ** -0.5):
                s()
            P.copy(kTb[:, :T], kc_[:, :T], eng="act")
            P.copy(qTb[:, :T], qc[:, :T], eng="act")
            pk = bank(2)
            for c in range(NCk):
                P.tr(pk[0:64, c * 128:(c + 1) * 128], kc_[:, c * 64:(c + 1) * 64], ident)
            P.tt(kbg, c8(pk[0:64, :]), q8(BEG)[:, :, h].bcast(2, 128), ALU.mult)
            P.tt(kdec, c8(pk[0:64, :]), q8(KD)[:, :, h].bcast(2, 128), ALU.mult)
            pKK, pGR = bank(), bank()
            for c in range(NCk):
                sl = slice(c * 64, (c + 1) * 64)
                P.mm(pKK[0:64, sl], kTb[:, sl], kTb[:, sl])
            diag = c8(scr(12, 1)[0:64])
            P.tt(diag, id64.bcast(1, 8), q8(GC)[:, :, h].bcast(2, 64), ALU.mult)
            P.mm(pGR[0:64, :T], ones64[:, 0:64], scr(12, 1)[0:64, :T])
            GR3 = c8(pGR[0:64, :])
            gcb = q8(GC)[:, :, h].bcast(2, 64)
            P.stt(Nm, GR3, -1.0, gcb, ALU.mult, ALU.add)
            P.ts(Nm, Nm, 0.0, ALU.min)
            P.act(Nm, Nm, AF.Exp)
            P.tt(At, GR3, gcb, ALU.subtract)
            P.ts(At, At, 0.0, ALU.min)
            P.act(At, At, AF.Exp)
            P.tt(Nm, Nm, c8(pKK[0:64, :]), ALU.mult)
            P.tt(Nm, Nm, q8(BETA)[:, :, h].bcast(2, 64), ALU.mult)
            P.tt(Nm, Nm, msln.bcast(1, 8), ALU.mult)

            def bg_qcast():
                P.copy(qTb[:, :T], qc[:, :T], eng="act")

            def bg_vtr():
                pv2 = bank(2)
                for c in range(NCk):
                    P.tr(pv2[0:64, c * 128:(c + 1) * 128], vc[:, c * 64:(c + 1) * 64], ident)
                P.tt(vb, c8(pv2[0:64, :]), q8(BETA)[:, :, h].bcast(2, 128), ALU.mult)

            def bg_qk():
                pQK = bank()
                for c in range(NCk):
                    sl = slice(c * 64, (c + 1) * 64)
                    P.mm(pQK[0:64, sl], kTb[:, sl], qTb[:, sl])
                P.tt(At, At, c8(pQK[0:64, :]), ALU.mult)
                P.tt(At, At, triU.bcast(1, 8), ALU.mult)

            def bg_qdec():
                dg2 = c8(scr(21, 1)[0:64])
                P.tt(dg2, id64.bcast(1, 8), q8(EG)[:, :, h].bcast(2, 64), ALU.mult)
                pe = bank()
                P.mm(pe[:, :T], ones64, scr(21, 1)[0:64, :T])
                P.tt(qc[:, :T], qc[:, :T], pe[:, :T], ALU.mult)
            bgq = [bg_qk, bg_vtr, bg_qdec]

            def run_bg(n):
                for _ in range(n):
                    if bgq:
                        bgq.pop(0)()

            Mm, Pm, Qm, Rm, Lm = (c8(scr(b, 1)[0:64]) for b in (12, 15, 16, 17, 18))
            ptr = bank()
            for c in range(NCk):
                P.tr(ptr[0:64, c * 64:(c + 1) * 64], Nm[:, c, :], id64)
            P.copy(Mm, c8(ptr[0:64, :]), eng="act")
            P.tt(Rm, Mm, id64.bcast(1, 8), ALU.add)
            P.tt(Lm, Nm, id64.bcast(1, 8), ALU.add, eng="pool")
            Pc, Qc = Nm, Mm
            for k in range(5):
                pQ = bank()
                for c in range(NCk):
                    P.mm(pQ[0:64, c * 64:(c + 1) * 64], Pc[:, c, :], Qc[:, c, :])
                if k < 4:
                    pP = bank()
                    for c in range(NCk):
                        P.mm(pP[0:64, c * 64:(c + 1) * 64], Qc[:, c, :], Pc[:, c, :])
                P.copy(Qm, c8(pQ[0:64, :]), eng="act")
                if k < 4:
                    P.copy(Pm, c8(pP[0:64, :]))
                Pc, Qc = Pm, Qm
                pR = bank()
                for c in range(NCk):
                    P.mm(pR[0:64, c * 64:(c + 1) * 64], Lm[:, c, :], Qm[:, c, :])
                if k < 4:
                    pL = bank()
                    for c in range(NCk):
                        P.mm(pL[0:64, c * 64:(c + 1) * 64], Rm[:, c, :], Pm[:, c, :])
                P.tt(Rm, Rm, c8(pR[0:64, :]), ALU.add)
                if k < 4:
                    P.tt(Lm, Lm, c8(pL[0:64, :]), ALU.add)
                run_bg(3)
            run_bg(len(bgq))
            pW = bank()
            for c in range(NCk):
                P.mm(pW[:, c * 64:(c + 1) * 64], kbg[:, c, :], Rm[:, c, :])
            wT = scr(3, 1)
            P.copy(wT[:, :T], pW[:, :T], eng="act")
            pU = bank(2)
            for c in range(NCk):
                P.mm(pU[0:64, c * 128:(c + 1) * 128], Rm[:, c, :], vb[:, c, :])
            u = c8(scr(0, 2)[0:64])
            P.copy(u, c8(pU[0:64, :]), eng="act")
            o_tm = c8(scr(10, 2)[0:64])
            for c in range(NCk):
                sl = slice(c * 64, (c + 1) * 64)
                pw = bank()
                P.mm(pw[0:64, 0:128], wT[:, sl], gS[:, h, :])
                vn = scr(19, 1)[0:64, (c % 2) * 128:(c % 2) * 128 + 128]
                P.tt(vn, u[:, c, :], pw[0:64, 0:128], ALU.subtract)
                po = bank()
                P.mm(po[0:64, 0:128], qc[:, sl], gS[:, h, :])
                po2 = bank()
                P.mm(po2[0:64, 0:128], At[:, c, :], vn)
                P.copy(o_tm[:, c, :], po[0:64, 0:128], eng="act")
                P.tt(o_tm[:, c, :], o_tm[:, c, :], po2[0:64, 0:128], ALU.add)
                pS = bank()
                P.mm(pS[:, 0:128], kdec[:, c, :], vn)
                P.stt(gS[:, h, :], gS[:, h, :], glS[:, c * 4 + h:c * 4 + h + 1], pS[:, 0:128], ALU.mult, ALU.add)
            sq = c8(scr(8, 2)[0:64])
            P.tt(sq, o_tm, o_tm, ALU.mult)
            ss = gt[:, X1, 0:8]
            P.reduce(ss, sq, ALU.add)
            P.act(ss, ss, AF.Ln, bias=epsc[0:64], scale=1.0 / 128)
            P.act(ss, ss, AF.Exp, scale=-0.5)
            P.tt(o_tm, o_tm, ss.bcast(2, 128), ALU.mult)
            P.tt(o_tm, o_tm, gng.bcast(1, 8), ALU.mult)
            pt = bank()
            for c in range(NCk):
                P.tr(pt[:, c * 64:(c + 1) * 64], o_tm[:, c, :], id64)
            P.tt(ygT[:, h, :T], pt[:, :T], zsT[:, :T], ALU.mult)
        if ti == n_ptiles - 1:
            P.dma("pool", o_gdn_p[l].re("h k v -> k h v"), gS, is_out=True)
            for j in range(3):
                P.dma("pool", o_conv_p[l][j].re("(ct p) -> p ct", p=128), gtail[:, :, j], is_out=True, allow_slow_non_contiguous=True)
        branch_out(l, T, ygT, 'w_br_gdn', 1)

    def gdn_sample(l):
        T = NS
        ygT = scr(22, 2, BF16).re("p (h t) -> p h t", h=4)
        id16 = ident[0:16, 0:16]
        ldc = scr(0, 3)[0:48, :]
        P.dma("sp", ldc, st_conv[l])
        pre = scr(3, 2)[:, 0:768].re("p (ct r j) -> p ct r j", ct=12, r=16)
        pb = bank(2)
        for ct in range(12):
            P.tr(pb[:, ct * 64:ct * 64 + 48], ldc[0:48, ct * 128:(ct + 1) * 128], ident[0:48, 0:48])
        pbv = pb[:, 0:768].re("p (ct x) -> p ct x", ct=12)[:, :, 0:48].re("p ct (r j) -> p ct r j", r=16)
        P.copy(pre[:, :, :, 0:3], pbv, eng="act")

        def cb_q(ft, pbk):
            P.copy(pre[:, ft, :, 3], pbk[:, :T], eng="act")
        dense_fm('w_in', l, 8, 12, xn, T, cb_q, c_base=OFF_QKV)
        cvf = scr(5, 1)[:, 0:192]
        cv = cvf.re("p (ct r) -> p ct r", ct=12)
        t1 = scr(6, 1)[:, 0:192].re("p (ct r) -> p ct r", ct=12)
        P.tt(cv, pre[:, :, :, 0], cw[:, :, 0].bcast(2, 16), ALU.mult)
        for j in range(1, 4):
            P.tt(t1, pre[:, :, :, j], cw[:, :, j].bcast(2, 16), ALU.mult)
            P.tt(cv, cv, t1, ALU.add)
        P.act(cv, cv, AF.Silu)
        nb = scr(7, 2)[:, 0:576].re("p (ct x) -> p ct x", ct=12)
        P.copy(nb.re("p ct (r j) -> p ct r j", r=16), pre[:, :, :, 1:4], eng="pool")
        pb4 = bank(4)
        for ct in range(12):
            P.tr(pb4[0:48, ct * 128:(ct + 1) * 128], nb[:, ct, :], ident)
        P.copy(ldc, pb4[0:48, 0:1536], eng="act")
        P.dma("pool", o_conv_s[l], ldc, is_out=True)
        sq = scr(6, 1)[:, 0:128]
        P.act(sq, cvf[:, 0:128], AF.Square)
        pb = bank()
        P.mm(pb[:, 0:128], ones, sq)
        P.act(sq, pb[:, 0:128], AF.Ln, bias=epsc)
        P.act(sq, sq, AF.Exp, scale=-0.5)
        P.tt(cvf[:, 0:128], cvf[:, 0:128], sq, ALU.mult)
        P.ts(cvf[:, 0:64], cvf[:, 0:64], 128.0 ** -0.5, ALU.mult)
        wba = wload('w_in', l, 8, OFF_B, 8)
        pba = bank()
        for kc in range(8):
            P.mm(pba[0:16, 0:8], xn[:, kc, :T], wba[:, kc, :], start=(kc == 0), stop=(kc == 7))
        be = gt[0:16, 0, 0:8]
        xg = gt[0:16, 1, 0:4]
        x2 = gt[0:16, 2, 0:4]
        P.act(be[:, 0:4], pba[0:16, 0:4], AF.Sigmoid)
        P.tt(xg, pba[0:16, 4:8], gdtb[0:16, :], ALU.add)
        softplus_ip(xg, x2)
        P.tt(xg, xg, gA[0:16, :], ALU.mult)
        P.act(be[:, 4:8], xg, AF.Exp)
        dg = scr(9, 1)[0:16, 0:128].re("p (n r) -> p n r", n=8)
        P.tt(dg, id16.bcast(1, 8), be.bcast(2, 16), ALU.mult)
        prb = bank()
        P.mm(prb[:, 0:128], ones[0:16, :], scr(9, 1)[0:16, 0:128])
        rowb = scr(10, 1)[:, 0:128].re("p (n r) -> p n r", n=8)
        P.copy(rowb, prb[:, 0:128].re("p (n r) -> p n r", n=8), eng="act")
        betaR = rowb[:, 0:4, :].re("p h r -> p r h")
        egR = rowb[:, 4:8, :].re("p h r -> p r h")
        kq = scr(11, 1)[:, 0:128].re("p (r h n) -> p r h n", r=16, h=4)
        P.copy(kq[:, :, :, 0], cv[:, 4:8, :].re("p h r -> p r h"), eng="pool")
        P.copy(kq[:, :, :, 1], cv[:, 0:4, :].re("p h r -> p r h"), eng="pool")
        vT = cv[:, 8:12, :].re("p h r -> p r h")
        pkq = bank()
        pkqb = (st.bank - 1) % 8
        st.reserved.add(pkqb)
        for r in range(NS):
            Sr = scr(12 + (r % 4), 1).re("p (h v) -> p h v", h=4)
            P.dma("sp", Sr, st_gdn[l, r].re("h k v -> k h v"))
            for h in range(4):
                o = (r * 4 + h) * 2
                P.mm(pkq[:, o:o + 2], Sr[:, h, :], kq[:, r, h, :])
        kqS = scr(16, 1)[:, 0:128].re("p (r h n) -> p r h n", r=16, h=4)
        P.copy(kqS, pkq[:, 0:128].re("p (r h n) -> p r h n", r=16, h=4), eng="act")
        st.reserved.discard(pkqb)

        def rh(b):
            return scr(b, 1)[:, 0:64].re("p (r h) -> p r h", r=16)
        vnew, tA, oT, tB = rh(17), rh(18), rh(19), rh(20)
        P.tt(tA, kqS[:, :, :, 0], egR, ALU.mult)
        P.tt(tA, vT, tA, ALU.subtract)
        P.tt(vnew, tA, betaR, ALU.mult)
        P.tt(tA, kq[:, :, :, 0], kq[:, :, :, 1], ALU.mult)
        pat = bank()
        P.mm(pat[:, 0:64], ones, scr(18, 1)[:, 0:64])
        P.tt(oT, kqS[:, :, :, 1], egR, ALU.mult)
        P.tt(tB, vnew, pat[:, 0:64].re("p (r h) -> p r h", r=16), ALU.mult)
        P.tt(oT, oT, tB, ALU.add)
        P.tt(tB, oT, oT, ALU.mult)
        pss = bank()
        P.mm(pss[:, 0:64], ones, scr(20, 1)[:, 0:64])
        P.act(tB, pss[:, 0:64].re("p (r h) -> p r h", r=16), AF.Ln, bias=epsc, scale=1.0 / 128)
        P.act(tB, tB, AF.Exp, scale=-0.5)
        P.tt(oT, oT, tB, ALU.mult)
        P.ts(oT, oT, gngc[:, 0:1], ALU.mult)
        zT = scr(21, 1)[:, 0:64].re("p (h r) -> p h r", h=4)

        def cb_z(ft, pbk):
            P.act(zT[:, ft, :], pbk[:, :T], AF.Silu)
        dense_fm('w_in', l, 8, 4, xn, T, cb_z, c_base=OFF_Z)
        P.tt(ygT[:, :, :T].re("p h r -> p r h"), oT, zT.re("p h r -> p r h"), ALU.mult)
        for r in range(NS):
            Sr = scr(12 + (r % 4), 1).re("p (h v) -> p h v", h=4)
            P.dma("sp", Sr, st_gdn[l, r].re("h k v -> k h v"))
            So = scr(0 + (r % 2), 1).re("p (h v) -> p h v", h=4)
            for h in range(4):
                dgv = scr(2 + (h % 2), 1)[:, 0:128]
                P.ts(dgv, ident, vnew[:, r, h:h + 1], ALU.mult)
                pvr = bank()
                P.mm(pvr[:, 0:128], ones, dgv)
                P.act(So[:, h, :], Sr[:, h, :], AF.Copy, scale=rowb[:, 4 + h, r:r + 1])
                P.stt(So[:, h, :], pvr[:, 0:128], kq[:, r, h, 0:1], So[:, h, :], ALU.mult, ALU.add)
            P.dma("pool", o_gdn_s[l, r].re("h k v -> k h v"), So, is_out=True)
        branch_out(l, T, ygT, 'w_br_gdn', 1)

    sel63 = cst[0:64, C_SEL63:C_SEL63 + 128]
    st_mC = P.dram("st_mC", [DEPTH, NS, 4, 64, 128])
    st_mn = P.dram("st_mn", [DEPTH, NS * 4, 64])
    st_mm = P.dram("st_mm", [DEPTH, NS, 4])
    o_mC_p = P.dram("o_mC_p", [DEPTH, 4, 64, 128], kind="ExternalOutput")
    o_mC_s = P.dram("o_mC_s", [DEPTH, NS, 4, 64, 128], kind="ExternalOutput")
    o_mn_p = P.dram("o_mn_p", [DEPTH, 4, 64], kind="ExternalOutput")
    o_mn_s = P.dram("o_mn_s", [DEPTH, NS * 4, 64], kind="ExternalOutput")
    o_mm_p = P.dram("o_mm_p", [DEPTH, 4], kind="ExternalOutput")
    o_mm_s = P.dram("o_mm_s", [DEPTH, NS, 4], kind="ExternalOutput")
    mbi = P.sb("mbi", [64, 4])
    mbf = P.sb("mbf", [64, 4])
    mng = P.sb("mng", [64, 512])
    mngc = P.sb("mngc", [128, 4])
    mC = P.sb("mC", [64, 4, 128])
    mn = P.sb("mn", [64, 4])
    mcar = P.sb("mcar", [64, 4])
    mt = P.sb("mt", [64, 12, 32])
    mp = P.sb("mp", [64, 16])
    mns = P.sb("mns", [64, 8])

    def ml_setup(l):
        P.dma("sp", mbi, W['ml_b_i'][l].re("(o h) -> o h", o=1).bc([64, 4]))
        P.dma("sp", mbf, W['ml_b_f'][l].re("(o h) -> o h", o=1).bc([64, 4]))
        P.dma("sp", mng, W['ml_norm_g'][l].re("(o n) -> o n", o=1).bc([64, 512]))
        P.dma("sp", mngc, W['ml_norm_g'][l].re("(h p) -> p h", p=128), allow_slow_non_contiguous=True)
        P.memset(mC, 0.0)
        P.memset(mn, 0.0)
        P.memset(mcar, 0.0)
        P.memset(mt, 0.0)
        P.memset(mp, 0.0)

    def gates_if(pv_i, pv_f, LIv, LFv, X1v, X2v, bi, bf):
        P.tt(X1v, pv_i, bi, ALU.add)
        P.act(LIv, X1v, AF.Tanh, scale=1.0 / 15.0)
        P.ts(LIv, LIv, 15.0, ALU.mult)
        P.tt(X1v, pv_f, bf, ALU.add)
        P.act(X1v, X1v, AF.Tanh, scale=1.0 / 15.0)
        P.ts(X1v, X1v, -15.0, ALU.mult)
        softplus_ip(X1v, X2v)
        P.ts(LFv, X1v, -1.0, ALU.mult)

    def ml_branch(l, ti, t0, T):
        NCk = T // 64
        LI, LF, BC, AS, MX, INTER, MT, SC, X1, X2, NEM, BL = range(12)

        def m8(i):
            return mt[:, i, :].re("p (c h) -> p c h", c=8)

        def c8(v):
            return v.re("p (c n) -> p c n", c=8)
        ymT = scr(22, 2, BF16).re("p (h t) -> p h t", h=4)
        wif = wload('w_in', l, 8, OFF_MI, 8)
        pfT = bank()
        for kc in range(8):
            P.mm(pfT[0:8, :T], wif[:, kc, :], xn[:, kc, :T], start=(kc == 0), stop=(kc == 7))
        ifT = scr(21, 1)[0:8, :]
        P.copy(ifT[:, :T], pfT[0:8, :T], eng="act")
        pif = bank()
        for c in range(NCk):
            P.tr(pif[0:64, c * 8:(c + 1) * 8], ifT[:, c * 64:(c + 1) * 64], ident[0:8, 0:8])
        pv = c8(pif[0:64, 0:64])
        gates_if(pv[:, :, 0:4], pv[:, :, 4:8], m8(LI), m8(LF), m8(X1), m8(X2), mbi.bcast(1, 8), mbf.bcast(1, 8))
        pg = bank()
        P.mm(pg[0:64, 0:32], triU, mt[:, LF, :])
        P.mm(pg[0:64, 32:64], ones64[:, 0:64], mt[:, LF, :])
        P.copy(mt[:, BC, :], pg[0:64, 0:32], eng="act")
        P.copy(mt[:, BL, :], pg[0:64, 32:64], eng="act")
        P.tt(mt[:, AS, :], mt[:, LI, :], mt[:, BC, :], ALU.subtract)

        for h in range(4):
            diag = c8(scr(12, 1)[0:64])
            P.tt(diag, id64.bcast(1, 8), m8(AS)[:, :, h].bcast(2, 64), ALU.mult)
            pAR = bank()
            P.mm(pAR[0:64, :T], ones64[:, 0:64], scr(12, 1)[0:64, :T])
            LW = c8(scr(6, 1)[0:64])
            P.tt(LW, c8(pAR[0:64, :]), m8(BC)[:, :, h].bcast(2, 64), ALU.add)
            P.tt(LW, LW, negU.bcast(1, 8), ALU.add)
            P.reduce(m8(MX)[:, :, h], LW, ALU.max)
            psel = bank()
            P.mm(psel[0:64, 0:32], sel63[:, 0:64], mt[:, MX, :])
            mx63 = m8(X2)
            P.copy(mt[:, X2, :], psel[0:64, 0:32], eng="act")
            P.copy(mp[:, 0:1], mcar[:, h:h + 1])
            for c in range(NCk):
                P.ts(mp[:, c + 1:c + 2], m8(BL)[:, c, h:h + 1], mp[:, c:c + 1], ALU.add, s2=mx63[:, c, h:h + 1], op1=ALU.max)
            P.copy(mcar[:, h:h + 1], mp[:, NCk:NCk + 1])
            P.tt(m8(INTER)[:, :, h], m8(BC)[:, :, h], mp[:, 0:NCk], ALU.add)
            P.tt(m8(MT)[:, :, h], m8(INTER)[:, :, h], m8(MX)[:, :, h], ALU.max)
            P.tt(m8(SC)[:, :, h], m8(INTER)[:, :, h], m8(MT)[:, :, h], ALU.subtract)
            P.act(m8(SC)[:, :, h], m8(SC)[:, :, h], AF.Exp)
            P.act(m8(NEM)[:, :, h], m8(MT)[:, :, h], AF.Exp, scale=-1.0)
            P.tt(LW, LW, m8(MT)[:, :, h].bcast(2, 64), ALU.subtract)
            P.act(LW, LW, AF.Exp)
            pscl = bank()
            P.mm(pscl[0:64, 0:32], sel63[:, 0:64], mt[:, SC, :])
            scl = mp[:, 9:9 + NCk - 1 + 1] if False else None
            P.copy(mt[:, X1, :], pscl[0:64, 0:32], eng="act")
            sclv = m8(X1)
            qT, kT, vT = scr(0, 1)[0:64], scr(1, 1)[0:64], scr(3, 1)
            qkb = scr(2, 1, BF16)
            qTb, kTb = qkb[0:64, 0:512], qkb[0:64, 512:1024]
            wq = wload('w_in', l, 8, OFF_MQ + h * 64, 64)
            pq = bank()
            for kc in range(8):
                P.mm(pq[0:64, :T], wq[:, kc, :], xn[:, kc, :T], start=(kc == 0), stop=(kc == 7))
            P.copy(qT[:, :T], pq[0:64, :T], eng="act")
            P.copy(qTb[:, :T], pq[0:64, :T], eng="act")
            wk = wload('w_in', l, 8, OFF_MK + h * 64, 64)
            pk = bank()
            for kc in range(8):
                P.mm(pk[0:64, :T], wk[:, kc, :], xn[:, kc, :T], start=(kc == 0), stop=(kc == 7))
            P.act(kT[:, :T], pk[0:64, :T], AF.Copy, scale=0.125)
            P.copy(kTb[:, :T], kT[:, :T], eng="act")
            wv = wload('w_in', l, 8, OFF_MV + h * 128, 128)
            pvv = bank()
            for kc in range(8):
                P.mm(pvv[:, :T], wv[:, kc, :], xn[:, kc, :T], start=(kc == 0), stop=(kc == 7))
            P.copy(vT[:, :T], pvv[:, :T], eng="act")
            v_tm = c8(scr(4, 2)[0:64])
            pvt = bank(2)
            for c in range(NCk):
                P.tr(pvt[0:64, c * 128:(c + 1) * 128], vT[:, c * 64:(c + 1) * 64], ident)
            P.copy(v_tm, c8(pvt[0:64, :]), eng="act")
            wtsT = c8(scr(7, 1)[0:64])
            pwt = bank()
            for c in range(NCk):
                P.tr(pwt[0:64, c * 64:(c + 1) * 64], LW[:, c, :], id64)
            P.copy(wtsT, c8(pwt[0:64, :]), eng="act")
            pQK = bank()
            for c in range(NCk):
                sl = slice(c * 64, (c + 1) * 64)
                P.mm(pQK[0:64, sl], kTb[:, sl], qTb[:, sl])
            sqkT = c8(scr(8, 1)[0:64])
            P.tt(sqkT, wtsT, c8(pQK[0:64, :]), ALU.mult)
            kw = c8(scr(9, 1)[0:64])
            pkt = bank()
            for c in range(NCk):
                P.tr(pkt[0:64, c * 64:(c + 1) * 64], kT[:, c * 64:(c + 1) * 64], id64)
            P.tt(kw, c8(pkt[0:64, :]), wtsT[:, :, 63].bcast(2, 64), ALU.mult)
            pN = bank(2)
            pD = bank()
            for c in range(NCk):
                P.mm(pN[0:64, c * 128:(c + 1) * 128], sqkT[:, c, :], v_tm[:, c, :])
                P.mm(pD[0:64, c:c + 1], sqkT[:, c, :], ones64[:, 0:1])
            num = c8(scr(10, 2)[0:64])
            P.copy(num, c8(pN[0:64, :]), eng="act")
            den = mt[:, X2, 0:8]
            P.copy(den[:, 0:NCk], pD[0:64, 0:NCk])
            pI = bank(2)
            pIn = bank()
            for c in range(NCk):
                P.mm(pI[0:64, c * 128:(c + 1) * 128], kw[:, c, :], v_tm[:, c, :])
                P.mm(pIn[0:64, c:c + 1], kw[:, c, :], ones64[:, 0:1])
            Inc = c8(scr(17, 2)[0:64])
            P.copy(Inc, c8(pI[0:64, :]), eng="act")
            IncN = mp[:, 8:16]
            P.copy(IncN[:, 0:NCk], pIn[0:64, 0:NCk], eng="act")
            Cs = c8(scr(19, 2)[0:64])
            ns = mns[:, 0:8]
            P.copy(Cs[:, 0, :], mC[:, h, :], eng="pool")
            P.copy(ns[:, 0:1], mn[:, h:h + 1], eng="pool")
            for c in range(NCk):
                sl_ = sclv[:, c, h:h + 1]
                dstC = Cs[:, c + 1, :] if c < NCk - 1 else mC[:, h, :]
                dstn = ns[:, c + 1:c + 2] if c < NCk - 1 else mn[:, h:h + 1]
                P.stt(dstC, Cs[:, c, :], sl_, Inc[:, c, :], ALU.mult, ALU.add)
                P.stt(dstn, ns[:, c:c + 1], sl_, IncN[:, c:c + 1], ALU.mult, ALU.add)
            pQC = bank(2)
            pQn = bank()
            for c in range(NCk):
                sl = slice(c * 64, (c + 1) * 64)
                P.mm(pQC[0:64, c * 128:(c + 1) * 128], qT[:, sl], Cs[:, c, :])
                P.mm(pQn[0:64, c:c + 1], qT[:, sl], ns[:, c:c + 1])
            scb = m8(SC)[:, :, h]
            tq = c8(scr(17, 2)[0:64])
            P.tt(tq, c8(pQC[0:64, :]), scb.bcast(2, 128), ALU.mult)
            P.tt(num, num, tq, ALU.add)
            tn = mp[:, 8:16]
            P.tt(tn[:, 0:NCk], pQn[0:64, 0:NCk], scb, ALU.mult)
            P.tt(den[:, 0:NCk], den[:, 0:NCk], tn[:, 0:NCk], ALU.add)
            dd = mp[:, 8:16]
            P.ts(dd[:, 0:NCk], den[:, 0:NCk], -1.0, ALU.mult)
            P.tt(dd[:, 0:NCk], dd[:, 0:NCk], den[:, 0:NCk], ALU.max)
            P.tt(dd[:, 0:NCk], dd[:, 0:NCk], m8(NEM)[:, :, h], ALU.max)
            P.recip(dd[:, 0:NCk], dd[:, 0:NCk])
            P.tt(num, num, dd[:, 0:NCk].bcast(2, 128), ALU.mult)
            wmo = wload('w_in', l, 8, OFF_MO + h * 128, 128)
            pz = bank()
            for kc in range(8):
                P.mm(pz[:, :T], wmo[:, kc, :], xn[:, kc, :T], start=(kc == 0), stop=(kc == 7))
            zsT = scr(13, 1)
            P.act(zsT[:, :T], pz[:, :T], AF.Sigmoid)
            sq = c8(scr(15, 2)[0:64])
            P.tt(sq, num, num, ALU.mult)
            ss = mp[:, 8:16]
            P.reduce(ss[:, 0:NCk], sq, ALU.add)
            P.act(ss[:, 0:NCk], ss[:, 0:NCk], AF.Ln, bias=epsc[0:64], scale=1.0 / 128)
            P.act(ss[:, 0:NCk], ss[:, 0:NCk], AF.Exp, scale=-0.5)
            P.tt(num, num, ss[:, 0:NCk].bcast(2, 128), ALU.mult)
            P.tt(num, num, mng[:, h * 128:(h + 1) * 128].bcast(1, 8), ALU.mult)
            pt = bank()
            for c in range(NCk):
                P.tr(pt[:, c * 64:(c + 1) * 64], num[:, c, :], id64)
            P.tt(ymT[:, h, :T], pt[:, :T], zsT[:, :T], ALU.mult)
        if ti == n_ptiles - 1:
            P.dma("pool", o_mC_p[l].re("h k v -> k h v"), mC, is_out=True)
            P.dma("pool", o_mn_p[l].re("h k -> k h"), mn, is_out=True, allow_slow_non_contiguous=True)
            P.dma("pool", o_mm_p[l].re("(o h) -> o h", o=1), mcar[0:1, :], is_out=True)
        branch_out(l, T, ymT, 'w_br_ml', 2)

    def ml_sample(l):
        T = NS
        ymT = scr(22, 2, BF16).re("p (h t) -> p h t", h=4)
        id16 = ident[0:16, 0:16]
        wif = wload('w_in', l, 8, OFF_MI, 8)
        pif = bank()
        for kc in range(8):
            P.mm(pif[0:16, 0:8], xn[:, kc, :T], wif[:, kc, :], start=(kc == 0), stop=(kc == 7))
        sc3 = mt[0:16, 0, 0:12]
        gates_if(pif[0:16, 0:4], pif[0:16, 4:8], sc3[:, 0:4], sc3[:, 4:8], mt[0:16, 1, 0:4], mt[0:16, 2, 0:4],
                 mbi[0:16, :], mbf[0:16, :])
        P.dma("sp", sc3[:, 8:12], st_mm[l])
        dg = scr(9, 1)[0:16, 0:192].re("p (n r) -> p n r", n=12)
        P.tt(dg, id16.bcast(1, 12), sc3.bcast(2, 16), ALU.mult)
        prb = bank()
        P.mm(prb[:, 0:192], ones[0:16, :], scr(9, 1)[0:16, 0:192])
        rb = scr(10, 1)[:, 0:192].re("p (n r) -> p n r", n=12)
        P.copy(rb, prb[:, 0:192].re("p (n r) -> p n r", n=12), eng="act")
        liR = rb[:, 0:4, :].re("p h r -> p r h")
        lfR = rb[:, 4:8, :].re("p h r -> p r h")
        m0R = rb[:, 8:12, :].re("p h r -> p r h")
        E = scr(11, 1).re("p (q n) -> p q n", q=8)

        def e(i):
            return E[:, i, :].re("p (r h) -> p r h", r=16)
        inter, m_t, wts, scv, nem, t1, t2, t3 = (e(i) for i in range(8))
        P.tt(inter, lfR, m0R, ALU.add)
        P.tt(m_t, inter, liR, ALU.max)
        P.tt(wts, liR, m_t, ALU.subtract)
        P.act(wts, wts, AF.Exp)
        P.tt(scv, inter, m_t, ALU.subtract)
        P.act(scv, scv, AF.Exp)
        P.act(nem, m_t, AF.Exp, scale=-1.0)
        P.dma("pool", o_mm_s[l].re("(o r) h -> o r h", o=1), m_t[0:1], is_out=True)
        F = scr(12, 1).re("p (q n) -> p q n", q=8)

        def f(i, np_=128):
            return F[0:np_, i, :].re("p (r h) -> p r h", r=16)
        qT, kT, vT, moT, n0T, kwT, nnew, t4 = f(0, 64), f(1, 64), f(2), f(3), f(4, 64), f(5, 64), f(6, 64), f(7)
        for (off, dk, dst, func, scl) in ((OFF_MQ, 64, qT, AF.Copy, 1.0), (OFF_MK, 64, kT, AF.Copy, 0.125),
                                          (OFF_MV, 128, vT, AF.Copy, 1.0), (OFF_MO, 128, moT, AF.Sigmoid, 1.0)):
            pp = bank()
            for h in range(4):
                wq = wload('w_in', l, 8, off + h * dk, dk)
                for kc in range(8):
                    P.mm(pp[0:dk, h * 16:(h + 1) * 16], wq[:, kc, :], xn[:, kc, :T], start=(kc == 0), stop=(kc == 7))
            P.act(dst, pp[0:dk, 0:64].re("p (h r) -> p r h", h=4), func, scale=scl)
        ldn = scr(13, 1)[0:64, 0:64]
        P.dma("sp", ldn, st_mn[l])
        pnt = bank()
        P.tr(pnt[0:64, 0:64], ldn, id64)
        P.copy(n0T, pnt[0:64, 0:64].re("p (r h) -> p r h", r=16), eng="act")
        pr2 = scr(13, 1)[0:64, 64:192].re("p (q r h) -> p q r h", q=2, r=16)
        P.tt(pr2[:, 0], qT, kT, ALU.mult)
        P.tt(pr2[:, 1], qT, n0T, ALU.mult)
        pqk = bank()
        P.mm(pqk[:, 0:128], ones64, scr(13, 1)[0:64, 64:192])
        qk = pqk[:, 0:64].re("p (r h) -> p r h", r=16)
        qn = pqk[:, 64:128].re("p (r h) -> p r h", r=16)
        pqc = bank()
        pqcb = (st.bank - 1) % 8
        st.reserved.add(pqcb)
        for r in range(NS):
            Cr = scr(14 + (r % 4), 1)[0:64].re("p (h v) -> p h v", h=4)
            P.dma("sp", Cr, st_mC[l, r].re("h k v -> k h v"))
            for h in range(4):
                P.mm(pqc[:, r * 4 + h:r * 4 + h + 1], Cr[:, h, :], qT[:, r, h:h + 1])
        sqk = t1
        P.tt(sqk, wts, qk, ALU.mult)
        P.tt(t2, scv, pqc[:, 0:64].re("p (r h) -> p r h", r=16), ALU.mult)
        st.reserved.discard(pqcb)
        P.tt(t3, sqk, vT, ALU.mult)
        P.tt(t3, t3, t2, ALU.add)
        P.tt(t2, scv, qn, ALU.mult)
        P.tt(t2, t2, sqk, ALU.add)
        P.ts(t4, t2, -1.0, ALU.mult)
        P.tt(t4, t4, t2, ALU.max)
        P.tt(t4, t4, nem, ALU.max)
        P.recip(t4, t4)
        P.tt(t3, t3, t4, ALU.mult)
        P.tt(t4, t3, t3, ALU.mult)
        pss = bank()
        P.mm(pss[:, 0:64], ones, F[:, 7, :])
        P.act(t4, pss[:, 0:64].re("p (r h) -> p r h", r=16), AF.Ln, bias=epsc, scale=1.0 / 128)
        P.act(t4, t4, AF.Exp, scale=-0.5)
        P.tt(t3, t3, t4, ALU.mult)
        P.tt(t3, t3, mngc.bcast(1, 16), ALU.mult)
        P.tt(ymT[:, :, :T].re("p h r -> p r h"), t3, moT, ALU.mult)
        P.tt(kwT, kT, wts[0:64], ALU.mult)
        P.tt(nnew, n0T, scv[0:64], ALU.mult)
        P.tt(nnew, nnew, kwT, ALU.add)
        pno = bank()
        P.tr(pno[0:64, 0:64], F[0:64, 6, :], id64)
        P.copy(ldn, pno[0:64, 0:64], eng="act")
        P.dma("pool", o_mn_s[l], ldn, is_out=True)
        for r in range(NS):
            Cr = scr(14 + (r % 4), 1)[0:64].re("p (h v) -> p h v", h=4)
            P.dma("sp", Cr, st_mC[l, r].re("h k v -> k h v"))
            Co = scr(0 + (r % 2), 1)[0:64].re("p (h v) -> p h v", h=4)
            for h in range(4):
                dgv = scr(2 + (h % 2), 1)[:, 0:128]
                P.ts(dgv, ident, vT[:, r, h:h + 1], ALU.mult)
                pvr = bank()
                P.mm(pvr[0:64, 0:128], ones[:, 0:64], dgv)
                P.act(Co[:, h, :], Cr[:, h, :], AF.Copy, scale=scv[0:64, r, h:h + 1])
                P.stt(Co[:, h, :], pvr[0:64, 0:128], kwT[:, r, h:h + 1], Co[:, h, :], ALU.mult, ALU.add)
            P.dma("pool", o_mC_s[l, r].re("h k v -> k h v"), Co, is_out=True)
        branch_out(l, T, ymT, 'w_br_ml', 2)

    hd = scr(0, 8, BF16).re("p (k t) -> p k t", k=16)
    for l in range(DEPTH):
        if BR_S5:
            s5_setup(l)
        if BR_GDN:
            gdn_setup(l)
        if BR_ML:
            ml_setup(l)
        for ti, (t0, T) in enumerate(tiles):
            xt = xT[:, :, t0:t0 + T]
            rmsnorm(xt, T, g1[:, l, :])
            P.memset(mg[:, :, :T], 0.0)
            if BR_S5:
                s5_branch(l, ti, t0, T)
            if BR_GDN:
                if ti < n_ptiles:
                    gdn_branch(l, ti, t0, T)
                else:
                    gdn_sample(l)
            if BR_ML:
                if ti < n_ptiles:
                    ml_branch(l, ti, t0, T)
                else:
                    ml_sample(l)

            def cb_res(ft, pb):
                P.tt(xt[:, ft, :], xt[:, ft, :], pb[:, :T], ALU.add)
            dense_fm('w_out', l, 8, 8, mg, T, cb_res)
            rmsnorm(xt, T, g2[:, l, :])
            for half in range(2):
                def cb_up(ft, pb):
                    tb_ = (tmpA, tmpB, sqt)[ft % 3]
                    P.act(tb_[:, :T], pb[:, :T], AF.Relu)
                    if ft % 2:
                        P.act(hd[:, ft, :T], tb_[:, :T], AF.Square)
                    else:
                        P.tt(hd[:, ft, :T], tb_[:, :T], tb_[:, :T], ALU.mult)
                dense_fm('w_up', l, 8, 16, xn, T, cb_up, c_base=half * 2048)
                dense_fm('w_down', l, 16, 8, hd, T, cb_res, k0=half * 16)
            for b0 in range(0, T, 128):
                nb = min(128, T - b0)
                P.dma("sp", ldp[0:nb, :], pin[l, t0 + b0:t0 + b0 + nb, :])
                pb = bank()
                for kc in range(2):
                    P.tr(pb[:, kc * 128:kc * 128 + nb], ldp[0:nb, kc * 128:(kc + 1) * 128], ident[0:nb, 0:nb])
                P.copy(pT[:, :, b0:b0 + nb], pb[:, 0:256].re("p (k n) -> p k n", k=2)[:, :, 0:nb], eng="act")
            for kc in range(8):
                P.copy(xn[:, kc, :T], xt[:, kc, :], eng="pool")
            for f0 in range(0, 8, 2):
                wpg = wload('w_ple_gate', l, 8, f0 * 128, 256)
                wpl = wload('w_ple', l, 2, f0 * 128, 256)
                for j in range(2):
                    ft = f0 + j
                    pg = bank()
                    for kc in range(8):
                        P.mm(pg[:, :T], wpg[:, kc, j * 128:(j + 1) * 128], xn[:, kc, :T], start=(kc == 0), stop=(kc == 7))
                    pp = bank()
                    for kc in range(2):
                        P.mm(pp[:, :T], wpl[:, kc, j * 128:(j + 1) * 128], pT[:, kc, :T], start=(kc == 0), stop=(kc == 1))
                    tA_, tB_ = ((tmpA, tmpB), (sqt, rstd))[ft % 2]
                    P.act(tA_[:, :T], pg[:, :T], AF.Sigmoid)
                    P.tt(tB_[:, :T], tA_[:, :T], pp[:, :T], ALU.mult)
                    P.tt(xt[:, ft, :], xt[:, ft, :], tB_[:, :T], ALU.add)

    gF = scr(0, 2)
    xtok = scr(2, 2)
    ytok = scr(4, 2)
    ssq = scr(6, 1)[:, 0:1]
    P.dma("sp", gF, W['final_norm_g'].re("(o n) -> o n", o=1).bc([128, 1024]))
    fbufs = [(scr(2, 2), scr(4, 2), scr(6, 1)[:, 0:1]), (scr(7, 2), scr(9, 2), scr(11, 1)[:, 0:1]), (scr(12, 2), scr(14, 2), scr(16, 1)[:, 0:1])]
    for b0 in range(0, NTOK, 128):
        nb = min(128, NTOK - b0)
        xtok, ytok, ssq = fbufs[(b0 // 128) % 3]
        pb = bank(2)
        for kc in range(8):
            P.tr(pb[0:nb, kc * 128:(kc + 1) * 128], xT[:, kc, b0:b0 + nb], ident)
        P.copy(xtok[0:nb, :], pb[0:nb, :], eng="act")
        P.act(ytok[0:nb, :], xtok[0:nb, :], AF.Square, accum=ssq[0:nb, :])
        P.act(ssq[0:nb, :], ssq[0:nb, :], AF.Ln, bias=epsc[0:nb, :], scale=1.0 / D)
        P.act(ssq[0:nb, :], ssq[0:nb, :], AF.Exp, scale=-0.5)
        P.stt(ytok[0:nb, :], xtok[0:nb, :], ssq[0:nb, :], gF[0:nb, :], ALU.mult, ALU.mult)
        P.dma("pool", yout[b0:b0 + nb, :], ytok[0:nb, :], is_out=True)

    P.emit()
    return nc, P


def make_in_maps(inputs, NP, ncores):
    cst = make_consts()
    maps = []
    for c in range(ncores):
        m = {}
        xs = inputs['x_sample'][c * NS:(c + 1) * NS, 0, :]
        m['xin'] = np.ascontiguousarray(np.concatenate([inputs['x_prompt'][c, :NP, :], xs], axis=0), dtype=np.float32)
        ps = inputs['p_sample'][:, c * NS:(c + 1) * NS, 0, :]
        m['pin'] = np.ascontiguousarray(np.concatenate([inputs['p_prompt'][:, c, :NP, :], ps], axis=1), dtype=np.float32)
        m['cst'] = cst
        m['st_s5re'] = np.ascontiguousarray(inputs['state_s5_re'][:, c * NS:(c + 1) * NS].reshape(2, NS, 2048), dtype=np.float32)
        m['st_conv'] = np.ascontiguousarray(inputs['state_gdn_conv'][:, c * NS:(c + 1) * NS].reshape(2, NS * 3, 1536), dtype=np.float32)
        m['st_gdn'] = np.ascontiguousarray(inputs['state_gdn'][:, c * NS:(c + 1) * NS], dtype=np.float32)
        m['st_mC'] = np.ascontiguousarray(inputs['state_mlstm_C'][:, c * NS:(c + 1) * NS], dtype=np.float32)
        m['st_mn'] = np.ascontiguousarray(inputs['state_mlstm_n'][:, c * NS:(c + 1) * NS].reshape(2, NS * 4, 64), dtype=np.float32)
        m['st_mm'] = np.ascontiguousarray(inputs['state_mlstm_m'][:, c * NS:(c + 1) * NS], dtype=np.float32)
        m['st_s5im'] = np.ascontiguousarray(inputs['state_s5_im'][:, c * NS:(c + 1) * NS].reshape(2, NS, 2048), dtype=np.float32)
        for n in WEIGHT_NAMES:
            m[n] = np.ascontiguousarray(inputs[n], dtype=np.float32)
        maps.append(m)
    return maps


def run(inputs, NP=2048, ncores=8, stage="all", debug=False, trace=False):
    nc, P = build_program(NP, stage, debug)
    maps = make_in_maps(inputs, NP, ncores)
    res = run_bass_kernel_spmd(nc, maps, core_ids=list(range(ncores)), trace=trace)
    if trace:
        print("EXEC_TIME_NS", res.exec_time_ns)
    R = res.results
    y = np.stack([r['y'] for r in R], axis=0)
    y_prompt = y[:, :NP, :]
    y_sample = y[:, NP:, :].reshape(ncores * NS, 1, D)
    def gat_p(name, shp):
        return np.stack([r[name] for r in R], axis=1).reshape((2, ncores) + shp)

    def gat_s(name, shp):
        return np.concatenate([r[name] for r in R], axis=1).reshape((2, ncores * NS) + shp)
    outs = [y_prompt, y_sample,
            gat_p('o_s5re_p', (32, 64)), gat_s('o_s5re_s', (32, 64)), gat_p('o_s5im_p', (32, 64)), gat_s('o_s5im_s', (32, 64)),
            gat_p('o_conv_p', (3, 1536)), gat_s('o_conv_s', (3, 1536)), gat_p('o_gdn_p', (4, 128, 128)), gat_s('o_gdn_s', (4, 128, 128)),
            gat_p('o_mC_p', (4, 64, 128)), gat_s('o_mC_s', (4, 64, 128)), gat_p('o_mn_p', (4, 64)), gat_s('o_mn_s', (4, 64)),
            gat_p('o_mm_p', (4,)), gat_s('o_mm_s', (4,))]
    return tuple(outs), R


def kernel(**inputs):
    outs, _ = run(inputs)
    return outs
```

```python
import contextlib
import numpy as np
import concourse.bass as bass
import concourse.mybir as mybir
from concourse.bass_utils import run_bass_kernel_spmd

F32 = mybir.dt.float32
BF16 = mybir.dt.bfloat16
I32 = mybir.dt.int32
ALU = mybir.AluOpType
AF = mybir.ActivationFunctionType
AX = mybir.AxisListType

SAME_ENG_SYNC = True
DMA_SLOTS = 8


class V:
    __slots__ = ("ap", "keys")

    def __init__(self, ap, keys):
        self.ap = ap
        self.keys = tuple(keys)

    def __getitem__(self, idx):
        return V(self.ap[idx], self.keys)

    def k(self, *keys):
        return V(self.ap, keys)

    def re(self, pat, **kw):
        return V(self.ap.rearrange(pat, **kw), self.keys)

    def bc(self, shape):
        return V(self.ap.to_broadcast(shape), self.keys)

    def bcast(self, axis, n):
        shp = list(self.ap.shape)
        shp.insert(axis, n)
        return V(self.ap.unsqueeze(axis).to_broadcast(shp), self.keys)

    def cast(self, dt):
        return V(self.ap.bitcast(dt), self.keys)


class Prog:
    def __init__(self, nc):
        self.nc = nc
        self.es = contextlib.ExitStack()
        self.ops = []
        self.last_w = {}
        self.readers = {}
        self.out_dmas = []
        self.nbuf = 0

    def sb(self, name, shape, dtype=F32):
        t = self.es.enter_context(self.nc.sbuf_tensor(name, list(shape), dtype))
        self.nbuf += 1
        return V(t[:], (("sb", name),))

    def ps(self, name, shape, dtype=F32):
        t = self.es.enter_context(self.nc.psum_tensor(name, list(shape), dtype))
        return V(t[:], (("ps", name),))

    def dram(self, name, shape, dtype=F32, kind="ExternalInput"):
        t = self.nc.dram_tensor(name, list(shape), dtype, kind=kind)
        return V(t.ap(), (("dr", name),))

    def op(self, eng, fn, reads, writes, dma=False, out=False):
        i = len(self.ops)
        deps = set()
        for v in reads:
            for k in v.keys:
                if k[0] == "dr" and k not in self.last_w:
                    continue
                j = self.last_w.get(k)
                if j is not None:
                    deps.add(j)
        raw = set(deps)
        for v in writes:
            for k in v.keys:
                j = self.last_w.get(k)
                if j is not None:
                    deps.add(j)
                deps.update(self.readers.get(k, ()))
        for v in reads:
            for k in v.keys:
                self.readers.setdefault(k, []).append(i)
        for v in writes:
            for k in v.keys:
                self.last_w[k] = i
                self.readers[k] = []
        deps.discard(i)
        self.ops.append((eng, fn, deps, dma, raw))
        if out:
            self.out_dmas.append(i)
        return i

    def dma(self, eng, out, in_, is_out=False, **kw):
        return self.op(eng, lambda e: e.dma_start(out=out.ap, in_=in_.ap, **kw), [in_], [out], dma=True, out=is_out)

    def mm(self, out, lhsT, rhs, start=True, stop=True, **kw):
        return self.op("pe", lambda e: e.matmul(out.ap, lhsT.ap, rhs.ap, start=start, stop=stop, **kw),
                       [lhsT, rhs], [out])

    def tr(self, out, in_, ident, **kw):
        return self.op("pe", lambda e: e.transpose(out.ap, in_.ap, ident.ap, **kw), [in_, ident], [out])

    def act(self, out, in_, func, bias=None, scale=1.0, accum=None, eng="act"):
        reads = [in_]
        b = bias
        s = scale
        if isinstance(bias, V):
            reads.append(bias)
            b = bias.ap
        if isinstance(scale, V):
            reads.append(scale)
            s = scale.ap
        writes = [out]
        kw = {}
        if accum is not None:
            writes.append(accum)
            kw["accum_out"] = accum.ap
        if b is not None:
            kw["bias"] = b
        return self.op(eng, lambda e: e.activation(out.ap, in_.ap, func, scale=s, **kw), reads, writes)

    def tt(self, out, a, b, op, eng="dve"):
        return self.op(eng, lambda e: e.tensor_tensor(out.ap, a.ap, b.ap, op), [a, b], [out])

    def ts(self, out, a, s1, op0, s2=None, op1=None, accum=None, eng="dve"):
        reads = [a]
        x1, x2 = s1, s2
        if isinstance(s1, V):
            reads.append(s1)
            x1 = s1.ap
        if isinstance(s2, V):
            reads.append(s2)
            x2 = s2.ap
        writes = [out]
        kw = {}
        if op1 is not None:
            kw["op1"] = op1
        if accum is not None:
            writes.append(accum)
            kw["accum_out"] = accum.ap
        return self.op(eng, lambda e: e.tensor_scalar(out.ap, a.ap, x1, x2, op0, **kw), reads, writes)

    def stt(self, out, a, s, b, op0, op1, accum=None, eng="dve"):
        reads = [a, b]
        x = s
        if isinstance(s, V):
            reads.append(s)
            x = s.ap
        writes = [out]
        kw = {}
        if accum is not None:
            writes.append(accum)
            kw["accum_out"] = accum.ap
        return self.op(eng, lambda e: e.scalar_tensor_tensor(out.ap, a.ap, x, b.ap, op0, op1, **kw), reads, writes)

    def copy(self, out, in_, eng="dve"):
        if eng == "act":
            return self.op(eng, lambda e: e.copy(out.ap, in_.ap), [in_], [out])
        return self.op(eng, lambda e: e.tensor_copy(out.ap, in_.ap), [in_], [out])

    def memset(self, out, val, eng="pool"):
        return self.op(eng, lambda e: e.memset(out.ap, val), [], [out])

    def scan(self, out, d0, d1, init, op0=ALU.mult, op1=ALU.add):
        reads = [d0, d1]
        x = init
        if isinstance(init, V):
            reads.append(init)
            x = init.ap
        return self.op("dve", lambda e: e.tensor_tensor_scan(out.ap, d0.ap, d1.ap, x, op0, op1), reads, [out])

    def reduce(self, out, in_, op, axis=AX.X, eng="dve"):
        return self.op(eng, lambda e: e.tensor_reduce(out.ap, in_.ap, axis, op), [in_], [out])

    def recip(self, out, in_):
        return self.op("dve", lambda e: e.reciprocal(out.ap, in_.ap), [in_], [out])

    def emit(self):
        nc = self.nc
        ops = self.ops
        n = len(ops)
        has_dep = [False] * n
        def skip(i, j):
            eng, _, _, dma, raw = ops[i]
            je, _, _, jd, _ = ops[j]
            if jd or dma or je != eng:
                return False
            if eng == "pe":
                return True
            if not SAME_ENG_SYNC:
                return True
            if SAME_ENG_SYNC == "all":
                return False
            return j not in raw

        for i, (eng, fn, deps, dma, raw) in enumerate(ops):
            for j in deps:
                if skip(i, j):
                    continue
                has_dep[j] = True
        engs = ["pe", "act", "dve", "pool", "sp"]
        sems = {e: self.es.enter_context(nc.semaphore("s_" + e)) for e in engs}
        dsems = {e: [self.es.enter_context(nc.semaphore("d_%s_%d" % (e, s))) for s in range(DMA_SLOTS)]
                 for e in ("sp", "act", "pool")}
        cnt = {e: 0 for e in engs}
        dcnt = {e: 0 for e in engs}
        sig = [None] * n
        prog = {e: [] for e in engs}
        waited = {e: {} for e in engs}

        def add_wait(e, sem, val):
            w = waited[e]
            key = id(sem)
            if w.get(key, 0) >= val:
                return
            w[key] = val
            prog[e].append(("w", sem, val))

        for i, (eng, fn, deps, dma, raw) in enumerate(ops):
            best = {}
            for j in deps:
                if skip(i, j):
                    continue
                s, v = sig[j]
                key = id(s)
                if key not in best or best[key][1] < v:
                    best[key] = (s, v)
            if dma:
                q = dcnt[eng]
                dcnt[eng] += 1
                slot = q % DMA_SLOTS
                rnd = q // DMA_SLOTS
                s = dsems[eng][slot]
                if rnd > 0:
                    key = id(s)
                    v = 16 * rnd
                    if key not in best or best[key][1] < v:
                        best[key] = (s, v)
                sig[i] = (s, 16 * (rnd + 1))
                for s_, v_ in best.values():
                    add_wait(eng, s_, v_)
                prog[eng].append(("d", fn, s))
            else:
                for s_, v_ in best.values():
                    add_wait(eng, s_, v_)
                if has_dep[i]:
                    cnt[eng] += 1
                    sig[i] = (sems[eng], cnt[eng])
                    prog[eng].append(("i", fn, sems[eng]))
                else:
                    prog[eng].append(("i", fn, None))
        last = {}
        for i in self.out_dmas:
            s, v = sig[i]
            if id(s) not in last or last[id(s)][1] < v:
                last[id(s)] = (s, v)
        for s, v in last.values():
            add_wait("sp", s, v)
        self.stats = {e: len(prog[e]) for e in engs}

        def run(e, items):
            for it in items:
                if it[0] == "w":
                    e.wait_ge(it[1], it[2])
                elif it[0] == "d":
                    it[1](e).then_inc(it[2], 16)
                else:
                    ins = it[1](e)
                    if it[2] is not None:
                        ins.then_inc(it[2], 1)

        with nc.Block() as block:
            @block.tensor
            def _(e):
                run(e, prog["pe"])

            @block.scalar
            def _(e):
                run(e, prog["act"])

            @block.vector
            def _(e):
                run(e, prog["dve"])

            @block.gpsimd
            def _(e):
                run(e, prog["pool"])

            @block.sync
            def _(e):
                run(e, prog["sp"])
        self.es.close()


D = 1024
DEPTH = 2
NS = 16
PLE = 256
DFF = 4096
EPS = 1e-6
D_IN = 7184
OFF_U, OFF_QKV, OFF_Z, OFF_B, OFF_A = 0, 512, 2048, 2560, 2564
OFF_MQ, OFF_MK, OFF_MV, OFF_MO, OFF_MI, OFF_MF, OFF_G = 2568, 2824, 3080, 3592, 4104, 4108, 4112

WEIGHT_NAMES = ['norm1_g', 'w_in', 's5_A_re', 's5_A_im', 's5_log_dt', 's5_B_re', 's5_B_im', 's5_C_re', 's5_C_im',
                's5_D', 's5_w_glu', 's5_b_glu', 'gdn_conv_w', 'gdn_A_log', 'gdn_dt_bias', 'gdn_norm_g',
                'ml_b_i', 'ml_b_f', 'ml_norm_g', 'w_br_s5', 'w_br_gdn', 'w_br_ml', 'w_out', 'norm2_g',
                'w_up', 'w_down', 'w_ple', 'w_ple_gate', 'final_norm_g']
WEIGHT_SHAPES = {
    'norm1_g': (2, 1024), 'w_in': (2, 1024, 7184), 's5_A_re': (2, 32, 64), 's5_A_im': (2, 32, 64),
    's5_log_dt': (2, 32), 's5_B_re': (2, 32, 64, 16), 's5_B_im': (2, 32, 64, 16), 's5_C_re': (2, 32, 16, 64),
    's5_C_im': (2, 32, 16, 64), 's5_D': (2, 512), 's5_w_glu': (2, 512, 512), 's5_b_glu': (2, 512),
    'gdn_conv_w': (2, 4, 1536), 'gdn_A_log': (2, 4), 'gdn_dt_bias': (2, 4), 'gdn_norm_g': (2, 128),
    'ml_b_i': (2, 4), 'ml_b_f': (2, 4), 'ml_norm_g': (2, 512), 'w_br_s5': (2, 512, 1024),
    'w_br_gdn': (2, 512, 1024), 'w_br_ml': (2, 512, 1024), 'w_out': (2, 1024, 1024), 'norm2_g': (2, 1024),
    'w_up': (2, 1024, 4096), 'w_down': (2, 4096, 1024), 'w_ple': (2, 256, 1024), 'w_ple_gate': (2, 1024, 1024),
    'final_norm_g': (1024,),
}

C_ID, C_ONE, C_EPS, C_TV = 0, 128, 256, 704
C_TRIU, C_SEL63, C_MSLN, C_NEGU = 322, 386, 514, 578
NCST = 704 + 129
CH = 128
PI = float(np.pi)
TWO_PI = float(2 * np.pi)
CW1 = 6.28125
CW2 = float(2 * np.pi - 6.28125)


def make_consts():
    c = np.zeros((128, NCST), np.float32)
    c[:, C_ID:C_ID + 128] = np.eye(128, dtype=np.float32)
    c[:, C_ONE:C_ONE + 128] = 1.0
    c[:, C_EPS] = EPS
    c[:, C_TV:C_TV + CH + 1] = np.arange(CH + 1, dtype=np.float32)[None, :]
    i = np.arange(64)
    c[0:64, C_TRIU:C_TRIU + 64] = (i[:, None] <= i[None, :]).astype(np.float32)
    c[63, C_SEL63:C_SEL63 + 128] = 1.0
    c[0:64, C_MSLN:C_MSLN + 64] = -(i[:, None] > i[None, :]).astype(np.float32)
    c[0:64, C_NEGU:C_NEGU + 64] = np.where(i[:, None] >= i[None, :], 0.0, -30000.0)
    return c


class Ctx:
    pass


WCH = 2048
NBLK = 24


def build_program(NP, stage="all", debug=False):
    BR_S5 = stage in ("s5", "all")
    BR_GDN = stage in ("gdn", "all")
    BR_ML = stage in ("ml", "all")
    nc = bass.Bass("TRN2", target_bir_lowering=False)
    P = Prog(nc)
    NTOK = NP + NS
    tiles = [(i * 512, 512) for i in range(NP // 512)] + [(NP, NS)]
    n_ptiles = NP // 512

    xin = P.dram("xin", [NTOK, D])
    pin = P.dram("pin", [DEPTH, NTOK, PLE])
    cstd = P.dram("cst", [128, NCST])
    W = {n: P.dram(n, WEIGHT_SHAPES[n]) for n in WEIGHT_NAMES}
    st_s5 = [P.dram("st_s5re", [DEPTH, NS, 2048]), P.dram("st_s5im", [DEPTH, NS, 2048])]
    yout = P.dram("y", [NTOK, D], kind="ExternalOutput")
    o_s5p = [P.dram("o_s5re_p", [DEPTH, 2048], kind="ExternalOutput"), P.dram("o_s5im_p", [DEPTH, 2048], kind="ExternalOutput")]
    o_s5s = [P.dram("o_s5re_s", [DEPTH, NS, 2048], kind="ExternalOutput"), P.dram("o_s5im_s", [DEPTH, NS, 2048], kind="ExternalOutput")]

    cst = P.sb("cst_sb", [128, NCST])
    ident = cst[:, C_ID:C_ID + 128]
    ones = cst[:, C_ONE:C_ONE + 128]
    epsc = cst[:, C_EPS:C_EPS + 1]
    tvec = cst[:, C_TV:C_TV + CH + 1]
    xT = P.sb("xT", [128, 8, NTOK])
    xn = P.sb("xn", [128, 8, 512], BF16)
    mg = P.sb("mg", [128, 8, 512], BF16)
    rstd = P.sb("rstd", [128, 512])
    sqt = P.sb("sqt", [128, 512])
    tmpA = P.sb("tmpA", [128, 512])
    tmpB = P.sb("tmpB", [128, 512])
    g1 = P.sb("g1", [128, DEPTH, 8])
    g2 = P.sb("g2", [128, DEPTH, 8])
    pT = P.sb("pT", [128, 2, 512], BF16)
    ldp = P.sb("ldp", [128, 256])
    NWR = 5
    wring = [P.sb("wr%d" % i, [128, WCH], BF16) for i in range(NWR)]
    big = P.sb("big", [128, NBLK * 512])
    psum = P.ps("psum", [128, 8 * 512])
    st = Ctx()
    st.bank = 0
    st.wr = 0
    st.reserved = set()
    st.tmp = 0
    st.dbg = {}

    def dbg(name, v, shape):
        if not debug or name in st.dbg:
            return
        d = P.dram("dbg_" + name, list(shape), kind="ExternalOutput")
        st.dbg[name] = d
        P.dma("pool", d, v, is_out=True)

    def scr(b0, nb, dtype=F32):
        v = V(big.ap[:, b0 * 512:(b0 + nb) * 512], [("big", b) for b in range(b0, b0 + nb)])
        if dtype != F32:
            v = v.cast(dtype)
        return v

    def bank(n=1):
        while True:
            b = st.bank % 8
            if b % n == 0 and b + n <= 8 and not any((b + i) in st.reserved for i in range(n)):
                break
            st.bank += 1
        st.bank += n
        return V(psum.ap[:, b * 512:(b + n) * 512], [("ps", b + i) for i in range(n)])

    BIGW = ['w_in', 's5_w_glu', 'w_br_s5', 'w_br_gdn', 'w_br_ml', 'w_out', 'w_up', 'w_down', 'w_ple', 'w_ple_gate']
    WIN_GROUPS = [(0, 512), (512, 2568), (2568, 4112), (4112, 5136), (5136, 6160), (6160, 7184)]
    Wb = {}
    for l in range(DEPTH):
        for n in BIGW:
            shp = WEIGHT_SHAPES[n][1:]
            Wb[(n, l)] = P.dram("%s_bf%d" % (n, l), list(shp), BF16, kind="Internal")

    def wkeys(name, l, c0, c1):
        if name != 'w_in':
            return (("drbf", name, l, 0),)
        return tuple(("drbf", name, l, g) for g, (a, b) in enumerate(WIN_GROUPS) if a < c1 and c0 < b)

    st.pending = []
    CAST_ORDER = [('w_in', 0), ('s5_w_glu', 0), ('w_br_s5', 0), ('w_in', 3), ('w_in', 1), ('w_br_gdn', 0), ('w_in', 4),
                  ('w_in', 2), ('w_br_ml', 0), ('w_in', 5), ('w_out', 0), ('w_up', 0), ('w_down', 0), ('w_ple_gate', 0), ('w_ple', 0)]

    def cast_weights(l):
        for (n, g) in CAST_ORDER:
            (a, b) = WIN_GROUPS[g] if n == 'w_in' else (0, WEIGHT_SHAPES[n][2])
            K_ = WEIGHT_SHAPES[n][1]
            for r0 in range(0, K_, 1024):
                r1 = min(K_, r0 + 1024)
                dst = V(Wb[(n, l)].ap[r0:r1, a:b], wkeys(n, l, a, b))
                st.pending.append((dst, W[n][l][r0:r1, a:b]))

    def pump(k=1):
        for _ in range(k):
            if st.pending:
                dst, src = st.pending.pop(0)
                P.dma("pool", dst, src)

    def wload(name, l, KC, c0, cols, k0=0):
        buf = wring[st.wr % NWR]
        st.wr += 1
        dst = buf[:, 0:KC * cols].re("p (k n) -> p k n", k=KC)
        src = V(Wb[(name, l)].ap.rearrange("(k p) n -> p k n", p=128)[:, k0:k0 + KC, c0:c0 + cols], wkeys(name, l, c0, c0 + cols))
        while st.pending and any(kk in st.pending[0][0].keys for kk in src.keys) or \
                any(any(kk in p[0].keys for kk in src.keys) for p in st.pending):
            pump(1)
        P.dma("sp", dst, src)
        if st.wr % 3 == 0:
            pump(1)
        return dst

    P.dma("sp", cst, cstd)
    cast_weights(0)
    cast_weights(1)
    pump(4)
    P.dma("sp", g1, W['norm1_g'].re("l (k p) -> p l k", p=128), allow_slow_non_contiguous=True)
    P.dma("sp", g2, W['norm2_g'].re("l (k p) -> p l k", p=128), allow_slow_non_contiguous=True)

    for b0 in range(0, NTOK, 128):
        nb = min(128, NTOK - b0)
        ldx = scr(2 * ((b0 // 128) % 3), 2)
        P.dma("sp", ldx[0:nb, :], xin[b0:b0 + nb, :])
        pb = bank(2)
        for kc in range(8):
            P.tr(pb[:, kc * 128:kc * 128 + nb], ldx[0:nb, kc * 128:(kc + 1) * 128], ident[0:nb, 0:nb])
        P.copy(xT[:, :, b0:b0 + nb], pb.re("p (k n) -> p k n", k=8)[:, :, 0:nb], eng="act")

    def rmsnorm(xt, T, gcol):
        pb = bank()
        sqs = (sqt, tmpA, tmpB)
        for kc in range(8):
            sq_ = sqs[kc % 3]
            P.act(sq_[:, :T], xt[:, kc, :], AF.Square)
            P.mm(pb[:, :T], ones, sq_[:, :T], start=(kc == 0), stop=(kc == 7))
        P.act(rstd[:, :T], pb[:, :T], AF.Ln, bias=epsc, scale=1.0 / D)
        P.act(rstd[:, :T], rstd[:, :T], AF.Exp, scale=-0.5)
        for kc in range(8):
            P.stt(xn[:, kc, :T], xt[:, kc, :], gcol[:, kc:kc + 1], rstd[:, :T], ALU.mult, ALU.mult)

    def dense_fm(name, l, KC, n_out_tiles, act, T, cb, c_base=0, k0=0):
        per = max(1, WCH // (KC * 128))
        for f0 in range(0, n_out_tiles, per):
            nt = min(per, n_out_tiles - f0)
            wb = wload(name, l, KC, c_base + f0 * 128, nt * 128, k0=k0)
            for j in range(nt):
                pb = bank()
                for kc in range(KC):
                    P.mm(pb[:, :T], wb[:, kc, j * 128:(j + 1) * 128], act[:, kc, :T], start=(kc == 0), stop=(kc == KC - 1))
                cb(f0 + j, pb)

    def branch_out(l, T, ybT, wname, br):
        for f0 in range(0, 8, 2):
            wg = wload('w_in', l, 8, OFF_G + br * 1024 + f0 * 128, 256)
            wbr = wload(wname, l, 4, f0 * 128, 256)
            for j in range(2):
                ft = f0 + j
                pg = bank()
                for kc in range(8):
                    P.mm(pg[:, :T], wg[:, kc, j * 128:(j + 1) * 128], xn[:, kc, :T], start=(kc == 0), stop=(kc == 7))
                pp = bank()
                fo = j * 128
                for kc in range(4):
                    P.mm(pp[:, :T], wbr[:, kc, fo:fo + 128], ybT[:, kc, :T], start=(kc == 0), stop=(kc == 3))
                tA_, tB_ = ((tmpA, tmpB), (sqt, rstd))[ft % 2]
                P.act(tA_[:, :T], pg[:, :T], AF.Sigmoid)
                P.tt(tB_[:, :T], tA_[:, :T], pp[:, :T], ALU.mult)
                P.tt(mg[:, ft, :T], mg[:, ft, :T], tB_[:, :T], ALU.add, eng="pool")

    s5c = P.sb("s5c", [128, 12, 16])
    A_RE, A_IM, DT, MAG, TH, LR, LI, FR, FI, T1, T2, T3 = [s5c[:, i, :].k(("s5c", i)) for i in range(12)]
    cosT = P.sb("cosT", [128, 16, CH + 1])
    sinT = P.sb("sinT", [128, 16, CH + 1])
    nsinC = P.sb("nsinC", [128, 16])
    bbT = P.sb("bbT", [128, 16, 2, 128], BF16)
    CTp = P.sb("CTp", [128, 16, 2, 128], BF16)
    s5D = P.sb("s5D", [128, 4])
    s5bg = P.sb("s5bg", [128, 4])
    carry = P.sb("carry", [128, 16, 2])
    ctmp = P.sb("ctmp", [128, 16, 2])
    hlast = P.sb("hlast", [128, 2, 16])

    def sin_reduced(dst, ang, q, qi, m):
        P.ts(q, ang, 1.0 / TWO_PI, ALU.mult)
        P.copy(qi, q)
        P.copy(q, qi)
        P.stt(m, q, -CW1, ang, ALU.mult, ALU.add)
        P.stt(m, q, -CW2, m, ALU.mult, ALU.add)
        P.ts(q, m, PI, ALU.is_gt)
        P.stt(m, q, -TWO_PI, m, ALU.mult, ALU.add)
        P.ts(q, m, -PI, ALU.is_lt)
        P.stt(m, q, TWO_PI, m, ALU.mult, ALU.add)
        P.act(dst, m, AF.Sin)

    def s5_setup(l):
        n = 16 * (CH + 1)
        P.dma("sp", A_RE, W['s5_A_re'][l].re("(st gi) p -> (gi p) st", gi=2), allow_slow_non_contiguous=True)
        P.dma("sp", A_IM, W['s5_A_im'][l].re("(st gi) p -> (gi p) st", gi=2), allow_slow_non_contiguous=True)
        for gi in range(2):
            P.dma("sp", DT[gi * 64:(gi + 1) * 64, :],
                  W['s5_log_dt'][l].re("(st gi) -> gi st", gi=2)[gi:gi + 1, :].bc([64, 16]), allow_slow_non_contiguous=True)
        P.dma("sp", s5D, W['s5_D'][l].re("(ct p) -> p ct", p=128), allow_slow_non_contiguous=True)
        P.dma("sp", s5bg, W['s5_b_glu'][l].re("(ct p) -> p ct", p=128), allow_slow_non_contiguous=True)
        P.act(DT, DT, AF.Exp)
        P.tt(MAG, A_RE, DT, ALU.mult)
        P.act(MAG, MAG, AF.Exp)
        P.tt(TH, A_IM, DT, ALU.mult)
        ang = scr(0, 5)[:, 0:n]
        q = scr(5, 5)[:, 0:n]
        qi = scr(10, 5).cast(I32)[:, 0:n]
        m = scr(15, 5)[:, 0:n]
        P.tt(ang.re("p (s t) -> p s t", s=16), TH.bcast(2, CH + 1), tvec.bcast(1, 16), ALU.mult)
        sin_reduced(sinT.re("p s t -> p (s t)"), ang, q, qi, m)
        P.ts(ang, ang, PI / 2, ALU.add)
        sin_reduced(cosT.re("p s t -> p (s t)"), ang, q, qi, m)
        P.ts(nsinC, sinT[:, :, CH], -1.0, ALU.mult)
        P.tt(LR, MAG, cosT[:, :, 1], ALU.mult)
        P.tt(LI, MAG, sinT[:, :, 1], ALU.mult)
        P.tt(T1, A_RE, A_RE, ALU.mult)
        P.tt(T2, A_IM, A_IM, ALU.mult)
        P.tt(T1, T1, T2, ALU.add)
        P.recip(T1, T1)
        P.ts(T2, LR, -1.0, ALU.add)
        P.tt(FR, T2, A_RE, ALU.mult)
        P.tt(T3, LI, A_IM, ALU.mult)
        P.tt(FR, FR, T3, ALU.add)
        P.tt(FR, FR, T1, ALU.mult)
        P.tt(FI, LI, A_RE, ALU.mult)
        P.tt(T3, T2, A_IM, ALU.mult)
        P.tt(FI, FI, T3, ALU.subtract)
        P.tt(FI, FI, T1, ALU.mult)
        Bre = scr(0, 1)[:, 0:256].re("p (s c) -> p s c", s=16)
        Bim = scr(1, 1)[:, 0:256].re("p (s c) -> p s c", s=16)
        bbr = scr(2, 1)[:, 0:256].re("p (s c) -> p s c", s=16)
        bbi = scr(3, 1)[:, 0:256].re("p (s c) -> p s c", s=16)
        tt1 = scr(4, 1)[:, 0:256].re("p (s c) -> p s c", s=16)
        P.dma("sp", Bre, W['s5_B_re'][l].re("(st gi) p c -> (gi p) st c", gi=2), allow_slow_non_contiguous=True)
        P.dma("sp", Bim, W['s5_B_im'][l].re("(st gi) p c -> (gi p) st c", gi=2), allow_slow_non_contiguous=True)
        frb, fib = FR.bcast(2, 16), FI.bcast(2, 16)
        P.tt(bbr, Bre, frb, ALU.mult)
        P.tt(tt1, Bim, fib, ALU.mult)
        P.tt(bbr, bbr, tt1, ALU.subtract)
        P.tt(bbi, Bim, frb, ALU.mult)
        P.tt(tt1, Bre, fib, ALU.mult)
        P.tt(bbi, bbi, tt1, ALU.add)
        bbBD = scr(8, 8).re("p (ct j r n) -> p ct j r n", ct=4, j=4, r=2)
        P.memset(bbBD, 0.0)
        for r, bbx in enumerate((bbr, bbi)):
            b4 = bbx.re("p (ct j) c -> p ct j c", ct=4)
            for j in range(4):
                for gi in range(2):
                    P.copy(bbBD[gi * 64:(gi + 1) * 64, :, j, r, 32 * j + 16 * gi:32 * j + 16 * gi + 16],
                           b4[gi * 64:(gi + 1) * 64, :, j, :], eng="pool")
        for ct in range(4):
            for r in range(2):
                pb = bank()
                for j in range(4):
                    P.tr(pb[:, j * 128:(j + 1) * 128], bbBD[:, ct, j, r, :], ident)
                P.copy(bbT[:, 4 * ct:4 * ct + 4, r, :], pb.re("p (j n) -> p j n", j=4), eng="act")
        P.memset(CTp, 0.0)
        CnBD = scr(16, 4).re("p (s n) -> p s n", s=16)
        for r, cname in enumerate(('s5_C_re', 's5_C_im')):
            P.memset(CnBD[0:32], 0.0)
            for gi in range(2):
                P.dma("sp", CnBD[gi * 16:(gi + 1) * 16, :, gi * 64:(gi + 1) * 64],
                      W[cname][l].re("(st gi) c p -> gi c st p", gi=2)[gi], allow_slow_non_contiguous=True)
            pb = bank()
            for s_ in range(16):
                P.tr(pb[:, s_ * 32:(s_ + 1) * 32], CnBD[0:32, s_, :], ident[0:32, 0:32])
            pv = pb.re("p (ct j c) -> p ct j c", ct=4, j=4)
            c4 = CTp.re("p (ct j) r n -> p ct j r n", ct=4)
            for j in range(4):
                P.act(c4[:, :, j, r, 32 * j:32 * j + 32], pv[:, :, j, :], AF.Copy, scale=(1.0 if r == 0 else -1.0))

    def s5_branch(l, ti, t0, T):
        is_s = (ti == n_ptiles)
        uT = scr(0, 4).re("p (c t) -> p c t", c=4)
        uTb = scr(4, 2, BF16).re("p (c t) -> p c t", c=4)
        y2b = scr(10, 2, BF16).re("p (c t) -> p c t", c=4)
        ysT = scr(12, 2, BF16).re("p (c t) -> p c t", c=4)
        plist = list(range(20, NBLK)) if is_s else [6, 7, 8, 9] + list(range(14, NBLK))

        def tmp():
            b = plist[st.tmp % len(plist)]
            st.tmp += 1
            return scr(b, 1)

        def cb_u(ft, pb):
            P.copy(uT[:, ft, :T], pb[:, :T], eng="act")
            P.copy(uTb[:, ft, :T], pb[:, :T], eng="act")
        dense_fm('w_in', l, 8, 4, xn, T, cb_u, c_base=OFF_U)

        if is_s:
            h0 = []
            for r in range(2):
                lds = scr(6, 4)
                P.dma("sp", lds[0:NS, :], st_s5[r][l])
                pb = bank()
                for s_ in range(16):
                    P.tr(pb[:, s_ * NS:(s_ + 1) * NS], lds[0:NS, s_ * 128:(s_ + 1) * 128], ident[0:NS, 0:NS])
                hh = scr(14 + r, 1)[:, 0:256]
                P.copy(hh, pb[:, 0:256], eng="act")
                h0.append(hh.re("p (s n) -> p s n", s=16))
            pbu = [bank(), bank()]
            for s_ in range(16):
                for r in range(2):
                    P.mm(pbu[r][:, s_ * NS:(s_ + 1) * NS], bbT[:, s_, r, :], uTb[:, s_ // 4, :NS])
            lrb, lib = LR.bcast(2, NS), LI.bcast(2, NS)
            hr = scr(16, 1)[:, 0:256].re("p (s n) -> p s n", s=16)
            hi = scr(17, 1)[:, 0:256].re("p (s n) -> p s n", s=16)
            t1 = scr(18, 1)[:, 0:256].re("p (s n) -> p s n", s=16)
            P.tt(hr, h0[0], lrb, ALU.mult)
            P.tt(t1, h0[1], lib, ALU.mult)
            P.tt(hr, hr, t1, ALU.subtract)
            P.tt(hr, hr, pbu[0][:, 0:256].re("p (s n) -> p s n", s=16), ALU.add)
            P.tt(hi, h0[1], lrb, ALU.mult)
            P.tt(t1, h0[0], lib, ALU.mult)
            P.tt(hi, hi, t1, ALU.add)
            P.tt(hi, hi, pbu[1][:, 0:256].re("p (s n) -> p s n", s=16), ALU.add)
            hb = scr(19, 1, BF16)[:, 0:512].re("p (r s n) -> p r s n", r=2, s=16)
            P.copy(hb[:, 0], hr, eng="act")
            P.copy(hb[:, 1], hi, eng="act")
            for r, hx in enumerate((hr, hi)):
                pb4 = bank(4)
                for s_ in range(16):
                    P.tr(pb4[0:NS, s_ * 128:(s_ + 1) * 128], hx[:, s_, :], ident)
                lds = scr(6, 4)
                P.copy(lds[0:NS, :], pb4[0:NS, :], eng="act")
                P.dma("pool", o_s5s[r][l], lds[0:NS, :], is_out=True)

        nch = max(1, T // CH)

        def v3(x):
            return x.re("p (k t) -> p k t", k=nch)

        def ck(tile_, s_, c):
            return V(tile_.ap[:, s_, c:c + 1], ((tile_.keys[0][1], s_, c),))

        for ct in range(4):
            py = bank()
            pyb = (st.bank - 1) % 8
            st.reserved.add(pyb)
            for jp in ((0, 1), (2, 3)):
                hb_pair = []
                if is_s:
                    for j in jp:
                        s_ = 4 * ct + j
                        hb_pair.append((hb[:, 0, s_, :], hb[:, 1, s_, :]))
                else:
                    AC = []
                    for j in jp:
                        s_ = 4 * ct + j
                        pre, pim = bank(), bank()
                        P.mm(pre[:, :T], bbT[:, s_, 0, :], uTb[:, ct, :T])
                        P.mm(pim[:, :T], bbT[:, s_, 1, :], uTb[:, ct, :T])
                        Rr, Ii = tmp(), tmp()
                        P.copy(Rr, pre, eng="act")
                        P.copy(Ii, pim, eng="act")
                        cb_ = cosT[:, s_, 0:CH].bcast(1, nch)
                        sb_ = sinT[:, s_, 0:CH].bcast(1, nch)
                        A, B, C, Dd = tmp(), tmp(), tmp(), tmp()
                        P.tt(v3(B), v3(Ii), sb_, ALU.mult, eng="pool")
                        P.tt(v3(Dd), v3(Rr), sb_, ALU.mult, eng="pool")
                        P.tt(v3(A), v3(Rr), cb_, ALU.mult)
                        P.tt(v3(C), v3(Ii), cb_, ALU.mult)
                        P.tt(A, A, B, ALU.add)
                        P.tt(C, C, Dd, ALU.subtract)
                        AC.append((A, C))
                    G = [(tmp(), tmp()) for _ in jp]
                    for k in range(nch):
                        sl = slice(k * CH, (k + 1) * CH)
                        first = (ti == 0 and k == 0)
                        e = (k + 1) * CH - 1
                        for idx, j in enumerate(jp):
                            s_ = 4 * ct + j
                            rb = MAG[:, s_:s_ + 1].bc([128, CH])
                            P.scan(G[idx][0][:, sl], rb, AC[idx][0][:, sl], 0.0 if first else ck(carry, s_, 0))
                            P.scan(G[idx][1][:, sl], rb, AC[idx][1][:, sl], 0.0 if first else ck(carry, s_, 1))
                        for idx, j in enumerate(jp):
                            s_ = 4 * ct + j
                            cC = cosT[:, s_, CH:CH + 1]
                            P.ts(ck(ctmp, s_, 0), G[idx][0][:, e:e + 1], cC, ALU.mult)
                            P.ts(ck(ctmp, s_, 1), G[idx][1][:, e:e + 1], cC, ALU.mult)
                        for idx, j in enumerate(jp):
                            s_ = 4 * ct + j
                            sC, nsC = sinT[:, s_, CH:CH + 1], nsinC[:, s_:s_ + 1]
                            P.stt(ck(carry, s_, 0), G[idx][1][:, e:e + 1], nsC, ck(ctmp, s_, 0), ALU.mult, ALU.add)
                            P.stt(ck(carry, s_, 1), G[idx][0][:, e:e + 1], sC, ck(ctmp, s_, 1), ALU.mult, ALU.add)
                    for idx, j in enumerate(jp):
                        s_ = 4 * ct + j
                        Gr, Gi = G[idx]
                        cb_ = cosT[:, s_, 0:CH].bcast(1, nch)
                        sb_ = sinT[:, s_, 0:CH].bcast(1, nch)
                        E, F, G2, H2 = tmp(), tmp(), tmp(), tmp()
                        P.tt(v3(F), v3(Gi), sb_, ALU.mult, eng="pool")
                        P.tt(v3(H2), v3(Gr), sb_, ALU.mult, eng="pool")
                        P.tt(v3(E), v3(Gr), cb_, ALU.mult)
                        P.tt(v3(G2), v3(Gi), cb_, ALU.mult)
                        Hrb, Hib = tmp().cast(BF16)[:, 0:512], tmp().cast(BF16)[:, 0:512]
                        if ti == n_ptiles - 1:
                            P.tt(E, E, F, ALU.subtract)
                            P.tt(G2, G2, H2, ALU.add)
                            P.copy(Hrb, E, eng="act")
                            P.copy(Hib, G2, eng="act")
                        else:
                            P.tt(Hrb, E, F, ALU.subtract)
                            P.tt(Hib, G2, H2, ALU.add)
                        if ti == n_ptiles - 1:
                            P.copy(hlast[:, 0, s_:s_ + 1], E[:, T - 1:T], eng="act")
                            P.copy(hlast[:, 1, s_:s_ + 1], G2[:, T - 1:T], eng="act")
                        hb_pair.append((Hrb, Hib))
                for idx, j in enumerate(jp):
                    s_ = 4 * ct + j
                    Hrb, Hib = hb_pair[idx]
                    P.mm(py[:, :T], CTp[:, s_, 0, :], Hrb[:, :T], start=(j == 0), stop=False)
                    P.mm(py[:, :T], CTp[:, s_, 1, :], Hib[:, :T], start=False, stop=(j == 3))
            yv = tmp()
            P.stt(yv[:, :T], uT[:, ct, :T], s5D[:, ct:ct + 1], py[:, :T], ALU.mult, ALU.add)
            st.reserved.discard(pyb)
            sq = tmp()
            P.tt(sq[:, :T], yv[:, :T], yv[:, :T], ALU.mult, eng="pool")
            P.ts(sq[:, :T], sq[:, :T], 0.044715, ALU.mult, s2=1.0, op1=ALU.add)
            P.tt(sq[:, :T], sq[:, :T], yv[:, :T], ALU.mult, eng="pool")
            P.act(sq[:, :T], sq[:, :T], AF.Sigmoid, scale=1.5957691216057308)
            P.tt(y2b[:, ct, :T], yv[:, :T], sq[:, :T], ALU.mult)
        if ti == n_ptiles - 1:
            for r in range(2):
                P.dma("pool", o_s5p[r][l].re("(s p) -> p s", p=128), hlast[:, r, :], is_out=True, allow_slow_non_contiguous=True)

        def cb_glu(ft, pb):
            sg = tmp()
            P.act(sg[:, :T], pb[:, :T], AF.Sigmoid, bias=s5bg[:, ft:ft + 1])
            P.tt(ysT[:, ft, :T], y2b[:, ft, :T], sg[:, :T], ALU.mult)
        dense_fm('s5_w_glu', l, 4, 4, y2b, T, cb_glu)
        branch_out(l, T, ysT, 'w_br_s5', 0)

    triU = cst[0:64, C_TRIU:C_TRIU + 64]
    msln = cst[0:64, C_MSLN:C_MSLN + 64]
    negU = cst[0:64, C_NEGU:C_NEGU + 64]
    id64 = cst[0:64, C_ID:C_ID + 64]
    ones64 = cst[0:64, C_ONE:C_ONE + 128]
    st_conv = P.dram("st_conv", [DEPTH, NS * 3, 1536])
    st_gdn = P.dram("st_gdn", [DEPTH, NS, 4, 128, 128])
    o_conv_p = P.dram("o_conv_p", [DEPTH, 3, 1536], kind="ExternalOutput")
    o_conv_s = P.dram("o_conv_s", [DEPTH, NS * 3, 1536], kind="ExternalOutput")
    o_gdn_p = P.dram("o_gdn_p", [DEPTH, 4, 128, 128], kind="ExternalOutput")
    o_gdn_s = P.dram("o_gdn_s", [DEPTH, NS, 4, 128, 128], kind="ExternalOutput")
    cw = P.sb("cw", [128, 12, 4])
    gA = P.sb("gA", [64, 4])
    gdtb = P.sb("gdtb", [64, 4])
    gng = P.sb("gng", [64, 128])
    gngc = P.sb("gngc", [128, 1])
    gS = P.sb("gS", [128, 4, 128])
    gtail = P.sb("gtail", [128, 12, 3])
    gt = P.sb("gt", [64, 12, 32])
    glS = P.sb("glS", [128, 32])

    def gdn_setup(l):
        for j in range(4):
            P.dma("sp", cw[:, :, j], W['gdn_conv_w'][l][j].re("(ct p) -> p ct", p=128), allow_slow_non_contiguous=True)
        P.dma("sp", gA, W['gdn_A_log'][l].re("(o h) -> o h", o=1).bc([64, 4]))
        P.dma("sp", gdtb, W['gdn_dt_bias'][l].re("(o h) -> o h", o=1).bc([64, 4]))
        P.dma("sp", gng, W['gdn_norm_g'][l].re("(o n) -> o n", o=1).bc([64, 128]))
        P.dma("sp", gngc, W['gdn_norm_g'][l].re("(n o) -> n o", o=1))
        P.act(gA, gA, AF.Exp)
        P.ts(gA, gA, -1.0, ALU.mult)
        P.memset(gS, 0.0)
        P.memset(gtail, 0.0)
        P.memset(gt, 0.0)

    def softplus_ip(x, t2):
        P.ts(t2, x, -1.0, ALU.mult)
        P.tt(t2, t2, x, ALU.max)
        P.act(t2, t2, AF.Exp, scale=-1.0)
        P.act(t2, t2, AF.Ln, bias=ones[0:x.ap.shape[0], 0:1])
        P.ts(x, x, 0.0, ALU.max)
        P.tt(x, x, t2, ALU.add)

    def gdn_branch(l, ti, t0, T):
        NCk = T // 64
        BETA, G, GC, EG, KD, BEG, X1, X2 = range(8)

        def q8(i):
            return gt[:, i, :].re("p (c h) -> p c h", c=8)

        def c8(v):
            return v.re("p (c n) -> p c n", c=8)
        ygT = scr(22, 2, BF16).re("p (h t) -> p h t", h=4)
        wba = wload('w_in', l, 8, OFF_B, 8)
        pbT = bank()
        for kc in range(8):
            P.mm(pbT[0:8, :T], wba[:, kc, :], xn[:, kc, :T], start=(kc == 0), stop=(kc == 7))
        baT = scr(21, 1)[0:8, :]
        P.copy(baT[:, :T], pbT[0:8, :T], eng="act")
        pba = bank()
        for c in range(NCk):
            P.tr(pba[0:64, c * 8:(c + 1) * 8], baT[:, c * 64:(c + 1) * 64], ident[0:8, 0:8])
        pv = c8(pba[0:64, 0:64])
        P.act(q8(BETA), pv[:, :, 0:4], AF.Sigmoid)
        P.tt(q8(X1), pv[:, :, 4:8], gdtb.bcast(1, 8), ALU.add)
        softplus_ip(gt[:, X1, :], gt[:, X2, :])
        P.tt(q8(G), q8(X1), gA.bcast(1, 8), ALU.mult)
        pg = bank()
        P.mm(pg[0:64, 0:32], triU, gt[:, G, :])
        P.mm(pg[:, 32:64], ones64, gt[:, G, :])
        P.copy(gt[:, GC, :], pg[0:64, 0:32], eng="act")
        P.act(gt[:, EG, :], pg[0:64, 0:32], AF.Exp)
        P.tt(gt[:, KD, :], pg[0:64, 32:64], gt[:, GC, :], ALU.subtract)
        P.act(gt[:, KD, :], gt[:, KD, :], AF.Exp)
        P.act(glS, pg[:, 32:64], AF.Exp)
        P.tt(gt[:, BEG, :], gt[:, BETA, :], gt[:, EG, :], ALU.mult)

        for h in range(4):
            pre = scr(0, 2)[:, 0:515]
            qc, kc_, vc = scr(2, 1), scr(3, 1), scr(4, 1)
            qkb = scr(5, 1, BF16)
            qTb, kTb = qkb[:, 0:512], qkb[:, 512:1024]
            kbg, kdec, vb = c8(scr(6, 2)[0:64]), c8(scr(8, 2)[0:64]), c8(scr(10, 2)[0:64])
            Nm, At = c8(scr(13, 1)[0:64]), c8(scr(14, 1)[0:64])

            def proj_conv(which, dst):
                ct = which * 4 + h
                pre = scr((0, 15, 17)[which], 2)[:, 0:515]

                def s1():
                    wq = wload('w_in', l, 8, OFF_QKV + ct * 128, 128)
                    pb = bank()
                    for kc in range(8):
                        P.mm(pb[:, :T], wq[:, kc, :], xn[:, kc, :T], start=(kc == 0), stop=(kc == 7))
                    P.copy(pre[:, 0:3], gtail[:, ct, :], eng="pool")
                    P.copy(pre[:, 3:3 + T], pb[:, :T], eng="act")
                    P.copy(gtail[:, ct, :], pre[:, T:T + 3], eng="pool")

                def s2():
                    P.ts(dst[:, :T], pre[:, 0:T], cw[:, ct, 0:1], ALU.mult)
                    P.stt(dst[:, :T], pre[:, 1:1 + T], cw[:, ct, 1:2], dst[:, :T], ALU.mult, ALU.add)

                def s3():
                    P.stt(dst[:, :T], pre[:, 2:2 + T], cw[:, ct, 2:3], dst[:, :T], ALU.mult, ALU.add)
                    P.stt(dst[:, :T], pre[:, 3:3 + T], cw[:, ct, 3:4], dst[:, :T], ALU.mult, ALU.add)
                return [s1, s2, s3]

            def l2norm(dst, scl):
                sq = scr(21, 1) if scl == 1.0 else scr(16, 1)

                def s1():
                    P.act(sq[:, :T], dst[:, :T], AF.Square)
                    pb = bank()
                    P.mm(pb[:, :T], ones, sq[:, :T])
                    P.act(sq[:, :T], pb[:, :T], AF.Ln, bias=epsc)
                    P.act(sq[:, :T], sq[:, :T], AF.Exp, scale=-0.5)

                def s2():
                    P.stt(dst[:, :T], dst[:, :T], scl, sq[:, :T], ALU.mult, ALU.mult)
                return [s1, s2]

            for s in proj_conv(1, kc_) + proj_conv(0, qc) + proj_conv(2, vc):
                s()
            wz = wload('w_in', l, 8, OFF_Z + h * 128, 128)
            pz = bank()
            for kc in range(8):
                P.mm(pz[:, :T], wz[:, kc, :], xn[:, kc, :T], start=(kc == 0), stop=(kc == 7))
            zsT = scr(20, 1)
            for dst in (kc_, qc, vc):
                P.act(dst[:, :T], dst[:, :T], AF.Silu)
            P.act(zsT[:, :T], pz[:, :T], AF.Silu)
            for s in l2norm(kc_, 1.0) + l2norm(qc, 128.0 ** -0.5):
                s()
            P.copy(kTb[:, :T], kc_[:, :T], eng="act")
            P.copy(qTb[:, :T], qc[:, :T], eng="act")
            pk = bank(2)
            for c in range(NCk):
                P.tr(pk[0:64, c * 128:(c + 1) * 128], kc_[:, c * 64:(c + 1) * 64], ident)
            P.tt(kbg, c8(pk[0:64, :]), q8(BEG)[:, :, h].bcast(2, 128), ALU.mult)
            P.tt(kdec, c8(pk[0:64, :]), q8(KD)[:, :, h].bcast(2, 128), ALU.mult)
            pKK, pGR = bank(), bank()
            for c in range(NCk):
                sl = slice(c * 64, (c + 1) * 64)
                P.mm(pKK[0:64, sl], kTb[:, sl], kTb[:, sl])
            diag = c8(scr(12, 1)[0:64])
            P.tt(diag, id64.bcast(1, 8), q8(GC)[:, :, h].bcast(2, 64), ALU.mult)
            P.mm(pGR[0:64, :T], ones64[:, 0:64], scr(12, 1)[0:64, :T])
            GR3 = c8(pGR[0:64, :])
            gcb = q8(GC)[:, :, h].bcast(2, 64)
            P.stt(Nm, GR3, -1.0, gcb, ALU.mult, ALU.add)
            P.ts(Nm, Nm, 0.0, ALU.min)
            P.act(Nm, Nm, AF.Exp)
            P.tt(At, GR3, gcb, ALU.subtract)
            P.ts(At, At, 0.0, ALU.min)
            P.act(At, At, AF.Exp)
            P.tt(Nm, Nm, c8(pKK[0:64, :]), ALU.mult)
            P.tt(Nm, Nm, q8(BETA)[:, :, h].bcast(2, 64), ALU.mult)
            P.tt(Nm, Nm, msln.bcast(1, 8), ALU.mult)

            def bg_qcast():
                P.copy(qTb[:, :T], qc[:, :T], eng="act")

            def bg_vtr():
                pv2 = bank(2)
                for c in range(NCk):
                    P.tr(pv2[0:64, c * 128:(c + 1) * 128], vc[:, c * 64:(c + 1) * 64], ident)
                P.tt(vb, c8(pv2[0:64, :]), q8(BETA)[:, :, h].bcast(2, 128), ALU.mult)

            def bg_qk():
                pQK = bank()
                for c in range(NCk):
                    sl = slice(c * 64, (c + 1) * 64)
                    P.mm(pQK[0:64, sl], kTb[:, sl], qTb[:, sl])
                P.tt(At, At, c8(pQK[0:64, :]), ALU.mult)
                P.tt(At, At, triU.bcast(1, 8), ALU.mult)

            def bg_qdec():
                dg2 = c8(scr(21, 1)[0:64])
                P.tt(dg2, id64.bcast(1, 8), q8(EG)[:, :, h].bcast(2, 64), ALU.mult)
                pe = bank()
                P.mm(pe[:, :T], ones64, scr(21, 1)[0:64, :T])
                P.tt(qc[:, :T], qc[:, :T], pe[:, :T], ALU.mult)
            bgq = [bg_qk, bg_vtr, bg_qdec]

            def run_bg(n):
                for _ in range(n):
                    if bgq:
                        bgq.pop(0)()

            Mm, Pm, Qm, Rm, Lm = (c8(scr(b, 1)[0:64]) for b in (12, 15, 16, 17, 18))
            ptr = bank()
            for c in range(NCk):
                P.tr(ptr[0:64, c * 64:(c + 1) * 64], Nm[:, c, :], id64)
            P.copy(Mm, c8(ptr[0:64, :]), eng="act")
            P.tt(Rm, Mm, id64.bcast(1, 8), ALU.add)
            P.tt(Lm, Nm, id64.bcast(1, 8), ALU.add, eng="pool")
            Pc, Qc = Nm, Mm
            for k in range(5):
                pQ = bank()
                for c in range(NCk):
                    P.mm(pQ[0:64, c * 64:(c + 1) * 64], Pc[:, c, :], Qc[:, c, :])
                if k < 4:
                    pP = bank()
                    for c in range(NCk):
                        P.mm(pP[0:64, c * 64:(c + 1) * 64], Qc[:, c, :], Pc[:, c, :])
                P.copy(Qm, c8(pQ[0:64, :]), eng="act")
                if k < 4:
                    P.copy(Pm, c8(pP[0:64, :]))
                Pc, Qc = Pm, Qm
                pR = bank()
                for c in range(NCk):
                    P.mm(pR[0:64, c * 64:(c + 1) * 64], Lm[:, c, :], Qm[:, c, :])
                if k < 4:
                    pL = bank()
                    for c in range(NCk):
                        P.mm(pL[0:64, c * 64:(c + 1) * 64], Rm[:, c, :], Pm[:, c, :])
                P.tt(Rm, Rm, c8(pR[0:64, :]), ALU.add)
                if k < 4:
                    P.tt(Lm, Lm, c8(pL[0:64, :]), ALU.add)
                run_bg(3)
            run_bg(len(bgq))
            pW = bank()
            for c in range(NCk):
                P.mm(pW[:, c * 64:(c + 1) * 64], kbg[:, c, :], Rm[:, c, :])
            wT = scr(3, 1)
            P.copy(wT[:, :T], pW[:, :T], eng="act")
            pU = bank(2)
            for c in range(NCk):
                P.mm(pU[0:64, c * 128:(c + 1) * 128], Rm[:, c, :], vb[:, c, :])
            u = c8(scr(0, 2)[0:64])
            P.copy(u, c8(pU[0:64, :]), eng="act")
            o_tm = c8(scr(10, 2)[0:64])
            for c in range(NCk):
                sl = slice(c * 64, (c + 1) * 64)
                pw = bank()
                P.mm(pw[0:64, 0:128], wT[:, sl], gS[:, h, :])
                vn = scr(19, 1)[0:64, (c % 2) * 128:(c % 2) * 128 + 128]
                P.tt(vn, u[:, c, :], pw[0:64, 0:128], ALU.subtract)
                po = bank()
                P.mm(po[0:64, 0:128], qc[:, sl], gS[:, h, :])
                po2 = bank()
                P.mm(po2[0:64, 0:128], At[:, c, :], vn)
                P.copy(o_tm[:, c, :], po[0:64, 0:128], eng="act")
                P.tt(o_tm[:, c, :], o_tm[:, c, :], po2[0:64, 0:128], ALU.add)
                pS = bank()
                P.mm(pS[:, 0:128], kdec[:, c, :], vn)
                P.stt(gS[:, h, :], gS[:, h, :], glS[:, c * 4 + h:c * 4 + h + 1], pS[:, 0:128], ALU.mult, ALU.add)
            sq = c8(scr(8, 2)[0:64])
            P.tt(sq, o_tm, o_tm, ALU.mult)
            ss = gt[:, X1, 0:8]
            P.reduce(ss, sq, ALU.add)
            P.act(ss, ss, AF.Ln, bias=epsc[0:64], scale=1.0 / 128)
            P.act(ss, ss, AF.Exp, scale=-0.5)
            P.tt(o_tm, o_tm, ss.bcast(2, 128), ALU.mult)
            P.tt(o_tm, o_tm, gng.bcast(1, 8), ALU.mult)
            pt = bank()
            for c in range(NCk):
                P.tr(pt[:, c * 64:(c + 1) * 64], o_tm[:, c, :], id64)
            P.tt(ygT[:, h, :T], pt[:, :T], zsT[:, :T], ALU.mult)
        if ti == n_ptiles - 1:
            P.dma("pool", o_gdn_p[l].re("h k v -> k h v"), gS, is_out=True)
            for j in range(3):
                P.dma("pool", o_conv_p[l][j].re("(ct p) -> p ct", p=128), gtail[:, :, j], is_out=True, allow_slow_non_contiguous=True)
        branch_out(l, T, ygT, 'w_br_gdn', 1)

    def gdn_sample(l):
        T = NS
        ygT = scr(22, 2, BF16).re("p (h t) -> p h t", h=4)
        id16 = ident[0:16, 0:16]
        ldc = scr(0, 3)[0:48, :]
        P.dma("sp", ldc, st_conv[l])
        pre = scr(3, 2)[:, 0:768].re("p (ct r j) -> p ct r j", ct=12, r=16)
        pb = bank(2)
        for ct in range(12):
            P.tr(pb[:, ct * 64:ct * 64 + 48], ldc[0:48, ct * 128:(ct + 1) * 128], ident[0:48, 0:48])
        pbv = pb[:, 0:768].re("p (ct x) -> p ct x", ct=12)[:, :, 0:48].re("p ct (r j) -> p ct r j", r=16)
        P.copy(pre[:, :, :, 0:3], pbv, eng="act")

        def cb_q(ft, pbk):
            P.copy(pre[:, ft, :, 3], pbk[:, :T], eng="act")
        dense_fm('w_in', l, 8, 12, xn, T, cb_q, c_base=OFF_QKV)
        cvf = scr(5, 1)[:, 0:192]
        cv = cvf.re("p (ct r) -> p ct r", ct=12)
        t1 = scr(6, 1)[:, 0:192].re("p (ct r) -> p ct r", ct=12)
        P.tt(cv, pre[:, :, :, 0], cw[:, :, 0].bcast(2, 16), ALU.mult)
        for j in range(1, 4):
            P.tt(t1, pre[:, :, :, j], cw[:, :, j].bcast(2, 16), ALU.mult)
            P.tt(cv, cv, t1, ALU.add)
        P.act(cv, cv, AF.Silu)
        nb = scr(7, 2)[:, 0:576].re("p (ct x) -> p ct x", ct=12)
        P.copy(nb.re("p ct (r j) -> p ct r j", r=16), pre[:, :, :, 1:4], eng="pool")
        pb4 = bank(4)
        for ct in range(12):
            P.tr(pb4[0:48, ct * 128:(ct + 1) * 128], nb[:, ct, :], ident)
        P.copy(ldc, pb4[0:48, 0:1536], eng="act")
        P.dma("pool", o_conv_s[l], ldc, is_out=True)
        sq = scr(6, 1)[:, 0:128]
        P.act(sq, cvf[:, 0:128], AF.Square)
        pb = bank()
        P.mm(pb[:, 0:128], ones, sq)
        P.act(sq, pb[:, 0:128], AF.Ln, bias=epsc)
        P.act(sq, sq, AF.Exp, scale=-0.5)
        P.tt(cvf[:, 0:128], cvf[:, 0:128], sq, ALU.mult)
        P.ts(cvf[:, 0:64], cvf[:, 0:64], 128.0 ** -0.5, ALU.mult)
        wba = wload('w_in', l, 8, OFF_B, 8)
        pba = bank()
        for kc in range(8):
            P.mm(pba[0:16, 0:8], xn[:, kc, :T], wba[:, kc, :], start=(kc == 0), stop=(kc == 7))
        be = gt[0:16, 0, 0:8]
        xg = gt[0:16, 1, 0:4]
        x2 = gt[0:16, 2, 0:4]
        P.act(be[:, 0:4], pba[0:16, 0:4], AF.Sigmoid)
        P.tt(xg, pba[0:16, 4:8], gdtb[0:16, :], ALU.add)
        softplus_ip(xg, x2)
        P.tt(xg, xg, gA[0:16, :], ALU.mult)
        P.act(be[:, 4:8], xg, AF.Exp)
        dg = scr(9, 1)[0:16, 0:128].re("p (n r) -> p n r", n=8)
        P.tt(dg, id16.bcast(1, 8), be.bcast(2, 16), ALU.mult)
        prb = bank()
        P.mm(prb[:, 0:128], ones[0:16, :], scr(9, 1)[0:16, 0:128])
        rowb = scr(10, 1)[:, 0:128].re("p (n r) -> p n r", n=8)
        P.copy(rowb, prb[:, 0:128].re("p (n r) -> p n r", n=8), eng="act")
        betaR = rowb[:, 0:4, :].re("p h r -> p r h")
        egR = rowb[:, 4:8, :].re("p h r -> p r h")
        kq = scr(11, 1)[:, 0:128].re("p (r h n) -> p r h n", r=16, h=4)
        P.copy(kq[:, :, :, 0], cv[:, 4:8, :].re("p h r -> p r h"), eng="pool")
        P.copy(kq[:, :, :, 1], cv[:, 0:4, :].re("p h r -> p r h"), eng="pool")
        vT = cv[:, 8:12, :].re("p h r -> p r h")
        pkq = bank()
        pkqb = (st.bank - 1) % 8
        st.reserved.add(pkqb)
        for r in range(NS):
            Sr = scr(12 + (r % 4), 1).re("p (h v) -> p h v", h=4)
            P.dma("sp", Sr, st_gdn[l, r].re("h k v -> k h v"))
            for h in range(4):
                o = (r * 4 + h) * 2
                P.mm(pkq[:, o:o + 2], Sr[:, h, :], kq[:, r, h, :])
        kqS = scr(16, 1)[:, 0:128].re("p (r h n) -> p r h n", r=16, h=4)
        P.copy(kqS, pkq[:, 0:128].re("p (r h n) -> p r h n", r=16, h=4), eng="act")
        st.reserved.discard(pkqb)

        def rh(b):
            return scr(b, 1)[:, 0:64].re("p (r h) -> p r h", r=16)
        vnew, tA, oT, tB = rh(17), rh(18), rh(19), rh(20)
        P.tt(tA, kqS[:, :, :, 0], egR, ALU.mult)
        P.tt(tA, vT, tA, ALU.subtract)
        P.tt(vnew, tA, betaR, ALU.mult)
        P.tt(tA, kq[:, :, :, 0], kq[:, :, :, 1], ALU.mult)
        pat = bank()
        P.mm(pat[:, 0:64], ones, scr(18, 1)[:, 0:64])
        P.tt(oT, kqS[:, :, :, 1], egR, ALU.mult)
        P.tt(tB, vnew, pat[:, 0:64].re("p (r h) -> p r h", r=16), ALU.mult)
        P.tt(oT, oT, tB, ALU.add)
        P.tt(tB, oT, oT, ALU.mult)
        pss = bank()
        P.mm(pss[:, 0:64], ones, scr(20, 1)[:, 0:64])
        P.act(tB, pss[:, 0:64].re("p (r h) -> p r h", r=16), AF.Ln, bias=epsc, scale=1.0 / 128)
        P.act(tB, tB, AF.Exp, scale=-0.5)
        P.tt(oT, oT, tB, ALU.mult)
        P.ts(oT, oT, gngc[:, 0:1], ALU.mult)
        zT = scr(21, 1)[:, 0:64].re("p (h r) -> p h r", h=4)

        def cb_z(ft, pbk):
            P.act(zT[:, ft, :], pbk[:, :T], AF.Silu)
        dense_fm('w_in', l, 8, 4, xn, T, cb_z, c_base=OFF_Z)
        P.tt(ygT[:, :, :T].re("p h r -> p r h"), oT, zT.re("p h r -> p r h"), ALU.mult)
        for r in range(NS):
            Sr = scr(12 + (r % 4), 1).re("p (h v) -> p h v", h=4)
            P.dma("sp", Sr, st_gdn[l, r].re("h k v -> k h v"))
            So = scr(0 + (r % 2), 1).re("p (h v) -> p h v", h=4)
            for h in range(4):
                dgv = scr(2 + (h % 2), 1)[:, 0:128]
                P.ts(dgv, ident, vnew[:, r, h:h + 1], ALU.mult)
                pvr = bank()
                P.mm(pvr[:, 0:128], ones, dgv)
                P.act(So[:, h, :], Sr[:, h, :], AF.Copy, scale=rowb[:, 4 + h, r:r + 1])
                P.stt(So[:, h, :], pvr[:, 0:128], kq[:, r, h, 0:1], So[:, h, :], ALU.mult, ALU.add)
            P.dma("pool", o_gdn_s[l, r].re("h k v -> k h v"), So, is_out=True)
        branch_out(l, T, ygT, 'w_br_gdn', 1)

    sel63 = cst[0:64, C_SEL63:C_SEL63 + 128]
    st_mC = P.dram("st_mC", [DEPTH, NS, 4, 64, 128])
    st_mn = P.dram("st_mn", [DEPTH, NS * 4, 64])
    st_mm = P.dram("st_mm", [DEPTH, NS, 4])
    o_mC_p = P.dram("o_mC_p", [DEPTH, 4, 64, 128], kind="ExternalOutput")
    o_mC_s = P.dram("o_mC_s", [DEPTH, NS, 4, 64, 128], kind="ExternalOutput")
    o_mn_p = P.dram("o_mn_p", [DEPTH, 4, 64], kind="ExternalOutput")
    o_mn_s = P.dram("o_mn_s", [DEPTH, NS * 4, 64], kind="ExternalOutput")
    o_mm_p = P.dram("o_mm_p", [DEPTH, 4], kind="ExternalOutput")
    o_mm_s = P.dram("o_mm_s", [DEPTH, NS, 4], kind="ExternalOutput")
    mbi = P.sb("mbi", [64, 4])
    mbf = P.sb("mbf", [64, 4])
    mng = P.sb("mng", [64, 512])
    mngc = P.sb("mngc", [128, 4])
    mC = P.sb("mC", [64, 4, 128])
    mn = P.sb("mn", [64, 4])
    mcar = P.sb("mcar", [64, 4])
    mt = P.sb("mt", [64, 12, 32])
    mp = P.sb("mp", [64, 16])
    mns = P.sb("mns", [64, 8])

    def ml_setup(l):
        P.dma("sp", mbi, W['ml_b_i'][l].re("(o h) -> o h", o=1).bc([64, 4]))
        P.dma("sp", mbf, W['ml_b_f'][l].re("(o h) -> o h", o=1).bc([64, 4]))
        P.dma("sp", mng, W['ml_norm_g'][l].re("(o n) -> o n", o=1).bc([64, 512]))
        P.dma("sp", mngc, W['ml_norm_g'][l].re("(h p) -> p h", p=128), allow_slow_non_contiguous=True)
        P.memset(mC, 0.0)
        P.memset(mn, 0.0)
        P.memset(mcar, 0.0)
        P.memset(mt, 0.0)
        P.memset(mp, 0.0)

    def gates_if(pv_i, pv_f, LIv, LFv, X1v, X2v, bi, bf):
        P.tt(X1v, pv_i, bi, ALU.add)
        P.act(LIv, X1v, AF.Tanh, scale=1.0 / 15.0)
        P.ts(LIv, LIv, 15.0, ALU.mult)
        P.tt(X1v, pv_f, bf, ALU.add)
        P.act(X1v, X1v, AF.Tanh, scale=1.0 / 15.0)
        P.ts(X1v, X1v, -15.0, ALU.mult)
        softplus_ip(X1v, X2v)
        P.ts(LFv, X1v, -1.0, ALU.mult)

    def ml_branch(l, ti, t0, T):
        NCk = T // 64
        LI, LF, BC, AS, MX, INTER, MT, SC, X1, X2, NEM, BL = range(12)

        def m8(i):
            return mt[:, i, :].re("p (c h) -> p c h", c=8)

        def c8(v):
            return v.re("p (c n) -> p c n", c=8)
        ymT = scr(22, 2, BF16).re("p (h t) -> p h t", h=4)
        wif = wload('w_in', l, 8, OFF_MI, 8)
        pfT = bank()
        for kc in range(8):
            P.mm(pfT[0:8, :T], wif[:, kc, :], xn[:, kc, :T], start=(kc == 0), stop=(kc == 7))
        ifT = scr(21, 1)[0:8, :]
        P.copy(ifT[:, :T], pfT[0:8, :T], eng="act")
        pif = bank()
        for c in range(NCk):
            P.tr(pif[0:64, c * 8:(c + 1) * 8], ifT[:, c * 64:(c + 1) * 64], ident[0:8, 0:8])
        pv = c8(pif[0:64, 0:64])
        gates_if(pv[:, :, 0:4], pv[:, :, 4:8], m8(LI), m8(LF), m8(X1), m8(X2), mbi.bcast(1, 8), mbf.bcast(1, 8))
        pg = bank()
        P.mm(pg[0:64, 0:32], triU, mt[:, LF, :])
        P.mm(pg[0:64, 32:64], ones64[:, 0:64], mt[:, LF, :])
        P.copy(mt[:, BC, :], pg[0:64, 0:32], eng="act")
        P.copy(mt[:, BL, :], pg[0:64, 32:64], eng="act")
        P.tt(mt[:, AS, :], mt[:, LI, :], mt[:, BC, :], ALU.subtract)

        for h in range(4):
            diag = c8(scr(12, 1)[0:64])
            P.tt(diag, id64.bcast(1, 8), m8(AS)[:, :, h].bcast(2, 64), ALU.mult)
            pAR = bank()
            P.mm(pAR[0:64, :T], ones64[:, 0:64], scr(12, 1)[0:64, :T])
            LW = c8(scr(6, 1)[0:64])
            P.tt(LW, c8(pAR[0:64, :]), m8(BC)[:, :, h].bcast(2, 64), ALU.add)
            P.tt(LW, LW, negU.bcast(1, 8), ALU.add)
            P.reduce(m8(MX)[:, :, h], LW, ALU.max)
            psel = bank()
            P.mm(psel[0:64, 0:32], sel63[:, 0:64], mt[:, MX, :])
            mx63 = m8(X2)
            P.copy(mt[:, X2, :], psel[0:64, 0:32], eng="act")
            P.copy(mp[:, 0:1], mcar[:, h:h + 1])
            for c in range(NCk):
                P.ts(mp[:, c + 1:c + 2], m8(BL)[:, c, h:h + 1], mp[:, c:c + 1], ALU.add, s2=mx63[:, c, h:h + 1], op1=ALU.max)
            P.copy(mcar[:, h:h + 1], mp[:, NCk:NCk + 1])
            P.tt(m8(INTER)[:, :, h], m8(BC)[:, :, h], mp[:, 0:NCk], ALU.add)
            P.tt(m8(MT)[:, :, h], m8(INTER)[:, :, h], m8(MX)[:, :, h], ALU.max)
            P.tt(m8(SC)[:, :, h], m8(INTER)[:, :, h], m8(MT)[:, :, h], ALU.subtract)
            P.act(m8(SC)[:, :, h], m8(SC)[:, :, h], AF.Exp)
            P.act(m8(NEM)[:, :, h], m8(MT)[:, :, h], AF.Exp, scale=-1.0)
            P.tt(LW, LW, m8(MT)[:, :, h].bcast(2, 64), ALU.subtract)
            P.act(LW, LW, AF.Exp)
            pscl = bank()
            P.mm(pscl[0:64, 0:32], sel63[:, 0:64], mt[:, SC, :])
            scl = mp[:, 9:9 + NCk - 1 + 1] if False else None
            P.copy(mt[:, X1, :], pscl[0:64, 0:32], eng="act")
            sclv = m8(X1)
            qT, kT, vT = scr(0, 1)[0:64], scr(1, 1)[0:64], scr(3, 1)
            qkb = scr(2, 1, BF16)
            qTb, kTb = qkb[0:64, 0:512], qkb[0:64, 512:1024]
            wq = wload('w_in', l, 8, OFF_MQ + h * 64, 64)
            pq = bank()
            for kc in range(8):
                P.mm(pq[0:64, :T], wq[:, kc, :], xn[:, kc, :T], start=(kc == 0), stop=(kc == 7))
            P.copy(qT[:, :T], pq[0:64, :T], eng="act")
            P.copy(qTb[:, :T], pq[0:64, :T], eng="act")
            wk = wload('w_in', l, 8, OFF_MK + h * 64, 64)
            pk = bank()
            for kc in range(8):
                P.mm(pk[0:64, :T], wk[:, kc, :], xn[:, kc, :T], start=(kc == 0), stop=(kc == 7))
            P.act(kT[:, :T], pk[0:64, :T], AF.Copy, scale=0.125)
            P.copy(kTb[:, :T], kT[:, :T], eng="act")
            wv = wload('w_in', l, 8, OFF_MV + h * 128, 128)
            pvv = bank()
            for kc in range(8):
                P.mm(pvv[:, :T], wv[:, kc, :], xn[:, kc, :T], start=(kc == 0), stop=(kc == 7))
            P.copy(vT[:, :T], pvv[:, :T], eng="act")
            v_tm = c8(scr(4, 2)[0:64])
            pvt = bank(2)
            for c in range(NCk):
                P.tr(pvt[0:64, c * 128:(c + 1) * 128], vT[:, c * 64:(c + 1) * 64], ident)
            P.copy(v_tm, c8(pvt[0:64, :]), eng="act")
            wtsT = c8(scr(7, 1)[0:64])
            pwt = bank()
            for c in range(NCk):
                P.tr(pwt[0:64, c * 64:(c + 1) * 64], LW[:, c, :], id64)
            P.copy(wtsT, c8(pwt[0:64, :]), eng="act")
            pQK = bank()
            for c in range(NCk):
                sl = slice(c * 64, (c + 1) * 64)
                P.mm(pQK[0:64, sl], kTb[:, sl], qTb[:, sl])
            sqkT = c8(scr(8, 1)[0:64])
            P.tt(sqkT, wtsT, c8(pQK[0:64, :]), ALU.mult)
            kw = c8(scr(9, 1)[0:64])
            pkt = bank()
            for c in range(NCk):
                P.tr(pkt[0:64, c * 64:(c + 1) * 64], kT[:, c * 64:(c + 1) * 64], id64)
            P.tt(kw, c8(pkt[0:64, :]), wtsT[:, :, 63].bcast(2, 64), ALU.mult)
            pN = bank(2)
            pD = bank()
            for c in range(NCk):
                P.mm(pN[0:64, c * 128:(c + 1) * 128], sqkT[:, c, :], v_tm[:, c, :])
                P.mm(pD[0:64, c:c + 1], sqkT[:, c, :], ones64[:, 0:1])
            num = c8(scr(10, 2)[0:64])
            P.copy(num, c8(pN[0:64, :]), eng="act")
            den = mt[:, X2, 0:8]
            P.copy(den[:, 0:NCk], pD[0:64, 0:NCk])
            pI = bank(2)
            pIn = bank()
            for c in range(NCk):
                P.mm(pI[0:64, c * 128:(c + 1) * 128], kw[:, c, :], v_tm[:, c, :])
                P.mm(pIn[0:64, c:c + 1], kw[:, c, :], ones64[:, 0:1])
            Inc = c8(scr(17, 2)[0:64])
            P.copy(Inc, c8(pI[0:64, :]), eng="act")
            IncN = mp[:, 8:16]
            P.copy(IncN[:, 0:NCk], pIn[0:64, 0:NCk], eng="act")
            Cs = c8(scr(19, 2)[0:64])
            ns = mns[:, 0:8]
            P.copy(Cs[:, 0, :], mC[:, h, :], eng="pool")
            P.copy(ns[:, 0:1], mn[:, h:h + 1], eng="pool")
            for c in range(NCk):
                sl_ = sclv[:, c, h:h + 1]
                dstC = Cs[:, c + 1, :] if c < NCk - 1 else mC[:, h, :]
                dstn = ns[:, c + 1:c + 2] if c < NCk - 1 else mn[:, h:h + 1]
                P.stt(dstC, Cs[:, c, :], sl_, Inc[:, c, :], ALU.mult, ALU.add)
                P.stt(dstn, ns[:, c:c + 1], sl_, IncN[:, c:c + 1], ALU.mult, ALU.add)
            pQC = bank(2)
            pQn = bank()
            for c in range(NCk):
                sl = slice(c * 64, (c + 1) * 64)
                P.mm(pQC[0:64, c * 128:(c + 1) * 128], qT[:, sl], Cs[:, c, :])
                P.mm(pQn[0:64, c:c + 1], qT[:, sl], ns[:, c:c + 1])
            scb = m8(SC)[:, :, h]
            tq = c8(scr(17, 2)[0:64])
            P.tt(tq, c8(pQC[0:64, :]), scb.bcast(2, 128), ALU.mult)
            P.tt(num, num, tq, ALU.add)
            tn = mp[:, 8:16]
            P.tt(tn[:, 0:NCk], pQn[0:64, 0:NCk], scb, ALU.mult)
            P.tt(den[:, 0:NCk], den[:, 0:NCk], tn[:, 0:NCk], ALU.add)
            dd = mp[:, 8:16]
            P.ts(dd[:, 0:NCk], den[:, 0:NCk], -1.0, ALU.mult)
            P.tt(dd[:, 0:NCk], dd[:, 0:NCk], den[:, 0:NCk], ALU.max)
            P.tt(dd[:, 0:NCk], dd[:, 0:NCk], m8(NEM)[:, :, h], ALU.max)
            P.recip(dd[:, 0:NCk], dd[:, 0:NCk])
            P.tt(num, num, dd[:, 0:NCk].bcast(2, 128), ALU.mult)
            wmo = wload('w_in', l, 8, OFF_MO + h * 128, 128)
            pz = bank()
            for kc in range(8):
                P.mm(pz[:, :T], wmo[:, kc, :], xn[:, kc, :T], start=(kc == 0), stop=(kc == 7))
            zsT = scr(13, 1)
            P.act(zsT[:, :T], pz[:, :T], AF.Sigmoid)
            sq = c8(scr(15, 2)[0:64])
            P.tt(sq, num, num, ALU.mult)
            ss = mp[:, 8:16]
            P.reduce(ss[:, 0:NCk], sq, ALU.add)
            P.act(ss[:, 0:NCk], ss[:, 0:NCk], AF.Ln, bias=epsc[0:64], scale=1.0 / 128)
            P.act(ss[:, 0:NCk], ss[:, 0:NCk], AF.Exp, scale=-0.5)
            P.tt(num, num, ss[:, 0:NCk].bcast(2, 128), ALU.mult)
            P.tt(num, num, mng[:, h * 128:(h + 1) * 128].bcast(1, 8), ALU.mult)
            pt = bank()
            for c in range(NCk):
                P.tr(pt[:, c * 64:(c + 1) * 64], num[:, c, :], id64)
            P.tt(ymT[:, h, :T], pt[:, :T], zsT[:, :T], ALU.mult)
        if ti == n_ptiles - 1:
            P.dma("pool", o_mC_p[l].re("h k v -> k h v"), mC, is_out=True)
            P.dma("pool", o_mn_p[l].re("h k -> k h"), mn, is_out=True, allow_slow_non_contiguous=True)
            P.dma("pool", o_mm_p[l].re("(o h) -> o h", o=1), mcar[0:1, :], is_out=True)
        branch_out(l, T, ymT, 'w_br_ml', 2)

    def ml_sample(l):
        T = NS
        ymT = scr(22, 2, BF16).re("p (h t) -> p h t", h=4)
        id16 = ident[0:16, 0:16]
        wif = wload('w_in', l, 8, OFF_MI, 8)
        pif = bank()
        for kc in range(8):
            P.mm(pif[0:16, 0:8], xn[:, kc, :T], wif[:, kc, :], start=(kc == 0), stop=(kc == 7))
        sc3 = mt[0:16, 0, 0:12]
        gates_if(pif[0:16, 0:4], pif[0:16, 4:8], sc3[:, 0:4], sc3[:, 4:8], mt[0:16, 1, 0:4], mt[0:16, 2, 0:4],
                 mbi[0:16, :], mbf[0:16, :])
        P.dma("sp", sc3[:, 8:12], st_mm[l])
        dg = scr(9, 1)[0:16, 0:192].re("p (n r) -> p n r", n=12)
        P.tt(dg, id16.bcast(1, 12), sc3.bcast(2, 16), ALU.mult)
        prb = bank()
        P.mm(prb[:, 0:192], ones[0:16, :], scr(9, 1)[0:16, 0:192])
        rb = scr(10, 1)[:, 0:192].re("p (n r) -> p n r", n=12)
        P.copy(rb, prb[:, 0:192].re("p (n r) -> p n r", n=12), eng="act")
        liR = rb[:, 0:4, :].re("p h r -> p r h")
        lfR = rb[:, 4:8, :].re("p h r -> p r h")
        m0R = rb[:, 8:12, :].re("p h r -> p r h")
        E = scr(11, 1).re("p (q n) -> p q n", q=8)

        def e(i):
            return E[:, i, :].re("p (r h) -> p r h", r=16)
        inter, m_t, wts, scv, nem, t1, t2, t3 = (e(i) for i in range(8))
        P.tt(inter, lfR, m0R, ALU.add)
        P.tt(m_t, inter, liR, ALU.max)
        P.tt(wts, liR, m_t, ALU.subtract)
        P.act(wts, wts, AF.Exp)
        P.tt(scv, inter, m_t, ALU.subtract)
        P.act(scv, scv, AF.Exp)
        P.act(nem, m_t, AF.Exp, scale=-1.0)
        P.dma("pool", o_mm_s[l].re("(o r) h -> o r h", o=1), m_t[0:1], is_out=True)
        F = scr(12, 1).re("p (q n) -> p q n", q=8)

        def f(i, np_=128):
            return F[0:np_, i, :].re("p (r h) -> p r h", r=16)
        qT, kT, vT, moT, n0T, kwT, nnew, t4 = f(0, 64), f(1, 64), f(2), f(3), f(4, 64), f(5, 64), f(6, 64), f(7)
        for (off, dk, dst, func, scl) in ((OFF_MQ, 64, qT, AF.Copy, 1.0), (OFF_MK, 64, kT, AF.Copy, 0.125),
                                          (OFF_MV, 128, vT, AF.Copy, 1.0), (OFF_MO, 128, moT, AF.Sigmoid, 1.0)):
            pp = bank()
            for h in range(4):
                wq = wload('w_in', l, 8, off + h * dk, dk)
                for kc in range(8):
                    P.mm(pp[0:dk, h * 16:(h + 1) * 16], wq[:, kc, :], xn[:, kc, :T], start=(kc == 0), stop=(kc == 7))
            P.act(dst, pp[0:dk, 0:64].re("p (h r) -> p r h", h=4), func, scale=scl)
        ldn = scr(13, 1)[0:64, 0:64]
        P.dma("sp", ldn, st_mn[l])
        pnt = bank()
        P.tr(pnt[0:64, 0:64], ldn, id64)
        P.copy(n0T, pnt[0:64, 0:64].re("p (r h) -> p r h", r=16), eng="act")
        pr2 = scr(13, 1)[0:64, 64:192].re("p (q r h) -> p q r h", q=2, r=16)
        P.tt(pr2[:, 0], qT, kT, ALU.mult)
        P.tt(pr2[:, 1], qT, n0T, ALU.mult)
        pqk = bank()
        P.mm(pqk[:, 0:128], ones64, scr(13, 1)[0:64, 64:192])
        qk = pqk[:, 0:64].re("p (r h) -> p r h", r=16)
        qn = pqk[:, 64:128].re("p (r h) -> p r h", r=16)
        pqc = bank()
        pqcb = (st.bank - 1) % 8
        st.reserved.add(pqcb)
        for r in range(NS):
            Cr = scr(14 + (r % 4), 1)[0:64].re("p (h v) -> p h v", h=4)
            P.dma("sp", Cr, st_mC[l, r].re("h k v -> k h v"))
            for h in range(4):
                P.mm(pqc[:, r * 4 + h:r * 4 + h + 1], Cr[:, h, :], qT[:, r, h:h + 1])
        sqk = t1
        P.tt(sqk, wts, qk, ALU.mult)
        P.tt(t2, scv, pqc[:, 0:64].re("p (r h) -> p r h", r=16), ALU.mult)
        st.reserved.discard(pqcb)
        P.tt(t3, sqk, vT, ALU.mult)
        P.tt(t3, t3, t2, ALU.add)
        P.tt(t2, scv, qn, ALU.mult)
        P.tt(t2, t2, sqk, ALU.add)
        P.ts(t4, t2, -1.0, ALU.mult)
        P.tt(t4, t4, t2, ALU.max)
        P.tt(t4, t4, nem, ALU.max)
        P.recip(t4, t4)
        P.tt(t3, t3, t4, ALU.mult)
        P.tt(t4, t3, t3, ALU.mult)
        pss = bank()
        P.mm(pss[:, 0:64], ones, F[:, 7, :])
        P.act(t4, pss[:, 0:64].re("p (r h) -> p r h", r=16), AF.Ln, bias=epsc, scale=1.0 / 128)
        P.act(t4, t4, AF.Exp, scale=-0.5)
        P.tt(t3, t3, t4, ALU.mult)
        P.tt(t3, t3, mngc.bcast(1, 16), ALU.mult)
        P.tt(ymT[:, :, :T].re("p h r -> p r h"), t3, moT, ALU.mult)
        P.tt(kwT, kT, wts[0:64], ALU.mult)
        P.tt(nnew, n0T, scv[0:64], ALU.mult)
        P.tt(nnew, nnew, kwT, ALU.add)
        pno = bank()
        P.tr(pno[0:64, 0:64], F[0:64, 6, :], id64)
        P.copy(ldn, pno[0:64, 0:64], eng="act")
        P.dma("pool", o_mn_s[l], ldn, is_out=True)
        for r in range(NS):
            Cr = scr(14 + (r % 4), 1)[0:64].re("p (h v) -> p h v", h=4)
            P.dma("sp", Cr, st_mC[l, r].re("h k v -> k h v"))
            Co = scr(0 + (r % 2), 1)[0:64].re("p (h v) -> p h v", h=4)
            for h in range(4):
                dgv = scr(2 + (h % 2), 1)[:, 0:128]
                P.ts(dgv, ident, vT[:, r, h:h + 1], ALU.mult)
                pvr = bank()
                P.mm(pvr[0:64, 0:128], ones[:, 0:64], dgv)
                P.act(Co[:, h, :], Cr[:, h, :], AF.Copy, scale=scv[0:64, r, h:h + 1])
                P.stt(Co[:, h, :], pvr[0:64, 0:128], kwT[:, r, h:h + 1], Co[:, h, :], ALU.mult, ALU.add)
            P.dma("pool", o_mC_s[l, r].re("h k v -> k h v"), Co, is_out=True)
        branch_out(l, T, ymT, 'w_br_ml', 2)

    hd = scr(0, 8, BF16).re("p (k t) -> p k t", k=16)
    for l in range(DEPTH):
        if BR_S5:
            s5_setup(l)
        if BR_GDN:
            gdn_setup(l)
        if BR_ML:
            ml_setup(l)
        for ti, (t0, T) in enumerate(tiles):
            xt = xT[:, :, t0:t0 + T]
            rmsnorm(xt, T, g1[:, l, :])
            P.memset(mg[:, :, :T], 0.0)
            if BR_S5:
                s5_branch(l, ti, t0, T)
            if BR_GDN:
                if ti < n_ptiles:
                    gdn_branch(l, ti, t0, T)
                else:
                    gdn_sample(l)
            if BR_ML:
                if ti < n_ptiles:
                    ml_branch(l, ti, t0, T)
                else:
                    ml_sample(l)

            def cb_res(ft, pb):
                P.tt(xt[:, ft, :], xt[:, ft, :], pb[:, :T], ALU.add)
            dense_fm('w_out', l, 8, 8, mg, T, cb_res)
            rmsnorm(xt, T, g2[:, l, :])
            for half in range(2):
                def cb_up(ft, pb):
                    tb_ = (tmpA, tmpB, sqt)[ft % 3]
                    P.act(tb_[:, :T], pb[:, :T], AF.Relu)
                    if ft % 2:
                        P.act(hd[:, ft, :T], tb_[:, :T], AF.Square)
                    else:
                        P.tt(hd[:, ft, :T], tb_[:, :T], tb_[:, :T], ALU.mult)
                dense_fm('w_up', l, 8, 16, xn, T, cb_up, c_base=half * 2048)
                dense_fm('w_down', l, 16, 8, hd, T, cb_res, k0=half * 16)
            for b0 in range(0, T, 128):
                nb = min(128, T - b0)
                P.dma("sp", ldp[0:nb, :], pin[l, t0 + b0:t0 + b0 + nb, :])
                pb = bank()
                for kc in range(2):
                    P.tr(pb[:, kc * 128:kc * 128 + nb], ldp[0:nb, kc * 128:(kc + 1) * 128], ident[0:nb, 0:nb])
                P.copy(pT[:, :, b0:b0 + nb], pb[:, 0:256].re("p (k n) -> p k n", k=2)[:, :, 0:nb], eng="act")
            for kc in range(8):
                P.copy(xn[:, kc, :T], xt[:, kc, :], eng="pool")
            for f0 in range(0, 8, 2):
                wpg = wload('w_ple_gate', l, 8, f0 * 128, 256)
                wpl = wload('w_ple', l, 2, f0 * 128, 256)
                for j in range(2):
                    ft = f0 + j
                    pg = bank()
                    for kc in range(8):
                        P.mm(pg[:, :T], wpg[:, kc, j * 128:(j + 1) * 128], xn[:, kc, :T], start=(kc == 0), stop=(kc == 7))
                    pp = bank()
                    for kc in range(2):
                        P.mm(pp[:, :T], wpl[:, kc, j * 128:(j + 1) * 128], pT[:, kc, :T], start=(kc == 0), stop=(kc == 1))
                    tA_, tB_ = ((tmpA, tmpB), (sqt, rstd))[ft % 2]
                    P.act(tA_[:, :T], pg[:, :T], AF.Sigmoid)
                    P.tt(tB_[:, :T], tA_[:, :T], pp[:, :T], ALU.mult)
                    P.tt(xt[:, ft, :], xt[:, ft, :], tB_[:, :T], ALU.add)

    gF = scr(0, 2)
    xtok = scr(2, 2)
    ytok = scr(4, 2)
    ssq = scr(6, 1)[:, 0:1]
    P.dma("sp", gF, W['final_norm_g'].re("(o n) -> o n", o=1).bc([128, 1024]))
    fbufs = [(scr(2, 2), scr(4, 2), scr(6, 1)[:, 0:1]), (scr(7, 2), scr(9, 2), scr(11, 1)[:, 0:1]), (scr(12, 2), scr(14, 2), scr(16, 1)[:, 0:1])]
    for b0 in range(0, NTOK, 128):
        nb = min(128, NTOK - b0)
        xtok, ytok, ssq = fbufs[(b0 // 128) % 3]
        pb = bank(2)
        for kc in range(8):
            P.tr(pb[0:nb, kc * 128:(kc + 1) * 128], xT[:, kc, b0:b0 + nb], ident)
        P.copy(xtok[0:nb, :], pb[0:nb, :], eng="act")
        P.act(ytok[0:nb, :], xtok[0:nb, :], AF.Square, accum=ssq[0:nb, :])
        P.act(ssq[0:nb, :], ssq[0:nb, :], AF.Ln, bias=epsc[0:nb, :], scale=1.0 / D)
        P.act(ssq[0:nb, :], ssq[0:nb, :], AF.Exp, scale=-0.5)
        P.stt(ytok[0:nb, :], xtok[0:nb, :], ssq[0:nb, :], gF[0:nb, :], ALU.mult, ALU.mult)
        P.dma("pool", yout[b0:b0 + nb, :], ytok[0:nb, :], is_out=True)

    P.emit()
    return nc, P


def make_in_maps(inputs, NP, ncores):
    cst = make_consts()
    maps = []
    for c in range(ncores):
        m = {}
        xs = inputs['x_sample'][c * NS:(c + 1) * NS, 0, :]
        m['xin'] = np.ascontiguousarray(np.concatenate([inputs['x_prompt'][c, :NP, :], xs], axis=0), dtype=np.float32)
        ps = inputs['p_sample'][:, c * NS:(c + 1) * NS, 0, :]
        m['pin'] = np.ascontiguousarray(np.concatenate([inputs['p_prompt'][:, c, :NP, :], ps], axis=1), dtype=np.float32)
        m['cst'] = cst
        m['st_s5re'] = np.ascontiguousarray(inputs['state_s5_re'][:, c * NS:(c + 1) * NS].reshape(2, NS, 2048), dtype=np.float32)
        m['st_conv'] = np.ascontiguousarray(inputs['state_gdn_conv'][:, c * NS:(c + 1) * NS].reshape(2, NS * 3, 1536), dtype=np.float32)
        m['st_gdn'] = np.ascontiguousarray(inputs['state_gdn'][:, c * NS:(c + 1) * NS], dtype=np.float32)
        m['st_mC'] = np.ascontiguousarray(inputs['state_mlstm_C'][:, c * NS:(c + 1) * NS], dtype=np.float32)
        m['st_mn'] = np.ascontiguousarray(inputs['state_mlstm_n'][:, c * NS:(c + 1) * NS].reshape(2, NS * 4, 64), dtype=np.float32)
        m['st_mm'] = np.ascontiguousarray(inputs['state_mlstm_m'][:, c * NS:(c + 1) * NS], dtype=np.float32)
        m['st_s5im'] = np.ascontiguousarray(inputs['state_s5_im'][:, c * NS:(c + 1) * NS].reshape(2, NS, 2048), dtype=np.float32)
        for n in WEIGHT_NAMES:
            m[n] = np.ascontiguousarray(inputs[n], dtype=np.float32)
        maps.append(m)
    return maps


def run(inputs, NP=2048, ncores=8, stage="all", debug=False, trace=False):
    nc, P = build_program(NP, stage, debug)
    maps = make_in_maps(inputs, NP, ncores)
    res = run_bass_kernel_spmd(nc, maps, core_ids=list(range(ncores)), trace=trace)
    if trace:
        print("EXEC_TIME_NS", res.exec_time_ns)
    R = res.results
    y = np.stack([r['y'] for r in R], axis=0)
    y_prompt = y[:, :NP, :]
    y_sample = y[:, NP:, :].reshape(ncores * NS, 1, D)
    def gat_p(name, shp):
        return np.stack([r[name] for r in R], axis=1).reshape((2, ncores) + shp)

    def gat_s(name, shp):
        return np.concatenate([r[name] for r in R], axis=1).reshape((2, ncores * NS) + shp)
    outs = [y_prompt, y_sample,
            gat_p('o_s5re_p', (32, 64)), gat_s('o_s5re_s', (32, 64)), gat_p('o_s5im_p', (32, 64)), gat_s('o_s5im_s', (32, 64)),
            gat_p('o_conv_p', (3, 1536)), gat_s('o_conv_s', (3, 1536)), gat_p('o_gdn_p', (4, 128, 128)), gat_s('o_gdn_s', (4, 128, 128)),
            gat_p('o_mC_p', (4, 64, 128)), gat_s('o_mC_s', (4, 64, 128)), gat_p('o_mn_p', (4, 64)), gat_s('o_mn_s', (4, 64)),
            gat_p('o_mm_p', (4,)), gat_s('o_mm_s', (4,))]
    return tuple(outs), R


def kernel(**inputs):
    outs, _ = run(inputs)
    return outs
```
